# Optimizing a Trainium2 kernel written in Bass

```python
import jax, jax.numpy as jnp
from jax import lax
import numpy as np

D_MODEL = 1024
BATCH = 4
SEQ = 8192
DEPTH = 4

N_A = DEPTH // 2
N_B = DEPTH - N_A
N_MEM = 256
HEAD_DIM = 64
RWKV_HEADS = 12
RWKV_DIM = RWKV_HEADS * HEAD_DIM
MEM_HEADS = 4
MEM_DIM = MEM_HEADS * HEAD_DIM
MIX_WIDTH = RWKV_DIM + MEM_DIM
DECAY_LORA = 64
AAA_LORA = 64
MV_LORA = 32
GATE_LORA = 128
RWKV_IN = 3 * RWKV_DIM + DECAY_LORA + AAA_LORA + GATE_LORA
A_IN = RWKV_IN + MEM_DIM
RWKV_SPLITS = (RWKV_DIM, 2 * RWKV_DIM, 3 * RWKV_DIM, 3 * RWKV_DIM + DECAY_LORA,
               3 * RWKV_DIM + DECAY_LORA + AAA_LORA)
MLA_HEADS = 12
QK_NOPE = 64
QK_ROPE = 32
V_HEAD = 64
Q_LORA = 512
KV_LORA = 256
MLA_DIM = MLA_HEADS * V_HEAD
B_IN = Q_LORA + MEM_DIM
FFN_HIDDEN = ((8 * D_MODEL + 3 * 256 - 1) // (3 * 256)) * 256
ROPE_BASE = 10000.0
NORM_EPS = 1e-6
LNX_EPS = 64e-5
Q_BLOCK = 128

kernel_name = 'yoco_rwkv7_mla_memory_trunk'


def rms_norm(x, g, eps=NORM_EPS):
    xf = x.astype(jnp.float32)
    y = xf * lax.rsqrt(jnp.mean(xf * xf, axis=-1, keepdims=True) + eps)
    return (y * g.astype(jnp.float32)).astype(x.dtype)


def token_shift(t):
    return jnp.pad(t, ((0, 0), (1, 0), (0, 0)))[:, :-1]


def l2_normalize(t):
    tf = t.astype(jnp.float32)
    n = jnp.sqrt(jnp.sum(tf * tf, axis=-1, keepdims=True))
    return (tf / jnp.maximum(n, 1e-12)).astype(t.dtype)


def head_group_norm(y, g, b):
    yf = y.astype(jnp.float32)
    mu = jnp.mean(yf, axis=-1, keepdims=True)
    var = jnp.mean(jnp.square(yf - mu), axis=-1, keepdims=True)
    out = (yf - mu) * lax.rsqrt(var + LNX_EPS)
    out = out * g.reshape(RWKV_HEADS, HEAD_DIM).astype(jnp.float32) + b.reshape(RWKV_HEADS, HEAD_DIM).astype(jnp.float32)
    return out.astype(y.dtype)


def swiglu(h, w_gu, w_down):
    gate, up = jnp.split(h @ w_gu, 2, axis=-1)
    return (jax.nn.silu(gate) * up) @ w_down


def rope_tables(positions):
    half = QK_ROPE // 2
    inv_freq = ROPE_BASE ** (-jnp.arange(half, dtype=jnp.float32) / half)
    ang = positions.astype(jnp.float32)[..., None] * inv_freq
    return jnp.cos(ang), jnp.sin(ang)


def rope(t, cos, sin):
    tf = t.astype(jnp.float32)
    t1, t2 = jnp.split(tf, 2, axis=-1)
    return jnp.concatenate([t1 * cos - t2 * sin, t2 * cos + t1 * sin], axis=-1).astype(t.dtype)


def rwkv7_scan(r, decay, k, v, a, b):
    dtype = r.dtype
    bsz, _, nh, n = r.shape
    seqs = tuple(jnp.moveaxis(t.astype(jnp.float32), 1, 0) for t in (r, decay, k, v, a, b))

    def step(state, inp):
        r_t, w_t, k_t, v_t, a_t, b_t = inp
        sa = jnp.einsum('bhvk,bhk->bhv', state, a_t)
        state = (state * w_t[:, :, None, :] + sa[..., None] * b_t[:, :, None, :]
                 + v_t[..., None] * k_t[:, :, None, :])
        return state, jnp.einsum('bhvk,bhk->bhv', state, r_t)

    s0 = jnp.zeros((bsz, nh, n, n), jnp.float32)
    _, ys = lax.scan(step, s0, seqs)
    return jnp.moveaxis(ys, 0, 1).astype(dtype)


def memory_attention(q, mem_n, w_kv):
    bsz, n_mem, _ = mem_n.shape
    mkv = mem_n @ w_kv
    mk = mkv[..., :MEM_DIM].reshape(bsz, n_mem, MEM_HEADS, HEAD_DIM)
    mv = mkv[..., MEM_DIM:].reshape(bsz, n_mem, MEM_HEADS, HEAD_DIM)
    s = jnp.einsum('bshd,bmhd->bhsm', q, mk, preferred_element_type=jnp.float32) * (HEAD_DIM ** -0.5)
    p = jax.nn.softmax(s, axis=-1).astype(mv.dtype)
    o = jnp.einsum('bhsm,bmhd->bshd', p, mv)
    return o.reshape(q.shape[0], q.shape[1], MEM_DIM)


def causal_mla_attention(q_nope, q_rope, k_nope, k_rope, v):
    bsz, s_len, nh, _ = q_nope.shape
    nb = s_len // Q_BLOCK
    key_pos = jnp.arange(s_len)
    scale = (QK_NOPE + QK_ROPE) ** -0.5

    def to_blocks(t):
        return jnp.moveaxis(t.reshape((bsz, nb, Q_BLOCK) + t.shape[2:]), 1, 0)

    def block(args):
        qn, qr, blk = args
        s = (jnp.einsum('bqhd,bkhd->bhqk', qn, k_nope, preferred_element_type=jnp.float32)
             + jnp.einsum('bqhd,bkd->bhqk', qr, k_rope, preferred_element_type=jnp.float32)) * scale
        q_pos = blk * Q_BLOCK + jnp.arange(Q_BLOCK)
        s = jnp.where(key_pos[None, :] <= q_pos[:, None], s, -jnp.inf)
        p = jax.nn.softmax(s, axis=-1).astype(v.dtype)
        return jnp.einsum('bhqk,bkhd->bqhd', p, v)

    out = lax.map(block, (to_blocks(q_nope), to_blocks(q_rope), jnp.arange(nb)))
    return jnp.moveaxis(out, 0, 1).reshape(bsz, s_len, nh * v.shape[-1])


def setup_inputs(seed: int = 0) -> dict:
    key = jax.random.key(seed)
    keys = iter(jax.random.split(key, 64))

    def nrm(shape, scale):
        return jax.random.normal(next(keys), shape, jnp.float32) * scale

    def gain(shape):
        return 1.0 + nrm(shape, 0.02)

    def unif(shape, lo, hi):
        return jax.random.uniform(next(keys), shape, jnp.float32, lo, hi)

    nv = max(N_A - 1, 0)
    qk_up = MLA_HEADS * (QK_NOPE + QK_ROPE)
    kv_up = MLA_HEADS * (QK_NOPE + V_HEAD)
    return {
        'x': nrm((BATCH, SEQ, D_MODEL), 1.0),
        'mem': nrm((BATCH, N_MEM, D_MODEL), 1.0),
        'positions': (jnp.arange(SEQ, dtype=jnp.int32)[None, :]
                      + jax.random.randint(next(keys), (BATCH, 1), 0, 1024, jnp.int32)),
        'mem_norm_g': gain((D_MODEL,)),
        'a_norm1_g': gain((N_A, D_MODEL)),
        'a_w_in': nrm((N_A, D_MODEL, A_IN), D_MODEL ** -0.5),
        'a_shift_mu': unif((N_A, RWKV_IN), 0.0, 1.0),
        'a_decay_up': nrm((N_A, DECAY_LORA, RWKV_DIM), 0.5 * DECAY_LORA ** -0.5),
        'a_decay_bias': unif((N_A, RWKV_DIM), -6.0, 0.5),
        'a_aaa_up': nrm((N_A, AAA_LORA, RWKV_DIM), AAA_LORA ** -0.5),
        'a_aaa_bias': nrm((N_A, RWKV_DIM), 0.1),
        'a_gate_up': nrm((N_A, GATE_LORA, RWKV_DIM), GATE_LORA ** -0.5),
        'a_k_k': 0.85 + nrm((N_A, RWKV_DIM), 0.05),
        'a_k_a': 1.0 + nrm((N_A, RWKV_DIM), 0.05),
        'a_r_k': nrm((N_A, RWKV_HEADS, HEAD_DIM), 0.1),
        'a_lnx_g': gain((N_A, RWKV_DIM)),
        'a_lnx_b': nrm((N_A, RWKV_DIM), 0.02),
        'a_mem_kv': nrm((N_A, D_MODEL, 2 * MEM_DIM), D_MODEL ** -0.5),
        'a_w_out': nrm((N_A, MIX_WIDTH, D_MODEL), MIX_WIDTH ** -0.5),
        'a_norm2_g': gain((N_A, D_MODEL)),
        'a_ffn_gu': nrm((N_A, D_MODEL, 2 * FFN_HIDDEN), D_MODEL ** -0.5),
        'a_ffn_down': nrm((N_A, FFN_HIDDEN, D_MODEL), FFN_HIDDEN ** -0.5),
        'vres_mu': unif((nv, D_MODEL), 0.0, 1.0),
        'vres_down': nrm((nv, D_MODEL, MV_LORA), D_MODEL ** -0.5),
        'vres_up': nrm((nv, MV_LORA, RWKV_DIM), MV_LORA ** -0.5),
        'vres_bias': nrm((nv, RWKV_DIM), 0.1),
        'kv_norm_g': gain((D_MODEL,)),
        'kv_w_down': nrm((D_MODEL, KV_LORA + QK_ROPE), D_MODEL ** -0.5),
        'kv_latent_g': gain((KV_LORA,)),
        'kv_w_up': nrm((KV_LORA, kv_up), KV_LORA ** -0.5),
        'b_norm1_g': gain((N_B, D_MODEL)),
        'b_w_in': nrm((N_B, D_MODEL, B_IN), D_MODEL ** -0.5),
        'b_q_norm_g': gain((N_B, Q_LORA)),
        'b_q_up': nrm((N_B, Q_LORA, qk_up), Q_LORA ** -0.5),
        'b_mem_kv': nrm((N_B, D_MODEL, 2 * MEM_DIM), D_MODEL ** -0.5),
        'b_w_out': nrm((N_B, MIX_WIDTH, D_MODEL), MIX_WIDTH ** -0.5),
        'b_norm2_g': gain((N_B, D_MODEL)),
        'b_ffn_gu': nrm((N_B, D_MODEL, 2 * FFN_HIDDEN), D_MODEL ** -0.5),
        'b_ffn_down': nrm((N_B, FFN_HIDDEN, D_MODEL), FFN_HIDDEN ** -0.5),
        'final_norm_g': gain((D_MODEL,)),
    }


def reference(x, mem, positions, mem_norm_g,
              a_norm1_g, a_w_in, a_shift_mu, a_decay_up, a_decay_bias, a_aaa_up, a_aaa_bias,
              a_gate_up, a_k_k, a_k_a, a_r_k, a_lnx_g, a_lnx_b, a_mem_kv, a_w_out, a_norm2_g,
              a_ffn_gu, a_ffn_down,
              vres_mu, vres_down, vres_up, vres_bias,
              kv_norm_g, kv_w_down, kv_latent_g, kv_w_up,
              b_norm1_g, b_w_in, b_q_norm_g, b_q_up, b_mem_kv, b_w_out, b_norm2_g,
              b_ffn_gu, b_ffn_down,
              final_norm_g):
    bsz, s_len, _ = x.shape

    def heads(t, nh):
        return t.reshape(bsz, s_len, nh, HEAD_DIM)

    mem_n = rms_norm(mem, mem_norm_g)
    cos, sin = rope_tables(positions)
    v_first = None
    k_nope = k_rope = v_mla = None

    for layer in range(DEPTH):
        if layer < N_A:
            i = layer
            h = rms_norm(x, a_norm1_g[i])
            proj = h @ a_w_in[i]
            p_tm, q_mem = proj[..., :RWKV_IN], proj[..., RWKV_IN:]
            p_tm = p_tm + (token_shift(p_tm) - p_tm) * a_shift_mu[i]
            r, k, v, d_lo, a_lo, g_lo = jnp.split(p_tm, RWKV_SPLITS, axis=-1)
            log_w = -jax.nn.softplus(-(a_decay_bias[i] + jnp.tanh(d_lo) @ a_decay_up[i])) - 0.5
            decay = jnp.exp(-jnp.exp(log_w.astype(jnp.float32)))
            lr = jax.nn.sigmoid(a_aaa_bias[i] + a_lo @ a_aaa_up[i])
            gate = jax.nn.sigmoid(g_lo) @ a_gate_up[i]
            if i == 0:
                v_first = v
            else:
                hv = h + (token_shift(h) - h) * vres_mu[i - 1]
                v = v + (v_first - v) * jax.nn.sigmoid(
                    vres_bias[i - 1] + (hv @ vres_down[i - 1]) @ vres_up[i - 1])
            kk = l2_normalize(heads(k * a_k_k[i], RWKV_HEADS))
            k = k * (1.0 + (lr - 1.0) * a_k_a[i])
            rh, kh, vh, lrh = (heads(t, RWKV_HEADS) for t in (r, k, v, lr))
            y = rwkv7_scan(rh, heads(decay, RWKV_HEADS), kh, vh, -kk, kk * lrh)
            y = head_group_norm(y, a_lnx_g[i], a_lnx_b[i])
            y = y + jnp.sum(rh * kh * a_r_k[i], axis=-1, keepdims=True) * vh
            y_mix = y.reshape(bsz, s_len, RWKV_DIM) * gate
            m = memory_attention(heads(q_mem, MEM_HEADS), mem_n, a_mem_kv[i])
            x = x + jnp.concatenate([y_mix, m], axis=-1) @ a_w_out[i]
            x = x + swiglu(rms_norm(x, a_norm2_g[i]), a_ffn_gu[i], a_ffn_down[i])
        else:
            j = layer - N_A
            if j == 0:
                hk = rms_norm(x, kv_norm_g)
                ckr = hk @ kv_w_down
                c_kv = rms_norm(ckr[..., :KV_LORA], kv_latent_g)
                k_rope = rope(ckr[..., KV_LORA:], cos, sin)
                kv = (c_kv @ kv_w_up).reshape(bsz, s_len, MLA_HEADS, QK_NOPE + V_HEAD)
                k_nope, v_mla = kv[..., :QK_NOPE], kv[..., QK_NOPE:]
            h = rms_norm(x, b_norm1_g[j])
            proj = h @ b_w_in[j]
            c_q = rms_norm(proj[..., :Q_LORA], b_q_norm_g[j])
            q_mem = proj[..., Q_LORA:]
            q = (c_q @ b_q_up[j]).reshape(bsz, s_len, MLA_HEADS, QK_NOPE + QK_ROPE)
            q_nope = q[..., :QK_NOPE]
            q_rope = rope(q[..., QK_NOPE:], cos[:, :, None, :], sin[:, :, None, :])
            o = causal_mla_attention(q_nope, q_rope, k_nope, k_rope, v_mla)
            m = memory_attention(heads(q_mem, MEM_HEADS), mem_n, b_mem_kv[j])
            x = x + jnp.concatenate([o, m], axis=-1) @ b_w_out[j]
            x = x + swiglu(rms_norm(x, b_norm2_g[j]), b_ffn_gu[j], b_ffn_down[j])

    return rms_norm(x, final_norm_g)
```

```python
import math
import numpy as np
from contextlib import ExitStack, contextmanager
import concourse.bass as bass
import concourse.mybir as mybir
from concourse.bass_utils import run_bass_kernel_spmd


F32 = mybir.dt.float32
BF16 = mybir.dt.bfloat16
I32 = mybir.dt.int32
ALU = mybir.AluOpType
AF = mybir.ActivationFunctionType
AX = mybir.AxisListType

ENGS = ("pe", "dve", "act", "pool", "sp")
NSEM = 6
NDMASEM = 24


class Dep:
    __slots__ = ("w", "r")

    def __init__(self):
        self.w = None
        self.r = []


class Tile:
    def __init__(self, h, name):
        self.h = h
        self.name = name
        self.whole = Dep()
        self.parts = {}

    def __getitem__(self, idx):
        return V(self, None, self.h[idx])

    def p(self, key, idx=None):
        return V(self, key, self.h[idx] if idx is not None else self.h[:])

    def ap(self, ap, key=None):
        return V(self, key, ap)


class V:
    __slots__ = ("t", "k", "ap")

    def __init__(self, t, k, ap):
        self.t, self.k, self.ap = t, k, ap

    def __getitem__(self, idx):
        return V(self.t, self.k, self.ap[idx])

    def deps(self):
        t = self.t
        if self.k is None:
            return [t.whole] + list(t.parts.values())
        if self.k not in t.parts:
            t.parts[self.k] = Dep()
        return [t.whole, t.parts[self.k]]

    def own(self):
        t = self.t
        if self.k is None:
            return t.whole
        return t.parts[self.k]


class Op:
    __slots__ = ("id", "eng", "fn", "deps", "signal", "sem", "val", "isdma", "waits")

    def __init__(self, id, eng, fn, isdma):
        self.id, self.eng, self.fn, self.isdma = id, eng, fn, isdma
        self.deps = set()
        self.signal = False
        self.sem = None
        self.val = 0
        self.waits = []


class Sched:
    def __init__(self, nc, stack):
        self.nc = nc
        self.stack = stack
        self.ops = []
        self.n_sb = 0

    def sb(self, name, shape, dt=F32):
        h = self.stack.enter_context(self.nc.sbuf_tensor(name, list(shape), dt))
        return Tile(h, name)

    def ps(self, name, shape, dt=F32):
        h = self.stack.enter_context(self.nc.psum_tensor(name, list(shape), dt))
        return Tile(h, name)

    def dram(self, name, shape, dt=F32, kind="Internal"):
        h = self.nc.dram_tensor(name, list(shape), dt, kind=kind)
        return Tile(h.ap(), name)

    def add(self, eng, fn, reads=(), writes=(), isdma=False):
        op = Op(len(self.ops), eng, fn, isdma)
        for v in reads:
            for d in v.deps():
                if d.w is not None:
                    op.deps.add(d.w)
        for v in writes:
            for d in v.deps():
                if d.w is not None:
                    op.deps.add(d.w)
                op.deps.update(d.r)
        for v in reads:
            v.own().r.append(op.id)
        for v in writes:
            o = v.own()
            o.w = op.id
            o.r = []
            if v.k is None:
                v.t.parts = {}
        op.deps.discard(op.id)
        if eng == "pe":
            op.deps = {d for d in op.deps if self.ops[d].eng != "pe" or self.ops[d].isdma}
        self.ops.append(op)
        return op

    def op(self, eng, method, extra_reads=(), extra_writes=(), isdma=False, **kw):
        reads, writes, args = list(extra_reads), list(extra_writes), {}
        for k, v in kw.items():
            if isinstance(v, V):
                (writes if (k.startswith("out") or k == "accum_out") else reads).append(v)
                args[k] = v.ap
            else:
                args[k] = v
        return self.add(eng, lambda e: getattr(e, method)(**args), reads=reads, writes=writes, isdma=isdma)

    def dma(self, out, in_, q="sp", **kw):
        return self.op(q, "dma_start", isdma=True, out=out, in_=in_, **kw)

    def mm(self, out, lhsT, rhs, start=True, stop=True):
        o, l, r = out.ap, lhsT.ap, rhs.ap
        return self.add("pe", lambda e: e.matmul(o, l, r, start=start, stop=stop),
                        reads=[lhsT, rhs], writes=[out])

    def transpose(self, out, in_, ident):
        o, i, d = out.ap, in_.ap, ident.ap
        return self.add("pe", lambda e: e.transpose(o, i, d),
                        reads=[in_, ident], writes=[out])

    def barrier_wait(self, eng, ops):
        op = Op(len(self.ops), eng, None, False)
        op.deps = {o.id for o in ops}
        self.ops.append(op)
        return op

    def emit(self):
        nc = self.nc
        ops = self.ops
        for op in ops:
            for d in op.deps:
                ops[d].signal = True
        sems = {}
        for e in ("pe", "dve", "act", "pool"):
            sems[e] = [self.stack.enter_context(nc.semaphore(f"s_{e}{i}")) for i in range(NSEM)]
        dsems = {}
        dma_queues = sorted({op.eng for op in ops if op.isdma})
        for q in dma_queues:
            dsems[q] = [self.stack.enter_context(nc.semaphore(f"d_{q}{i}")) for i in range(NDMASEM)]
        cnt = {e: 0 for e in ENGS}
        dcnt = {q: 0 for q in dma_queues}
        prev_on_slot = {}
        for op in ops:
            if op.isdma:
                k = dcnt[op.eng]
                dcnt[op.eng] += 1
                slot = k % NDMASEM
                op.sem = dsems[op.eng][slot]
                op.val = 16 * (k // NDMASEM + 1)
                pk = (op.eng, slot)
                if pk in prev_on_slot:
                    op.deps.add(prev_on_slot[pk])
                prev_on_slot[pk] = op.id
                op.signal = True
            elif op.signal:
                k = cnt[op.eng]
                cnt[op.eng] += 1
                op.sem = sems[op.eng][k % NSEM]
                op.val = k // NSEM + 1
        waited = {e: {} for e in ENGS}
        per_eng = {e: [] for e in ENGS}
        for op in ops:
            need = {}
            for d in op.deps:
                p = ops[d]
                key = id(p.sem)
                if key not in need or need[key][1] < p.val:
                    need[key] = (p.sem, p.val)
            w = waited[op.eng]
            for key, (sem, val) in need.items():
                if w.get(key, 0) >= val:
                    continue
                w[key] = val
                op.waits.append((sem, val))
            per_eng[op.eng].append(op)
        self.stats = {e: len(per_eng[e]) for e in ENGS}
        self.stats["waits"] = sum(len(o.waits) for o in ops)

        def run(eng_obj, lst):
            for op in lst:
                for sem, val in op.waits:
                    eng_obj.wait_ge(sem, val)
                if op.fn is None:
                    continue
                ins = op.fn(eng_obj)
                if op.signal:
                    ins.then_inc(op.sem, 16 if op.isdma else 1)

        with nc.Block() as block:
            @block.sync
            def _(e):
                run(e, per_eng["sp"])

            @block.tensor
            def _(e):
                run(e, per_eng["pe"])

            @block.vector
            def _(e):
                run(e, per_eng["dve"])

            @block.scalar
            def _(e):
                run(e, per_eng["act"])

            @block.gpsimd
            def _(e):
                run(e, per_eng["pool"])


def _sched_scope_init(self):
    if not hasattr(self, "stacks"):
        self.stacks = [self.stack]
        self.last_barrier = 0


def _sb(self, name, shape, dt=F32):
    _sched_scope_init(self)
    self.n_sb += 1
    name = f"{name}_{self.n_sb}"
    h = self.stacks[-1].enter_context(self.nc.sbuf_tensor(name, list(shape), dt))
    return Tile(h, name)


def _ps(self, name, shape, dt=F32):
    _sched_scope_init(self)
    h = self.stacks[-1].enter_context(self.nc.psum_tensor(name, list(shape), dt))
    return Tile(h, name)


def _barrier_all(self):
    _sched_scope_init(self)
    last = {}
    dmas = set()
    for op in self.ops[self.last_barrier:]:
        if op.fn is None:
            continue
        last[op.eng] = op.id
        if op.isdma:
            dmas.add(op.id)
    deps = set(last.values()) | dmas
    if not deps:
        return
    for e in ENGS:
        b = Op(len(self.ops), e, None, False)
        b.deps = set(deps)
        self.ops.append(b)
    self.last_barrier = len(self.ops)


@contextmanager
def _scope(self):
    _sched_scope_init(self)
    st = ExitStack()
    self.stacks.append(st)
    try:
        yield
    finally:
        self.barrier_all()
        self.stacks.pop()
        st.close()


Sched.sb = _sb
Sched.ps = _ps
Sched.barrier_all = _barrier_all
Sched.scope = _scope


D = 1024
KD = D // 128
FH = 2816
HD = 64
NMEM = 256
T = 512


def dview(v, pattern, **kw):
    return V(v.t, v.k, v.ap.rearrange(pattern, **kw))


def consts(S, Wd):
    C = {}
    for nm, val in (("ones1024", 1.0 / 1024), ("ones512", 1.0 / 512), ("ones256", 1.0 / 256)):
        C[nm] = S.sb(nm, [128, 128], F32)
        S.op("pool", "memset", ap=C[nm][:], constant=val, extra_writes=[C[nm][:]])
    C["ones_mean"] = C["ones1024"]
    C["eps6"] = S.sb("eps6", [128, 1], F32)
    S.op("pool", "memset", ap=C["eps6"][:], constant=1e-6, extra_writes=[C["eps6"][:]])
    C["sel"] = S.sb("sel", [128, 64], F32)
    S.op("pool", "memset", ap=C["sel"][:], constant=0.0, extra_writes=[C["sel"][:]])
    S.op("pool", "memset", ap=C["sel"][64:65, :], constant=1.0, extra_writes=[C["sel"][:]])
    C["ident"] = S.sb("ident", [128, 128], F32)
    S.op("pool", "memset", ap=C["ident"][:], constant=1.0, extra_writes=[C["ident"][:]])
    S.op("pool", "affine_select", out=C["ident"][:], in_=C["ident"][:], pattern=[[-1, 128]],
         compare_op=ALU.is_equal, fill=0.0, base=0, channel_multiplier=1)
    C["tri"] = S.sb("tri", [128, 128], BF16)
    S.op("pool", "memset", ap=C["tri"][:], constant=1.0, extra_writes=[C["tri"][:]])
    S.op("pool", "affine_select", out=C["tri"][:], in_=C["tri"][:], pattern=[[1, 128]],
         compare_op=ALU.is_ge, fill=0.0, base=0, channel_multiplier=-1)
    C["bones"] = S.sb("bones", [128, 128], F32)
    S.op("pool", "memset", ap=C["bones"][:], constant=0.0, extra_writes=[C["bones"][:]])
    S.op("pool", "memset", ap=C["bones"][0:64, 0:64], constant=1.0, extra_writes=[C["bones"][:]])
    S.op("pool", "memset", ap=C["bones"][64:128, 64:128], constant=1.0, extra_writes=[C["bones"][:]])
    C["P"] = [S.ps(f"P{i}", [128, 512], F32) for i in range(8)]
    return C


def rms_rstd(S, C, x, ps_ss, sq, rstd, nk, ones, n=T):
    for kc in range(nk):
        S.op("act", "activation", out=sq[:, :n], in_=x[:, kc, :], func=AF.Square)
        S.mm(ps_ss[:, :n], ones[:], sq[:, :n], start=(kc == 0), stop=(kc == nk - 1))
    S.op("act", "activation", out=rstd[:, :n], in_=ps_ss[:, :n], func=AF.Sqrt, bias=C["eps6"][:, 0:1], scale=1.0)
    S.op("dve", "reciprocal", out=rstd[:, :n], in_=rstd[:, :n])


def load_vec(S, dst, src, q="sp"):
    S.dma(dst, dview(src, "(k p) -> p k", p=128), q=q, allow_slow_non_contiguous=True)


def norm_apply(S, h, x, g_sb, rstd, nk, n=T):
    for kc in range(nk):
        S.op("dve", "scalar_tensor_tensor", out=h.p(kc)[:, kc, :], in0=x[:, kc, :], scalar=g_sb[:, kc:kc + 1],
             in1=rstd[:, :n], op0=ALU.mult, op1=ALU.mult)


def attn_finish(S, C, ps_o, osb, ps_d, rec, out_v, n=T):
    S.op("act", "activation", out=osb[0:65, :n], in_=ps_o[0:65, :n], func=AF.Copy)
    S.mm(ps_d[0:64, :n], C["sel"][0:65, 0:64], osb[0:65, :n])
    S.op("dve", "reciprocal", out=rec[0:64, :n], in_=ps_d[0:64, :n])
    S.op("dve", "tensor_tensor", out=out_v, in0=osb[0:64, :n], in1=rec[0:64, :n], op=ALU.mult)


def ffn_phase(S, C, x_in, x_out, g_dram, gu_dram, down_dram, ntok, FHc=FH, tag="f", final_g=None):
    NH = FHc // 128
    outs = []
    with S.scope():
        gu_sb = S.sb(tag + "gu_sb", [128, KD, 2 * FHc], BF16)
        dn_sb = S.sb(tag + "dn_sb", [128, NH, D], BF16)
        g_sb = S.sb(tag + "g_sb", [128, KD], F32)
        for kc in range(KD):
            S.dma(gu_sb.p(("k", kc))[:, kc, :], gu_dram[kc * 128:(kc + 1) * 128, :], q="pool")
        for j in range(NH):
            S.dma(dn_sb.p(("j", j))[:, j, :], down_dram[j * 128:(j + 1) * 128, :], q="pool")
        load_vec(S, g_sb[:], g_dram)
        if final_g is not None:
            fg_sb = S.sb(tag + "fg_sb", [128, KD], F32)
            load_vec(S, fg_sb[:], final_g)
        xs = [S.sb(tag + f"x{i}", [128, KD, T], F32) for i in range(2)]
        h = S.sb(tag + "h", [128, KD, T], BF16)
        sq = S.sb(tag + "sq", [128, T], F32)
        rstd = S.sb(tag + "rstd", [128, T], F32)
        hid = S.sb(tag + "hid", [128, NH, T], BF16)
        sg = [S.sb(tag + f"sg{i}", [128, T], BF16) for i in range(2)]
        P = C["P"]
        ps_ss, ps_g, ps_u, ps_o = P[0], P[1:3], P[3:5], P[5:7]
        xin_v = x_in.h.rearrange("(k p) t -> p k t", p=128)
        xout_v = x_out.h.rearrange("(k p) t -> p k t", p=128)
        for it in range(ntok // T):
            x = xs[it % 2]
            S.dma(x[:], x_in.ap(xin_v[:, :, it * T:(it + 1) * T], key=it), q="sp")
            rms_rstd(S, C, x, ps_ss, sq, rstd, KD, C["ones1024"])
            norm_apply(S, h, x, g_sb, rstd, KD)
            for j in range(NH):
                pg, pu = ps_g[j % 2], ps_u[j % 2]
                for kc in range(KD):
                    S.mm(pg[:], gu_sb.p(("k", kc))[:, kc, j * 128:(j + 1) * 128], h.p(kc)[:, kc, :],
                         start=(kc == 0), stop=(kc == KD - 1))
                for kc in range(KD):
                    S.mm(pu[:], gu_sb.p(("k", kc))[:, kc, FHc + j * 128:FHc + (j + 1) * 128], h.p(kc)[:, kc, :],
                         start=(kc == 0), stop=(kc == KD - 1))
                s = sg[j % 2]
                S.op("act", "activation", out=s[:], in_=pg[:], func=AF.Silu)
                S.op("dve", "tensor_tensor", out=hid.p(j)[:, j, :], in0=s[:], in1=pu[:], op=ALU.mult)
            for oc in range(KD):
                po = ps_o[oc % 2]
                for j in range(NH):
                    S.mm(po[:], dn_sb.p(("j", j))[:, j, oc * 128:(oc + 1) * 128], hid.p(j)[:, j, :],
                         start=(j == 0), stop=(j == NH - 1))
                S.op("dve", "tensor_tensor", out=x[:, oc, :], in0=x[:, oc, :], in1=po[:], op=ALU.add)
            if final_g is not None:
                rms_rstd(S, C, x, ps_ss, sq, rstd, KD, C["ones1024"])
                for kc in range(KD):
                    S.op("dve", "scalar_tensor_tensor", out=x[:, kc, :], in0=x[:, kc, :], scalar=fg_sb[:, kc:kc + 1],
                         in1=rstd[:], op0=ALU.mult, op1=ALU.mult)
            outs.append(S.dma(x_out.ap(xout_v[:, :, it * T:(it + 1) * T], key=it), x[:], q="pool"))
    return outs


def outproj_phase(S, C, x_in, x_out, YD, MD, w_out, ntok, tag="o"):
    with S.scope():
        wo = S.sb(tag + "wo", [64, 16, D], BF16)
        S.dma(wo[:], dview(w_out, "(c p) n -> p c n", p=64), q="pool")
        xs = [S.sb(tag + f"x{i}", [128, KD, T], F32) for i in range(2)]
        ys = [S.sb(tag + f"y{i}", [64, 16, T], BF16) for i in range(2)]
        P = C["P"]
        xin_v = x_in.h.rearrange("(k p) t -> p k t", p=128)
        xout_v = x_out.h.rearrange("(k p) t -> p k t", p=128)
        yv = YD.h.rearrange("h p t -> p h t")
        mv = MD.h.rearrange("h p t -> p h t")
        for it in range(ntok // T):
            x, y = xs[it % 2], ys[it % 2]
            ts = slice(it * T, (it + 1) * T)
            S.dma(x[:], x_in.ap(xin_v[:, :, ts], key=it), q="sp")
            S.dma(y.p("y")[:, 0:12, :], YD.ap(yv[:, :, ts], key=it), q="sp")
            S.dma(y.p("m")[:, 12:16, :], MD.ap(mv[:, :, ts], key=it), q="sp")
            for oc in range(KD):
                po = P[1 + oc % 2]
                for hc in range(16):
                    S.mm(po[:], wo[:, hc, oc * 128:(oc + 1) * 128], y.p("y" if hc < 12 else "m")[:, hc, :],
                         start=(hc == 0), stop=(hc == 15))
                S.op("dve", "tensor_tensor", out=x[:, oc, :], in0=x[:, oc, :], in1=po[:], op=ALU.add)
            S.dma(x_out.ap(xout_v[:, :, ts], key=it), x[:], q="pool")


NH_MLA = 12
SCALE_MLA = (64 + 32) ** -0.5
SCALE_MEM = 64 ** -0.5


def rope_tables_phase(S, C, pos, invf, tabs, ntok):
    CH = 1024 if ntok >= 1024 else ntok
    C1 = 6.28125
    C2 = 2 * math.pi - 6.28125
    with S.scope():
        iv = S.sb("rt_iv", [128, 1], F32)
        S.dma(iv[:], dview(invf, "(p o) -> p o", o=1), q="sp")
        pi_t = S.sb("rt_pi", [128, CH], I32)
        pf = S.sb("rt_pf", [128, CH], F32)
        ang = S.sb("rt_ang", [128, CH], F32)
        tmp = S.sb("rt_tmp", [128, CH], F32)
        ki = S.sb("rt_ki", [128, CH], I32)
        kf = S.sb("rt_kf", [128, CH], F32)
        r = S.sb("rt_r", [128, CH], F32)
        o = {k: S.sb("rt_o" + k, [128, CH], F32) for k in ("sink", "cosk", "sinq", "cosq")}
        for c in range(ntok // CH):
            cs = slice(c * CH, (c + 1) * CH)
            S.dma(pi_t[:], V(pos.t, None, pos.ap[cs].partition_broadcast(128)), q="sp")
            S.op("dve", "tensor_copy", out=pf[:], in_=pi_t[:])
            S.op("dve", "tensor_scalar", out=ang[:], in0=pf[:], scalar1=iv[:, 0:1], scalar2=None, op0=ALU.mult)
            S.op("dve", "tensor_scalar", out=tmp[:], in0=ang[:], scalar1=1.0 / (2 * math.pi), scalar2=None, op0=ALU.mult)
            S.op("dve", "tensor_copy", out=ki[:], in_=tmp[:])
            S.op("dve", "tensor_copy", out=kf[:], in_=ki[:])
            S.op("dve", "scalar_tensor_tensor", out=r[:], in0=kf[:], scalar=-C1, in1=ang[:], op0=ALU.mult, op1=ALU.add)
            S.op("dve", "scalar_tensor_tensor", out=r[:], in0=kf[:], scalar=-C2, in1=r[:], op0=ALU.mult, op1=ALU.add)
            S.op("dve", "tensor_scalar", out=tmp[:], in0=r[:], scalar1=math.pi, scalar2=-2 * math.pi, op0=ALU.is_gt, op1=ALU.mult)
            S.op("dve", "tensor_tensor", out=r[:], in0=r[:], in1=tmp[:], op=ALU.add)
            S.op("act", "activation", out=o["sink"][:], in_=r[:], func=AF.Sin)
            S.op("dve", "tensor_scalar", out=r[:], in0=r[:], scalar1=math.pi / 2, scalar2=None, op0=ALU.add)
            S.op("dve", "tensor_scalar", out=tmp[:], in0=r[:], scalar1=math.pi, scalar2=-2 * math.pi, op0=ALU.is_gt, op1=ALU.mult)
            S.op("dve", "tensor_tensor", out=r[:], in0=r[:], in1=tmp[:], op=ALU.add)
            S.op("act", "activation", out=o["cosk"][:], in_=r[:], func=AF.Sin)
            S.op("dve", "tensor_scalar", out=o["sinq"][:], in0=o["sink"][:], scalar1=SCALE_MLA, scalar2=None, op0=ALU.mult)
            S.op("dve", "tensor_scalar", out=o["cosq"][:], in0=o["cosk"][:], scalar1=SCALE_MLA, scalar2=None, op0=ALU.mult)
            for k in o:
                S.dma(tabs[k][:, cs], o[k][:], q="pool")


def rope_apply(S, t1, t2, cos, sin, o1, o2, tmpa, tmpb, np_, n=T):
    S.op("dve", "tensor_tensor", out=tmpa[0:np_, :n], in0=t1, in1=cos, op=ALU.mult)
    S.op("dve", "tensor_tensor", out=tmpb[0:np_, :n], in0=t2, in1=sin, op=ALU.mult)
    S.op("dve", "tensor_tensor", out=o1, in0=tmpa[0:np_, :n], in1=tmpb[0:np_, :n], op=ALU.subtract)
    S.op("dve", "tensor_tensor", out=tmpa[0:np_, :n], in0=t2, in1=cos, op=ALU.mult)
    S.op("dve", "tensor_tensor", out=tmpb[0:np_, :n], in0=t1, in1=sin, op=ALU.mult)
    S.op("dve", "tensor_tensor", out=o2, in0=tmpa[0:np_, :n], in1=tmpb[0:np_, :n], op=ALU.add)


def kv_phase(S, C, x_in, Wd, ntok, KN, KR, VD, tabs):
    P = C["P"]
    with S.scope():
        wdn = S.sb("kv_wdn", [128, KD, 288], BF16)
        for kc in range(KD):
            S.dma(wdn.p(kc)[:, kc, :], Wd["kv_w_down"][kc * 128:(kc + 1) * 128, :], q="pool")
        wup_n = S.sb("kv_wupn", [128, 2, 768], BF16)
        wup_v = S.sb("kv_wupv", [128, 2, 768], BF16)
        src = dview(Wd["kv_w_up"], "(k p) (h c) -> p k h c", p=128, c=128)
        for kc in range(2):
            S.dma(dview(wup_n.p(kc)[:, kc, :], "p (h c) -> p h c", c=64), src[:, kc, :, 0:64], q="pool")
            S.dma(dview(wup_v.p(kc)[:, kc, :], "p (h c) -> p h c", c=64), src[:, kc, :, 64:128], q="pool")
        kvg = S.sb("kv_g", [128, KD], F32)
        latg = S.sb("kv_latg", [128, 2], F32)
        load_vec(S, kvg[:], Wd["kv_norm_g"])
        load_vec(S, latg[:], Wd["kv_latent_g"])
        xs = [S.sb(f"kv_x{i}", [128, KD, T], F32) for i in range(2)]
        hk = S.sb("kv_hk", [128, KD, T], BF16)
        sq = S.sb("kv_sq", [128, T], F32)
        rstd = S.sb("kv_rstd", [128, T], F32)
        ckv = S.sb("kv_ckv", [128, 2, T], F32)
        ckvn = S.sb("kv_ckvn", [128, 2, T], BF16)
        cs_t = S.sb("kv_cos", [16, T], F32)
        sn_t = S.sb("kv_sin", [16, T], F32)
        tmpa = S.sb("kv_tmpa", [16, T], F32)
        tmpb = S.sb("kv_tmpb", [16, T], F32)
        kr = [S.sb(f"kv_kr{i}", [16, 2, T], BF16) for i in range(2)]
        kn = [S.sb(f"kv_kn{i}", [128, T], BF16) for i in range(2)]
        vt = [S.sb(f"kv_vt{i}", [128, 4, 12, 65], BF16) for i in range(2)]
        for i in range(2):
            S.op("pool", "memset", ap=vt[i][:], constant=1.0, extra_writes=[vt[i][:]])
        xin_v = x_in.h.rearrange("(k p) t -> p k t", p=128)
        vd_v = VD.h.rearrange("n p h c -> p n h c")
        for it in range(ntok // T):
            ts = slice(it * T, (it + 1) * T)
            x = xs[it % 2]
            S.dma(x[:], x_in.ap(xin_v[:, :, ts], key=it), q="sp")
            S.dma(cs_t[:], tabs["cosk"][0:16, ts], q="sp")
            S.dma(sn_t[:], tabs["sink"][0:16, ts], q="sp")
            rms_rstd(S, C, x, P[0], sq, rstd, KD, C["ones1024"])
            norm_apply(S, hk, x, kvg, rstd, KD)
            for c in range(2):
                pm = P[1 + c]
                for kc in range(KD):
                    S.mm(pm[:], wdn.p(kc)[:, kc, c * 128:(c + 1) * 128], hk.p(kc)[:, kc, :], start=(kc == 0), stop=(kc == KD - 1))
                S.op("act", "activation", out=ckv[:, c, :], in_=pm[:], func=AF.Copy)
            for kc in range(KD):
                S.mm(P[3][0:16, :], wdn.p(kc)[:, kc, 256:272], hk.p(kc)[:, kc, :], start=(kc == 0), stop=(kc == KD - 1))
            for kc in range(KD):
                S.mm(P[4][0:16, :], wdn.p(kc)[:, kc, 272:288], hk.p(kc)[:, kc, :], start=(kc == 0), stop=(kc == KD - 1))
            rms_rstd(S, C, ckv, P[0], sq, rstd, 2, C["ones256"])
            norm_apply(S, ckvn, ckv, latg, rstd, 2)
            k = kr[it % 2]
            rope_apply(S, P[3][0:16, :], P[4][0:16, :], cs_t[:], sn_t[:], k[:, 0, :], k[:, 1, :], tmpa, tmpb, 16)
            S.dma(KR[0:16, ts], k[:, 0, :], q="pool")
            S.dma(KR[16:32, ts], k[:, 1, :], q="pool")
            for a in range(6):
                pm = P[1 + a % 2]
                for kc in range(2):
                    S.mm(pm[:], wup_n.p(kc)[:, kc, a * 128:(a + 1) * 128], ckvn.p(kc)[:, kc, :], start=(kc == 0), stop=(kc == 1))
                kk = kn[a % 2]
                S.op("act", "activation", out=kk[:], in_=pm[:], func=AF.Copy)
                S.dma(KN[2 * a, :, ts], kk[0:64, :], q="pool")
                S.dma(KN[2 * a + 1, :, ts], kk[64:128, :], q="pool")
            v = vt[it % 2]
            for st in range(4):
                for half in range(2):
                    pv = P[5 + half]
                    for kc in range(2):
                        S.mm(pv[:, 0:384], ckvn.p(kc)[:, kc, st * 128:(st + 1) * 128], wup_v.p(kc)[:, kc, half * 384:(half + 1) * 384],
                             start=(kc == 0), stop=(kc == 1))
                    S.op("act" if half == 0 else "dve", "activation" if half == 0 else "tensor_copy",
                         out=v[:, st, half * 6:(half + 1) * 6, 0:64], in_=dview(pv[:, 0:384], "p (h c) -> p h c", c=64),
                         **({"func": AF.Copy} if half == 0 else {}))
            S.dma(VD.ap(vd_v[:, it * 4:(it + 1) * 4, :, :], key=it), v[:], q="pool")


def mem_prep(S, C, Wd, memn):
    P = C["P"]
    with S.scope():
        mx = S.sb("mp_x", [128, KD, NMEM], F32)
        sq = S.sb("mp_sq", [128, NMEM], F32)
        rstd = S.sb("mp_rstd", [128, NMEM], F32)
        g = S.sb("mp_g", [128, KD], F32)
        S.dma(mx[:], dview(Wd["memT"], "(k p) t -> p k t", p=128), q="sp")
        load_vec(S, g[:], Wd["mem_norm_g"])
        rms_rstd(S, C, mx, P[0], sq, rstd, KD, C["ones1024"], n=NMEM)
        norm_apply(S, memn, mx, g, rstd, KD, n=NMEM)


def mem_kv(S, C, memn, w_kv, MK, MV, tag):
    P = C["P"]
    with S.scope():
        wkv = S.sb(tag + "wkv", [128, KD, 512], BF16)
        for kc in range(KD):
            S.dma(wkv.p(kc)[:, kc, :], w_kv[kc * 128:(kc + 1) * 128, :], q="pool")
        S.op("pool", "memset", ap=MV[:], constant=1.0, extra_writes=[MV[:]])
        for c in range(2):
            pm = P[1 + c]
            for kc in range(KD):
                S.mm(pm[:, 0:NMEM], wkv.p(kc)[:, kc, c * 128:(c + 1) * 128], memn[:, kc, :], start=(kc == 0), stop=(kc == KD - 1))
            S.op("act", "activation", out=MK[:, c, :], in_=pm[:, 0:NMEM], func=AF.Copy)
        for mt in range(2):
            pm = P[3 + mt]
            for kc in range(KD):
                S.mm(pm[:, 0:256], memn[:, kc, mt * 128:(mt + 1) * 128], wkv.p(kc)[:, kc, 256:512], start=(kc == 0), stop=(kc == KD - 1))
            S.op("dve", "tensor_copy", out=MV[:, mt, :, 0:64], in_=dview(pm[:, 0:256], "p (h c) -> p h c", c=64))


def mem_attn(S, C, qm, MK, MV, MD, ts, it, bufs):
    P = C["P"]
    pT, osb, rec, mo = bufs
    for hm in range(4):
        c, pb = hm // 2, (hm % 2) * 64
        for mt in range(2):
            ps = P[1 + mt]
            S.mm(ps[:], MK[pb:pb + 64, c, mt * 128:(mt + 1) * 128], qm[pb:pb + 64, c, :])
            p = pT[mt]
            S.op("act", "activation", out=p[:], in_=ps[:], func=AF.Exp, scale=SCALE_MEM)
            S.mm(P[5][0:65, :], MV[:, mt, hm, :], p[:], start=(mt == 0), stop=(mt == 1))
        m = mo[hm % 2]
        attn_finish(S, C, P[5], osb, P[6], rec, m[:])
        S.dma(MD.ap(MD.h[hm, :, ts], key=it), m[:], q="pool")


def mla_q_phase(S, C, x_in, Wd, j, ntok, QD, MD, memn, tabs):
    P = C["P"]
    tag = f"q{j}_"
    with S.scope():
        MK = S.sb(tag + "MK", [128, 2, NMEM], BF16)
        MV = S.sb(tag + "MV", [128, 2, 4, 65], BF16)
        mem_kv(S, C, memn, Wd["b_mem_kv"][j], MK, MV, tag)
        win = S.sb(tag + "win", [128, KD, 768], BF16)
        for kc in range(KD):
            S.dma(win.p(kc)[:, kc, :], Wd["b_w_in"][j, kc * 128:(kc + 1) * 128, :], q="pool")
        qn_w = S.sb(tag + "qn_w", [128, 4, 768], BF16)
        qr1_w = S.sb(tag + "qr1_w", [128, 4, 192], BF16)
        qr2_w = S.sb(tag + "qr2_w", [128, 4, 192], BF16)
        src = dview(Wd["b_q_up"][j], "(k p) (h c) -> p k h c", p=128, c=96)
        for kc in range(4):
            S.dma(dview(qn_w.p(kc)[:, kc, :], "p (h c) -> p h c", c=64), src[:, kc, :, 0:64], q="pool")
            S.dma(dview(qr1_w.p(kc)[:, kc, :], "p (h c) -> p h c", c=16), src[:, kc, :, 64:80], q="pool")
            S.dma(dview(qr2_w.p(kc)[:, kc, :], "p (h c) -> p h c", c=16), src[:, kc, :, 80:96], q="pool")
        g1 = S.sb(tag + "g1", [128, KD], F32)
        qg = S.sb(tag + "qg", [128, 4], F32)
        load_vec(S, g1[:], Wd["b_norm1_g"][j])
        load_vec(S, qg[:], Wd["b_q_norm_g"][j])
        xs = [S.sb(tag + f"x{i}", [128, KD, T], F32) for i in range(2)]
        h = S.sb(tag + "h", [128, KD, T], BF16)
        sq = S.sb(tag + "sq", [128, T], F32)
        rstd = S.sb(tag + "rstd", [128, T], F32)
        cq = S.sb(tag + "cq", [128, 4, T], F32)
        cqn = S.sb(tag + "cqn", [128, 4, T], BF16)
        qm = S.sb(tag + "qm", [128, 2, T], BF16)
        cs_t = S.sb(tag + "cos", [128, T], F32)
        sn_t = S.sb(tag + "sin", [128, T], F32)
        tmpa = S.sb(tag + "tmpa", [128, T], F32)
        tmpb = S.sb(tag + "tmpb", [128, T], F32)
        qn = [S.sb(tag + f"qn{i}", [128, T], BF16) for i in range(2)]
        qr = [S.sb(tag + f"qr{i}", [128, 2, T], BF16) for i in range(2)]
        pT = [S.sb(tag + f"pT{i}", [128, T], BF16) for i in range(2)]
        osb = S.sb(tag + "osb", [65, T], F32)
        rec = S.sb(tag + "rec", [64, T], F32)
        mo = [S.sb(tag + f"mo{i}", [64, T], BF16) for i in range(2)]
        xin_v = x_in.h.rearrange("(k p) t -> p k t", p=128)
        for it in range(ntok // T):
            ts = slice(it * T, (it + 1) * T)
            x = xs[it % 2]
            S.dma(x[:], x_in.ap(xin_v[:, :, ts], key=it), q="sp")
            S.dma(cs_t[:], tabs["cosq"][:, ts], q="sp")
            S.dma(sn_t[:], tabs["sinq"][:, ts], q="sp")
            rms_rstd(S, C, x, P[0], sq, rstd, KD, C["ones1024"])
            norm_apply(S, h, x, g1, rstd, KD)
            for c in range(6):
                pm = P[1 + c % 2]
                for kc in range(KD):
                    S.mm(pm[:], win.p(kc)[:, kc, c * 128:(c + 1) * 128], h.p(kc)[:, kc, :], start=(kc == 0), stop=(kc == KD - 1))
                if c < 4:
                    S.op("act", "activation", out=cq[:, c, :], in_=pm[:], func=AF.Copy)
                else:
                    S.op("act", "activation", out=qm[:, c - 4, :], in_=pm[:], func=AF.Copy)
            rms_rstd(S, C, cq, P[0], sq, rstd, 4, C["ones512"])
            norm_apply(S, cqn, cq, qg, rstd, 4)
            for a in range(6):
                pm = P[1 + a % 2]
                for kc in range(4):
                    S.mm(pm[:], qn_w.p(kc)[:, kc, a * 128:(a + 1) * 128], cqn.p(kc)[:, kc, :], start=(kc == 0), stop=(kc == 3))
                q = qn[a % 2]
                S.op("act", "activation", out=q[:], in_=pm[:], func=AF.Copy, scale=SCALE_MLA)
                S.dma(QD.ap(QD.h[2 * a, 0:64, ts], key=it), q[0:64, :], q="pool")
                S.dma(QD.ap(QD.h[2 * a + 1, 0:64, ts], key=it), q[64:128, :], q="pool")
            for grp, (h0, nh) in enumerate(((0, 8), (8, 4))):
                np_ = nh * 16
                for kc in range(4):
                    S.mm(P[3][0:np_, :], qr1_w.p(kc)[:, kc, h0 * 16:(h0 + nh) * 16], cqn.p(kc)[:, kc, :], start=(kc == 0), stop=(kc == 3))
                for kc in range(4):
                    S.mm(P[4][0:np_, :], qr2_w.p(kc)[:, kc, h0 * 16:(h0 + nh) * 16], cqn.p(kc)[:, kc, :], start=(kc == 0), stop=(kc == 3))
                r = qr[grp]
                rope_apply(S, P[3][0:np_, :], P[4][0:np_, :], cs_t[0:np_, :], sn_t[0:np_, :], r[0:np_, 0, :], r[0:np_, 1, :], tmpa, tmpb, np_)
                for hh in range(nh):
                    S.dma(QD.ap(QD.h[h0 + hh, 64:80, ts], key=it), r[hh * 16:(hh + 1) * 16, 0, :], q="pool")
                    S.dma(QD.ap(QD.h[h0 + hh, 80:96, ts], key=it), r[hh * 16:(hh + 1) * 16, 1, :], q="pool")
            mem_attn(S, C, qm, MK, MV, MD, ts, it, (pT, osb, rec, mo))


def attn_phase(S, C, QD, KN, KR, VD, OD, ntok, heads=range(12)):
    P = C["P"]
    NKT = ntok // 128
    with S.scope():
        Ks = [S.sb(f"at_K{i}", [96, ntok], BF16) for i in range(2)]
        Qs = [S.sb(f"at_Q{i}", [96, ntok], BF16) for i in range(2)]
        Vs = [S.sb(f"at_V{i}", [128, NKT, 65], BF16) for i in range(2)]
        pT = [S.sb(f"at_pT{i}", [128, T], BF16) for i in range(3)]
        osb = S.sb("at_osb", [65, T], F32)
        rec = S.sb("at_rec", [64, T], F32)
        oo = [S.sb(f"at_oo{i}", [64, T], BF16) for i in range(2)]
        vd_v = VD.h.rearrange("n p h c -> p n h c")
        cnt = 0
        for ih, hd in enumerate(heads):
            K, Q, Vv = Ks[ih % 2], Qs[ih % 2], Vs[ih % 2]
            S.dma(K.p("n")[0:64, :], KN[hd, :, :], q="sp")
            S.dma(K.p("r")[64:96, :], KR[:, :], q="sp")
            S.dma(Q[:], QD[hd, :, :], q="sp")
            S.dma(Vv[:], V(VD, None, vd_v[:, :, hd, :]), q="sp")
            for qi in range(ntok // T):
                po = P[3 + qi % 2]
                nk = 4 * qi + 4
                for kj in range(nk):
                    d = kj - 4 * qi
                    c0 = 128 * d if d > 0 else 0
                    ps = P[(1, 2, 7)[cnt % 3]]
                    p = pT[cnt % 3]
                    cnt += 1
                    S.mm(ps[:, c0:T], K[:, kj * 128:(kj + 1) * 128], Q[:, qi * T + c0:(qi + 1) * T])
                    S.op("act", "activation", out=p[:, c0:T], in_=ps[:, c0:T], func=AF.Exp)
                    if d >= 0:
                        S.op("dve", "tensor_tensor", out=p[:, c0:c0 + 128], in0=p[:, c0:c0 + 128], in1=C["tri"][:], op=ALU.mult)
                    S.mm(po[0:65, c0:T], Vv[:, kj, :], p[:, c0:T], start=(kj == 0), stop=(kj == nk - 1))
                o = oo[qi % 2]
                attn_finish(S, C, po, osb, P[6], rec, o[:])
                S.dma(OD.ap(OD.h[hd, :, qi * T:(qi + 1) * T], key=(hd, qi)), o[:], q="pool")


EXPM05 = math.exp(-0.5)
LNX_EPS = 64e-5
CH = 64
NCH = T // CH


def rwkv_layer(S, C, x_in, x_mid, Wd, i, ntok, memn, st):
    tag = f"r{i}_"
    SD = {k: S.dram(tag + k, [768, ntok], F32) for k in ("r", "sg", "k2", "v", "kk", "b")}
    GL = S.dram(tag + "gl", [128, ntok], BF16)
    MD = S.dram(tag + "MD", [4, 64, ntok], BF16)
    YD = S.dram(tag + "YD", [12, 64, ntok], BF16)
    if i == 0:
        st["VF"] = SD["v"]
    rwkv_proj_phase(S, C, x_in, Wd, i, ntok, memn, SD, GL, MD, st, tag)
    rwkv_scan_phase(S, C, Wd, i, ntok, SD, GL, YD, tag)
    outproj_phase(S, C, x_in, x_mid, YD, MD, Wd["a_w_out"][i], ntok, tag=tag + "o")


def rwkv_proj_phase(S, C, x_in, Wd, i, ntok, memn, SD, GL, MD, st, tag):
    P = C["P"]
    with S.scope():
        MK = S.sb(tag + "MK", [128, 2, NMEM], BF16)
        MV = S.sb(tag + "MV", [128, 2, 4, 65], BF16)
        mem_kv(S, C, memn, Wd["a_mem_kv"][i], MK, MV, tag)
        W1 = S.sb(tag + "W1", [128, KD, 2560], BF16)
        W2 = S.sb(tag + "W2", [128, KD, 2560], BF16)
        wq = S.sb(tag + "wq", [128, KD, 256], BF16)
        for kc in range(KD):
            S.dma(wq.p(kc)[:, kc, :], Wd["a_w_in"][i, kc * 128:(kc + 1) * 128, 2560:2816], q="pool")
        with S.scope():
            mu_bc = S.sb(tag + "mu_bc", [128, 2560], F32)
            omm_bc = S.sb(tag + "omm_bc", [128, 2560], F32)
            wf = [S.sb(tag + f"wf{k}", [128, 2560], F32) for k in range(2)]
            S.dma(mu_bc[:], V(Wd["a_shift_mu"].t, None, Wd["a_shift_mu"].ap[i].partition_broadcast(128)), q="sp")
            S.op("dve", "tensor_scalar", out=omm_bc[:], in0=mu_bc[:], scalar1=-1.0, scalar2=1.0, op0=ALU.mult, op1=ALU.add)
            for kc in range(KD):
                w = wf[kc % 2]
                S.dma(w[:], Wd["a_w_in"][i, kc * 128:(kc + 1) * 128, 0:2560], q="sp")
                S.op("dve", "tensor_tensor", out=W1.p(kc)[:, kc, :], in0=w[:], in1=omm_bc[:], op=ALU.mult)
                S.op("dve", "tensor_tensor", out=W2.p(kc)[:, kc, :], in0=w[:], in1=mu_bc[:], op=ALU.mult)
        dup = S.sb(tag + "dup", [128, 768], BF16)
        S.dma(dup.p("d")[0:64, :], Wd["a_decay_up"][i], q="pool")
        S.dma(dup.p("a")[64:128, :], Wd["a_aaa_up"][i], q="pool")
        vecs = S.sb(tag + "vecs", [128, 8, 6], F32)
        for vi, nm in enumerate(("a_decay_bias", "a_aaa_bias", "a_k_k", "a_k_a")):
            S.dma(vecs.p(vi)[:, vi, :], dview(Wd[nm][i], "(k p) -> p k", p=128), q="sp", allow_slow_non_contiguous=True)
        S.op("dve", "tensor_scalar", out=vecs.p(4)[:, 4, :], in0=vecs.p(3)[:, 3, :], scalar1=-1.0, scalar2=1.0, op0=ALU.mult, op1=ALU.add)
        g1 = S.sb(tag + "g1", [128, KD], F32)
        load_vec(S, g1[:], Wd["a_norm1_g"][i])
        if i > 0:
            S.dma(vecs.p(5)[:, 5, :], dview(Wd["vres_bias"][i - 1], "(k p) -> p k", p=128), q="sp", allow_slow_non_contiguous=True)
            vmu = S.sb(tag + "vmu", [128, KD, 2], F32)
            S.dma(vmu.p(0)[:, :, 0], dview(Wd["vres_mu"][i - 1], "(k p) -> p k", p=128), q="sp", allow_slow_non_contiguous=True)
            S.op("dve", "tensor_scalar", out=vmu.p(1)[:, :, 1], in0=vmu.p(0)[:, :, 0], scalar1=-1.0, scalar2=1.0, op0=ALU.mult, op1=ALU.add)
            vdf = S.sb(tag + "vdf", [128, KD, 32], F32)
            S.dma(vdf[:], dview(Wd["vres_down"][i - 1], "(k p) n -> p k n", p=128), q="sp")
            vd1 = S.sb(tag + "vd1", [128, KD, 32], BF16)
            vd2 = S.sb(tag + "vd2", [128, KD, 32], BF16)
            for kc in range(KD):
                S.op("dve", "tensor_scalar", out=vd1.p(kc)[:, kc, :], in0=vdf[:, kc, :], scalar1=vmu.p(1)[:, kc, 1:2], scalar2=None, op0=ALU.mult)
                S.op("dve", "tensor_scalar", out=vd2.p(kc)[:, kc, :], in0=vdf[:, kc, :], scalar1=vmu.p(0)[:, kc, 0:1], scalar2=None, op0=ALU.mult)
            vup = S.sb(tag + "vup", [32, 768], BF16)
            S.dma(vup[:], Wd["vres_up"][i - 1], q="pool")
            vlo = S.sb(tag + "vlo", [32, T], BF16)
        xs = [S.sb(tag + f"x{k}", [128, KD, T], F32) for k in range(2)]
        h = S.sb(tag + "h", [128, KD, T + 1], BF16)
        S.op("pool", "memset", ap=h[:], constant=0.0, extra_writes=[h[:]])
        sq = S.sb(tag + "sq", [128, T], F32)
        rstd = S.sb(tag + "rstd", [128, T], F32)
        lo = S.sb(tag + "lo", [128, T], BF16)
        gl = S.sb(tag + "gl", [128, T], BF16)
        qm = S.sb(tag + "qm", [128, 2, T], BF16)
        pT = [S.sb(tag + f"pT{k}", [128, T], BF16) for k in range(2)]
        osb = S.sb(tag + "osb", [65, T], F32)
        rec = S.sb(tag + "rec", [64, T], F32)
        mo = [S.sb(tag + f"mo{k}", [64, T], BF16) for k in range(2)]
        names = ("r", "k", "v", "sg", "lr", "kks", "kk", "k2", "b", "t1", "vf")
        tb = {nm: [S.sb(tag + f"t_{nm}{k}", [128, T], F32) for k in range(2)] for nm in names}
        xin_v = x_in.h.rearrange("(k p) t -> p k t", p=128)
        bones = C["bones"]

        def proj_chunk(pm, c):
            for kc in range(KD):
                S.mm(pm[:], W1.p(kc)[:, kc, c * 128:(c + 1) * 128], h.p(kc)[:, kc, 1:T + 1], start=(kc == 0), stop=False)
                S.mm(pm[:], W2.p(kc)[:, kc, c * 128:(c + 1) * 128], h.p(kc)[:, kc, 0:T], start=False, stop=(kc == KD - 1))

        for it in range(ntok // T):
            ts = slice(it * T, (it + 1) * T)
            x = xs[it % 2]
            S.dma(x[:], x_in.ap(xin_v[:, :, ts], key=it), q="sp")
            if it > 0:
                S.op("dve", "tensor_copy", out=h[:, :, 0], in_=h[:, :, T])
            rms_rstd(S, C, x, P[0], sq, rstd, KD, C["ones1024"])
            for kc in range(KD):
                S.op("dve", "scalar_tensor_tensor", out=h.p(kc)[:, kc, 1:T + 1], in0=x[:, kc, :], scalar=g1[:, kc:kc + 1],
                     in1=rstd[:], op0=ALU.mult, op1=ALU.mult)
            proj_chunk(P[1], 18)
            S.op("act", "activation", out=lo.p("d")[0:64, :], in_=P[1][0:64, :], func=AF.Tanh)
            S.op("act", "activation", out=lo.p("a")[64:128, :], in_=P[1][64:128, :], func=AF.Copy)
            proj_chunk(P[2], 19)
            S.op("act", "activation", out=gl[:], in_=P[2][:], func=AF.Sigmoid)
            S.dma(GL[:, ts], gl[:], q="pool")
            for c in range(2):
                pm = P[1 + c]
                for kc in range(KD):
                    S.mm(pm[:], wq.p(kc)[:, kc, c * 128:(c + 1) * 128], h.p(kc)[:, kc, 1:T + 1], start=(kc == 0), stop=(kc == KD - 1))
                S.op("act", "activation", out=qm[:, c, :], in_=pm[:], func=AF.Copy)
            mem_attn(S, C, qm, MK, MV, MD, ts, it, (pT, osb, rec, mo))
            if i > 0:
                for kc in range(KD):
                    S.mm(P[7][0:32, :], vd1.p(kc)[:, kc, :], h.p(kc)[:, kc, 1:T + 1], start=(kc == 0), stop=False)
                    S.mm(P[7][0:32, :], vd2.p(kc)[:, kc, :], h.p(kc)[:, kc, 0:T], start=False, stop=(kc == KD - 1))
                S.op("act", "activation", out=vlo[:], in_=P[7][0:32, :], func=AF.Copy)
            for c in range(6):
                b_ = c % 2
                t = {nm: tb[nm][b_] for nm in names}
                for j, nm in enumerate(("r", "k", "v")):
                    pm = P[1 + j]
                    proj_chunk(pm, j * 6 + c)
                    S.op("act", "activation", out=t[nm][:], in_=pm[:], func=AF.Copy)
                S.mm(P[4][:], dup.p("d")[0:64, c * 128:(c + 1) * 128], lo.p("d")[0:64, :])
                S.op("act", "activation", out=t["sg"][:], in_=P[4][:], func=AF.Sigmoid, bias=vecs.p(0)[:, 0, c:c + 1], scale=1.0)
                S.mm(P[5][:], dup.p("a")[64:128, c * 128:(c + 1) * 128], lo.p("a")[64:128, :])
                S.op("act", "activation", out=t["lr"][:], in_=P[5][:], func=AF.Sigmoid, bias=vecs.p(1)[:, 1, c:c + 1], scale=1.0)
                if i > 0:
                    S.mm(P[6][:], vup[:, c * 128:(c + 1) * 128], vlo[:])
                    S.op("act", "activation", out=t["t1"][:], in_=P[6][:], func=AF.Sigmoid, bias=vecs.p(5)[:, 5, c:c + 1], scale=1.0)
                    S.dma(t["vf"][:], st["VF"][c * 128:(c + 1) * 128, ts], q="sp")
                    S.op("dve", "tensor_tensor", out=t["vf"][:], in0=t["vf"][:], in1=t["v"][:], op=ALU.subtract)
                    S.op("dve", "tensor_tensor", out=t["vf"][:], in0=t["vf"][:], in1=t["t1"][:], op=ALU.mult)
                    S.op("dve", "tensor_tensor", out=t["v"][:], in0=t["v"][:], in1=t["vf"][:], op=ALU.add)
                S.op("dve", "tensor_scalar", out=t["kks"][:], in0=t["k"][:], scalar1=vecs.p(2)[:, 2, c:c + 1], scalar2=None, op0=ALU.mult)
                S.op("act", "activation", out=t["t1"][:], in_=t["kks"][:], func=AF.Square)
                S.mm(P[6][:], bones[:], t["t1"][:])
                S.op("act", "activation", out=t["t1"][:], in_=P[6][:], func=AF.Sqrt)
                S.op("dve", "tensor_scalar", out=t["t1"][:], in0=t["t1"][:], scalar1=1e-12, scalar2=None, op0=ALU.max)
                S.op("dve", "reciprocal", out=t["t1"][:], in_=t["t1"][:])
                S.op("dve", "tensor_tensor", out=t["kk"][:], in0=t["kks"][:], in1=t["t1"][:], op=ALU.mult)
                S.op("dve", "tensor_scalar", out=t["t1"][:], in0=t["lr"][:], scalar1=vecs.p(3)[:, 3, c:c + 1], scalar2=vecs.p(4)[:, 4, c:c + 1],
                     op0=ALU.mult, op1=ALU.add)
                S.op("dve", "tensor_tensor", out=t["k2"][:], in0=t["k"][:], in1=t["t1"][:], op=ALU.mult)
                S.op("dve", "tensor_tensor", out=t["b"][:], in0=t["kk"][:], in1=t["lr"][:], op=ALU.mult)
                for nm in ("r", "sg", "k2", "v", "kk", "b"):
                    S.dma(SD[nm][c * 128:(c + 1) * 128, ts], t[nm][:], q="pool")


def c3(v):
    return dview(v, "p (c j) -> p c j", j=CH)


def bcast_mid(v, n):
    p, j = v.ap.shape
    return V(v.t, v.k, v.ap.unsqueeze(1).broadcast_to([p, n, j]))


def bcast_last(v, j):
    p, n = v.ap.shape
    return V(v.t, v.k, v.ap.unsqueeze(2).broadcast_to([p, n, j]))


def to_bd(S, bd, src3, engs=("act", "dve")):
    for hh in range(2):
        ps = slice(hh * 64, hh * 64 + 64)
        if engs[hh] == "act":
            S.op("act", "activation", out=bd[ps, :, hh * 64:hh * 64 + 64], in_=src3[ps, :, :], func=AF.Copy)
        else:
            S.op("dve", "tensor_copy", out=bd[ps, :, hh * 64:hh * 64 + 64], in_=src3[ps, :, :])


def rwkv_scan_phase(S, C, Wd, i, ntok, SD, GL, YD, tag):
    P = C["P"]
    tag = tag + "s"
    ident = C["ident"]
    with S.scope():
        cm = S.sb(tag + "cm", [128, 4, CH], F32)
        S.dma(cm[:], Wd["cmask"], q="sp")
        mXs, mYs, mYi, Ist = (cm[:, k, :] for k in range(4))
        ones_col = S.sb(tag + "ones_col", [128, 1], F32)
        S.op("pool", "memset", ap=ones_col[:], constant=1.0, extra_writes=[ones_col[:]])
        epsl = S.sb(tag + "epsl", [128, 1], F32)
        S.op("pool", "memset", ap=epsl[:], constant=LNX_EPS, extra_writes=[epsl[:]])
        rkv = S.sb(tag + "rkv", [128, 6], F32)
        S.dma(rkv[:], V(Wd["a_r_k"].t, None, Wd["a_r_k"].ap[i].rearrange("h k -> (h k)").rearrange("(c p) -> p c", p=128)),
              q="sp", allow_slow_non_contiguous=True)
        gup = S.sb(tag + "gup", [128, 768], BF16)
        S.dma(gup[:], Wd["a_gate_up"][i], q="pool")
        lng = S.sb(tag + "lng", [128, 6, 64], F32)
        lnb = S.sb(tag + "lnb", [128, 6, 64], F32)
        for nm, dst in (("a_lnx_g", lng), ("a_lnx_b", lnb)):
            src = Wd[nm].ap[i].rearrange("(c h j) -> c h j", h=2, j=64)
            for hh in range(2):
                for c in range(6):
                    S.dma(dst.p((c, hh))[hh * 64:hh * 64 + 64, c, :],
                          V(Wd[nm].t, None, src[c, hh].partition_broadcast(64)), q="sp")
        names_ld = ("r", "sg", "k2", "v", "kk", "b")
        ld = {nm: [S.sb(tag + f"l_{nm}{k}", [128, T], F32) for k in range(2)] for nm in names_ld}
        gl = [S.sb(tag + f"gl{k}", [128, T], BF16) for k in range(2)]
        f = {nm: S.sb(tag + "f_" + nm, [128, T], F32) for nm in
             ("lw", "csA", "csB", "Ein", "Eex", "Eneg", "Eend", "tmp", "rt", "at", "bt")}
        bdn = ("rt", "at", "bt", "kt", "bh", "kh", "v", "u", "atT", "bhT", "khT", "X", "Y", "AKT", "RBT", "RKT", "AhT", "TT")
        bd = {nm: S.sb(tag + "bd_" + nm, [128, NCH, 128], F32) for nm in bdn}
        for nm in bdn:
            S.op("pool", "memset", ap=bd[nm][:], constant=0.0, extra_writes=[bd[nm][:]])
        s3 = {nm: S.sb(tag + "s3_" + nm, [128, NCH, CH], F32) for nm in
              ("Vtm", "Xs", "Ys", "ATs", "AVs", "U8", "y8", "yc", "ysq", "yt")}
        st2 = {nm: S.sb(tag + "st_" + nm, [128, NCH], F32) for nm in ("mean", "var", "rk")}
        Sst = [S.sb(tag + f"S{k}", [128, CH], F32) for k in range(2)]
        g_sb = S.sb(tag + "g_sb", [64, 2, T], F32)
        ym = [S.sb(tag + f"ym{k}", [64, T], BF16) for k in range(2)]
        sidx = 0
        for c in range(6):
            S.op("dve", "memset", ap=Sst[sidx][:], constant=0.0, extra_writes=[Sst[sidx][:]])
            for it in range(ntok // T):
                ts = slice(it * T, (it + 1) * T)
                bsel = (c * (ntok // T) + it) % 2
                L = {nm: ld[nm][bsel] for nm in names_ld}
                for nm in names_ld:
                    S.dma(L[nm][:], SD[nm][c * 128:(c + 1) * 128, ts], q="sp")
                glt = gl[bsel]
                S.dma(glt[:], GL[:, ts], q="sp")
                S.op("dve", "tensor_scalar", out=f["lw"][:], in0=L["sg"][:], scalar1=-EXPM05, scalar2=None, op0=ALU.mult)
                src, dst = f["lw"], f["csA"]
                for s_ in (1, 2, 4, 8, 16, 32):
                    S.op("act", "activation", out=c3(dst[:])[:, :, 0:s_], in_=c3(src[:])[:, :, 0:s_], func=AF.Copy)
                    S.op("dve", "tensor_tensor", out=c3(dst[:])[:, :, s_:CH], in0=c3(src[:])[:, :, s_:CH], in1=c3(src[:])[:, :, 0:CH - s_], op=ALU.add)
                    src = dst
                    dst = f["csB"] if dst is f["csA"] else f["csA"]
                cs = src
                S.op("act", "activation", out=f["Ein"][:], in_=cs[:], func=AF.Exp)
                S.op("act", "activation", out=f["Eneg"][:], in_=cs[:], func=AF.Exp, scale=-1.0)
                S.op("dve", "tensor_tensor", out=f["tmp"][:], in0=cs[:], in1=f["lw"][:], op=ALU.subtract)
                S.op("act", "activation", out=f["Eex"][:], in_=f["tmp"][:], func=AF.Exp)
                S.op("dve", "tensor_tensor", out=c3(f["tmp"][:]), in0=c3(cs[:]), in1=bcast_last(c3(cs[:])[:, :, CH - 1], CH), op=ALU.subtract)
                S.op("act", "activation", out=f["Eend"][:], in_=f["tmp"][:], func=AF.Exp, scale=-1.0)
                Wc = c3(f["Ein"][:])[:, :, CH - 1]
                S.op("dve", "tensor_tensor", out=f["rt"][:], in0=L["r"][:], in1=f["Ein"][:], op=ALU.mult)
                S.op("dve", "scalar_tensor_tensor", out=f["at"][:], in0=L["kk"][:], scalar=-1.0, in1=f["Eex"][:], op0=ALU.mult, op1=ALU.mult)
                S.op("dve", "tensor_tensor", out=f["bt"][:], in0=L["b"][:], in1=f["Eneg"][:], op=ALU.mult)
                to_bd(S, bd["rt"], c3(f["rt"][:]))
                to_bd(S, bd["at"], c3(f["at"][:]))
                to_bd(S, bd["bt"], c3(f["bt"][:]))
                to_bd(S, bd["v"], c3(L["v"][:]))
                for hh in range(2):
                    ps = slice(hh * 64, hh * 64 + 64)
                    cs_ = slice(hh * 64, hh * 64 + 64)
                    S.op("dve", "tensor_tensor", out=bd["kt"][ps, :, cs_], in0=c3(L["k2"][:])[ps], in1=c3(f["Eneg"][:])[ps], op=ALU.mult)
                    S.op("dve", "tensor_tensor", out=bd["bh"][ps, :, cs_], in0=c3(L["b"][:])[ps], in1=c3(f["Eend"][:])[ps], op=ALU.mult)
                    S.op("dve", "tensor_tensor", out=bd["kh"][ps, :, cs_], in0=c3(L["k2"][:])[ps], in1=c3(f["Eend"][:])[ps], op=ALU.mult)
                    S.op("dve", "scalar_tensor_tensor", out=bd["u"][ps, :, cs_], in0=c3(L["r"][:])[ps], scalar=rkv[ps, c:c + 1],
                         in1=c3(L["k2"][:])[ps], op0=ALU.mult, op1=ALU.mult)
                for nm_s, nm_d in (("at", "atT"), ("bh", "bhT"), ("kh", "khT")):
                    for half in range(2):
                        pt = P[half]
                        for q4 in range(4):
                            ch = half * 4 + q4
                            S.transpose(pt[:, q4 * 128:(q4 + 1) * 128], bd[nm_s][:, ch, :], ident[:])
                        S.op("act" if half == 0 else "dve", "activation" if half == 0 else "tensor_copy",
                             out=bd[nm_d][:, half * 4:half * 4 + 4, :], in_=dview(pt[:], "p (c j) -> p c j", j=128),
                             **({"func": AF.Copy} if half == 0 else {}))
                for half in range(2):
                    pt = P[half]
                    for q4 in range(4):
                        ch = half * 4 + q4
                        S.transpose(pt[:, q4 * 128:(q4 + 1) * 128], bd["v"][:, ch, :], ident[:])
                    p3 = dview(pt[:], "p (c j) -> p c j", j=128)
                    S.op("act", "activation", out=s3["Vtm"][:, half * 4:half * 4 + 4, :], in_=p3[:, :, 0:64], func=AF.Copy)
                    S.op("dve", "tensor_tensor", out=s3["Vtm"][:, half * 4:half * 4 + 4, :], in0=s3["Vtm"][:, half * 4:half * 4 + 4, :],
                         in1=p3[:, :, 64:128], op=ALU.add)
                def amat(pt, lhs_bd, rhs_f):
                    for ch in range(NCH):
                        S.mm(pt[:, ch * CH:(ch + 1) * CH], bd[lhs_bd][:, ch, :], f[rhs_f][:, ch * CH:(ch + 1) * CH])
                    return c3(pt[:])
                px = amat(P[0], "at", "bt")
                S.op("dve", "tensor_tensor", out=s3["Xs"][:], in0=px, in1=bcast_mid(mXs, NCH), op=ALU.mult)
                to_bd(S, bd["X"], s3["Xs"][:])
                py = amat(P[1], "bt", "at")
                S.op("dve", "tensor_tensor", out=s3["Ys"][:], in0=py, in1=bcast_mid(mYs, NCH), op=ALU.mult)
                to_bd(S, bd["Y"], s3["Ys"][:])
                S.op("dve", "tensor_tensor", out=s3["ATs"][:], in0=s3["Ys"][:], in1=bcast_mid(Ist, NCH), op=ALU.add)
                pk = amat(P[2], "kt", "at")
                S.op("dve", "tensor_tensor", out=s3["yt"][:], in0=pk, in1=bcast_mid(mYs, NCH), op=ALU.mult)
                to_bd(S, bd["AKT"], s3["yt"][:])
                prb = amat(P[0], "bt", "rt")
                S.op("dve", "tensor_tensor", out=s3["yc"][:], in0=prb, in1=bcast_mid(mYi, NCH), op=ALU.mult)
                to_bd(S, bd["RBT"], s3["yc"][:])
                prk = amat(P[1], "kt", "rt")
                S.op("dve", "tensor_tensor", out=s3["ysq"][:], in0=prk, in1=bcast_mid(mYi, NCH), op=ALU.mult)
                to_bd(S, bd["RKT"], s3["ysq"][:])
                for rnd in range(1, 6):
                    last = rnd == 5
                    if rnd > 1 or True:
                        pass
                    for ch in range(NCH):
                        S.mm(P[0][:, ch * CH:(ch + 1) * CH], bd["Y"][:, ch, :], s3["Xs"][:, ch, :])
                        S.mm(P[1][:, ch * CH:(ch + 1) * CH], bd["X"][:, ch, :], s3["Ys"][:, ch, :])
                    S.op("act", "activation", out=s3["Xs"][:], in_=c3(P[0][:]), func=AF.Copy)
                    S.op("dve", "tensor_copy", out=s3["Ys"][:], in_=c3(P[1][:]))
                    to_bd(S, bd["X"], s3["Xs"][:])
                    if not last:
                        to_bd(S, bd["Y"], s3["Ys"][:])
                    for ch in range(NCH):
                        S.mm(P[2][:, ch * CH:(ch + 1) * CH], bd["X"][:, ch, :], s3["ATs"][:, ch, :])
                    S.op("dve", "tensor_tensor", out=s3["ATs"][:], in0=s3["ATs"][:], in1=c3(P[2][:]), op=ALU.add)
                to_bd(S, bd["TT"], s3["ATs"][:])
                for ch in range(NCH):
                    S.mm(P[0][:, ch * CH:(ch + 1) * CH], bd["AKT"][:, ch, :], s3["Vtm"][:, ch, :])
                S.op("act", "activation", out=s3["AVs"][:], in_=c3(P[0][:]), func=AF.Copy)
                for ch in range(NCH):
                    S.mm(P[1][:, ch * CH:(ch + 1) * CH], bd["atT"][:, ch, :], s3["ATs"][:, ch, :])
                to_bd(S, bd["AhT"], c3(P[1][:]))
                for ch in range(NCH):
                    S.mm(P[3][:, ch:ch + 1], bd["u"][:, ch, :], ones_col[:])
                S.op("dve", "tensor_copy", out=st2["rk"][:], in_=P[3][:, 0:NCH])
                for hh in range(2):
                    S.mm(P[7][0:64, :], gup[:, (2 * c + hh) * 64:(2 * c + hh + 1) * 64], glt[:])
                    S.op("act", "activation", out=g_sb[:, hh, :], in_=P[7][0:64, :], func=AF.Copy)
                for ch in range(NCH):
                    S0 = Sst[sidx]
                    S1 = Sst[1 - sidx]
                    cc = slice(ch * CH, (ch + 1) * CH)
                    S.mm(P[4].p(ch)[:, cc], bd["AhT"][:, ch, :], S0[:], start=True, stop=False)
                    S.mm(P[4].p(ch)[:, cc], bd["TT"][:, ch, :], s3["AVs"][:, ch, :], start=False, stop=True)
                    S.op("act", "activation", out=s3["U8"].p(ch)[:, ch, :], in_=P[4].p(ch)[:, cc], func=AF.Copy)
                    S.mm(P[6].p(ch)[:, cc], bd["bhT"][:, ch, :], s3["U8"].p(ch)[:, ch, :], start=True, stop=False)
                    S.mm(P[6].p(ch)[:, cc], bd["khT"][:, ch, :], s3["Vtm"][:, ch, :], start=False, stop=True)
                    S.op("dve", "scalar_tensor_tensor", out=S1[:], in0=S0[:], scalar=Wc[:, ch:ch + 1], in1=P[6].p(ch)[:, cc],
                         op0=ALU.mult, op1=ALU.add)
                    S.mm(P[5].p(ch)[:, cc], bd["rt"][:, ch, :], S0[:], start=True, stop=False)
                    S.mm(P[5].p(ch)[:, cc], bd["RBT"][:, ch, :], s3["U8"].p(ch)[:, ch, :], start=False, stop=False)
                    S.mm(P[5].p(ch)[:, cc], bd["RKT"][:, ch, :], s3["Vtm"][:, ch, :], start=False, stop=True)
                    sidx = 1 - sidx
                y8, yc, ysq, yt = s3["y8"], s3["yc"], s3["ysq"], s3["yt"]
                S.op("act", "activation", out=y8[:], in_=c3(P[5][:]), func=AF.Copy)
                S.op("dve", "tensor_reduce", out=st2["mean"][:], in_=y8[:], axis=AX.X, op=ALU.add)
                S.op("dve", "tensor_scalar", out=st2["mean"][:], in0=st2["mean"][:], scalar1=1.0 / CH, scalar2=None, op0=ALU.mult)
                S.op("dve", "tensor_tensor", out=yc[:], in0=y8[:], in1=bcast_last(st2["mean"][:], CH), op=ALU.subtract)
                S.op("act", "activation", out=ysq[:], in_=yc[:], func=AF.Square)
                S.op("dve", "tensor_reduce", out=st2["var"][:], in_=ysq[:], axis=AX.X, op=ALU.add)
                S.op("act", "activation", out=st2["var"][:], in_=st2["var"][:], func=AF.Sqrt, bias=epsl[:, 0:1], scale=1.0 / CH)
                S.op("dve", "reciprocal", out=st2["var"][:], in_=st2["var"][:])
                S.op("dve", "tensor_tensor", out=yc[:], in0=yc[:], in1=bcast_last(st2["var"][:], CH), op=ALU.mult)
                S.op("dve", "tensor_tensor", out=yc[:], in0=yc[:], in1=bcast_mid(lng[:, c, :], NCH), op=ALU.mult)
                S.op("dve", "tensor_tensor", out=yc[:], in0=yc[:], in1=bcast_mid(lnb[:, c, :], NCH), op=ALU.add)
                S.op("dve", "tensor_tensor", out=yt[:], in0=s3["Vtm"][:], in1=bcast_last(st2["rk"][:], CH), op=ALU.mult)
                S.op("dve", "tensor_tensor", out=yc[:], in0=yc[:], in1=yt[:], op=ALU.add)
                for half in range(2):
                    pt = P[half]
                    for q4 in range(4):
                        ch = half * 4 + q4
                        S.transpose(pt[0:64, q4 * 128:(q4 + 1) * 128], yc[:, ch, :], ident[:])
                    p4 = dview(pt[0:64, :], "p (c h j) -> p c h j", h=2, j=CH)
                    for hh in range(2):
                        S.op("dve", "tensor_tensor", out=dview(ym[hh][:, half * 256:(half + 1) * 256], "p (c j) -> p c j", j=CH),
                             in0=p4[:, :, hh, :], in1=dview(g_sb[:, hh, half * 256:(half + 1) * 256], "p (c j) -> p c j", j=CH), op=ALU.mult)
                for hh in range(2):
                    S.dma(YD.ap(YD.h[2 * c + hh, :, ts], key=(c, it, hh)), ym[hh][:], q="pool")


WSHAPES = {
    'mem_norm_g': (D,),
    'a_norm1_g': ('A', D), 'a_w_in': ('A', D, 2816), 'a_shift_mu': ('A', 2560), 'a_decay_up': ('A', 64, 768),
    'a_decay_bias': ('A', 768), 'a_aaa_up': ('A', 64, 768), 'a_aaa_bias': ('A', 768), 'a_gate_up': ('A', 128, 768),
    'a_k_k': ('A', 768), 'a_k_a': ('A', 768), 'a_r_k': ('A', 12, 64), 'a_lnx_g': ('A', 768), 'a_lnx_b': ('A', 768),
    'a_mem_kv': ('A', D, 512), 'a_w_out': ('A', D, D), 'a_norm2_g': ('A', D), 'a_ffn_gu': ('A', D, 2 * FH),
    'a_ffn_down': ('A', FH, D),
    'vres_mu': ('V', D), 'vres_down': ('V', D, 32), 'vres_up': ('V', 32, 768), 'vres_bias': ('V', 768),
    'kv_norm_g': (D,), 'kv_w_down': (D, 288), 'kv_latent_g': (256,), 'kv_w_up': (256, 1536),
    'b_norm1_g': ('B', D), 'b_w_in': ('B', D, 768), 'b_q_norm_g': ('B', 512), 'b_q_up': ('B', 512, 1152),
    'b_mem_kv': ('B', D, 512), 'b_w_out': ('B', D, D), 'b_norm2_g': ('B', D), 'b_ffn_gu': ('B', D, 2 * FH),
    'b_ffn_down': ('B', FH, D),
    'final_norm_g': (D,),
}


def build_program(ntok, N_A, N_B):
    nc = bass.Bass("TRN2", target_bir_lowering=False)
    dims = {'A': N_A, 'B': N_B, 'V': max(N_A - 1, 0)}
    with ExitStack() as stack:
        S = Sched(nc, stack)
        Wd = {}
        Wd["xT"] = S.dram("xT", [D, ntok], F32, kind="ExternalInput")
        Wd["memT"] = S.dram("memT", [D, NMEM], F32, kind="ExternalInput")[:]
        Wd["pos"] = S.dram("pos", [ntok], I32, kind="ExternalInput")[:]
        Wd["invf"] = S.dram("invf", [128], F32, kind="ExternalInput")[:]
        Wd["cmask"] = S.dram("cmask", [128, 4, 64], F32, kind="ExternalInput")[:]
        for nm, shp in WSHAPES.items():
            shp = [dims[s] if isinstance(s, str) else s for s in shp]
            if 0 in shp:
                continue
            Wd[nm] = S.dram(nm, shp, F32, kind="ExternalInput")[:]
        outT = S.dram("outT", [D, ntok], F32, kind="ExternalOutput")
        C = consts(S, Wd)
        memn = S.sb("memn", [128, KD, NMEM], BF16)
        mem_prep(S, C, Wd, memn)
        x_cur = Wd["xT"]
        nlayers = N_A + N_B
        li = 0
        st = {}
        for i in range(N_A):
            x_mid = S.dram(f"xmid{li}", [D, ntok], F32)
            rwkv_layer(S, C, x_cur, x_mid, Wd, i, ntok, memn, st)
            li += 1
            last = (li == nlayers)
            x_next = outT if last else S.dram(f"xres{li}", [D, ntok], F32)
            outs = ffn_phase(S, C, x_mid, x_next, Wd["a_norm2_g"][i], Wd["a_ffn_gu"][i], Wd["a_ffn_down"][i], ntok,
                             tag=f"fa{i}_", final_g=Wd["final_norm_g"] if last else None)
            x_cur = x_next
        if N_B > 0:
            tabs = {k: S.dram("tab_" + k, [128, ntok], F32) for k in ("cosk", "sink", "cosq", "sinq")}
            rope_tables_phase(S, C, Wd["pos"], Wd["invf"], tabs, ntok)
            KN = S.dram("KN", [12, 64, ntok], BF16)
            KR = S.dram("KR", [32, ntok], BF16)
            VD = S.dram("VD", [ntok // 128, 128, 12, 65], BF16)
            kv_phase(S, C, x_cur, Wd, ntok, KN, KR, VD, tabs)
        for j in range(N_B):
            QD = S.dram(f"QD{j}", [12, 96, ntok], BF16)
            MD = S.dram(f"MD{j}", [4, 64, ntok], BF16)
            OD = S.dram(f"OD{j}", [12, 64, ntok], BF16)
            PH = ("q", "attn", "out")
            if "q" in PH:
                mla_q_phase(S, C, x_cur, Wd, j, ntok, QD, MD, memn, tabs)
            if "attn" in PH:
                attn_phase(S, C, QD, KN, KR, VD, OD, ntok)
            x_mid = S.dram(f"xmid{li}", [D, ntok], F32)
            if "out" in PH:
                outproj_phase(S, C, x_cur, x_mid, OD, MD, Wd["b_w_out"][j], ntok, tag=f"ob{j}_")
            else:
                x_mid = x_cur
            li += 1
            last = (li == nlayers)
            x_next = outT if last else S.dram(f"xres{li}", [D, ntok], F32)
            outs = ffn_phase(S, C, x_mid, x_next, Wd["b_norm2_g"][j], Wd["b_ffn_gu"][j], Wd["b_ffn_down"][j], ntok,
                             tag=f"fb{j}_", final_g=Wd["final_norm_g"] if last else None)
            x_cur = x_next
        S.barrier_wait("sp", outs)
        S.emit()
        print("stats", S.stats, flush=True)
    return nc


def host_inputs(inputs, b, ntok):
    m = {}
    m["xT"] = np.ascontiguousarray(np.asarray(inputs["x"])[b, :ntok].T)
    m["memT"] = np.ascontiguousarray(np.asarray(inputs["mem"])[b].T)
    m["pos"] = np.ascontiguousarray(np.asarray(inputs["positions"])[b, :ntok]).astype(np.int32)
    m["invf"] = np.tile((10000.0 ** (-(np.arange(16, dtype=np.float32)) / 16)).astype(np.float32), 8)
    pj = (np.arange(128) % 64)[:, None]
    jj = np.arange(64)[None, :]
    m["cmask"] = np.ascontiguousarray(np.stack([jj < pj, jj > pj, jj >= pj, jj == pj], axis=1).astype(np.float32))
    for nm in WSHAPES:
        a = np.asarray(inputs[nm])
        if a.size == 0:
            continue
        m[nm] = np.ascontiguousarray(a, dtype=np.float32)
    return m

SEQ_FULL = 8192
_NC_CACHE = {}


def kernel(**inputs):
    x = np.asarray(inputs["x"])
    B, S_len, _ = x.shape
    key = (S_len,)
    if key not in _NC_CACHE:
        _NC_CACHE[key] = build_program(S_len, 2, 2)
    nc = _NC_CACHE[key]
    in_maps = [host_inputs(inputs, b, S_len) for b in range(B)]
    res = run_bass_kernel_spmd(nc, in_maps, core_ids=list(range(B)))
    out = np.stack([np.ascontiguousarray(res.results[b]["outT"].T) for b in range(B)], axis=0)
    return out.astype(np.float32)
```

```python
import math
import numpy as np
from contextlib import ExitStack, contextmanager
import concourse.bass as bass
import concourse.mybir as mybir
from concourse.bass_utils import run_bass_kernel_spmd


F32 = mybir.dt.float32
BF16 = mybir.dt.bfloat16
I32 = mybir.dt.int32
ALU = mybir.AluOpType
AF = mybir.ActivationFunctionType
AX = mybir.AxisListType

ENGS = ("pe", "dve", "act", "pool", "sp")
NSEM = 6
NDMASEM = 24


class Dep:
    __slots__ = ("w", "r")

    def __init__(self):
        self.w = None
        self.r = []


class Tile:
    def __init__(self, h, name):
        self.h = h
        self.name = name
        self.whole = Dep()
        self.parts = {}

    def __getitem__(self, idx):
        return V(self, None, self.h[idx])

    def p(self, key, idx=None):
        return V(self, key, self.h[idx] if idx is not None else self.h[:])

    def ap(self, ap, key=None):
        return V(self, key, ap)


class V:
    __slots__ = ("t", "k", "ap")

    def __init__(self, t, k, ap):
        self.t, self.k, self.ap = t, k, ap

    def __getitem__(self, idx):
        return V(self.t, self.k, self.ap[idx])

    def deps(self):
        t = self.t
        if self.k is None:
            return [t.whole] + list(t.parts.values())
        if self.k not in t.parts:
            t.parts[self.k] = Dep()
        return [t.whole, t.parts[self.k]]

    def own(self):
        t = self.t
        if self.k is None:
            return t.whole
        return t.parts[self.k]


class Op:
    __slots__ = ("id", "eng", "fn", "deps", "signal", "sem", "val", "isdma", "waits")

    def __init__(self, id, eng, fn, isdma):
        self.id, self.eng, self.fn, self.isdma = id, eng, fn, isdma
        self.deps = set()
        self.signal = False
        self.sem = None
        self.val = 0
        self.waits = []


class Sched:
    def __init__(self, nc, stack):
        self.nc = nc
        self.stack = stack
        self.ops = []
        self.n_sb = 0

    def sb(self, name, shape, dt=F32):
        h = self.stack.enter_context(self.nc.sbuf_tensor(name, list(shape), dt))
        return Tile(h, name)

    def ps(self, name, shape, dt=F32):
        h = self.stack.enter_context(self.nc.psum_tensor(name, list(shape), dt))
        return Tile(h, name)

    def dram(self, name, shape, dt=F32, kind="Internal"):
        h = self.nc.dram_tensor(name, list(shape), dt, kind=kind)
        return Tile(h.ap(), name)

    def add(self, eng, fn, reads=(), writes=(), isdma=False):
        op = Op(len(self.ops), eng, fn, isdma)
        for v in reads:
            for d in v.deps():
                if d.w is not None:
                    op.deps.add(d.w)
        for v in writes:
            for d in v.deps():
                if d.w is not None:
                    op.deps.add(d.w)
                op.deps.update(d.r)
        for v in reads:
            v.own().r.append(op.id)
        for v in writes:
            o = v.own()
            o.w = op.id
            o.r = []
            if v.k is None:
                v.t.parts = {}
        op.deps.discard(op.id)
        if eng == "pe":
            op.deps = {d for d in op.deps if self.ops[d].eng != "pe" or self.ops[d].isdma}
        self.ops.append(op)
        return op

    def op(self, eng, method, extra_reads=(), extra_writes=(), isdma=False, **kw):
        reads, writes, args = list(extra_reads), list(extra_writes), {}
        for k, v in kw.items():
            if isinstance(v, V):
                (writes if (k.startswith("out") or k == "accum_out") else reads).append(v)
                args[k] = v.ap
            else:
                args[k] = v
        return self.add(eng, lambda e: getattr(e, method)(**args), reads=reads, writes=writes, isdma=isdma)

    def dma(self, out, in_, q="sp", **kw):
        return self.op(q, "dma_start", isdma=True, out=out, in_=in_, **kw)

    def mm(self, out, lhsT, rhs, start=True, stop=True):
        o, l, r = out.ap, lhsT.ap, rhs.ap
        return self.add("pe", lambda e: e.matmul(o, l, r, start=start, stop=stop),
                        reads=[lhsT, rhs], writes=[out])

    def transpose(self, out, in_, ident):
        o, i, d = out.ap, in_.ap, ident.ap
        return self.add("pe", lambda e: e.transpose(o, i, d),
                        reads=[in_, ident], writes=[out])

    def barrier_wait(self, eng, ops):
        op = Op(len(self.ops), eng, None, False)
        op.deps = {o.id for o in ops}
        self.ops.append(op)
        return op

    def emit(self):
        nc = self.nc
        ops = self.ops
        for op in ops:
            for d in op.deps:
                ops[d].signal = True
        sems = {}
        for e in ("pe", "dve", "act", "pool"):
            sems[e] = [self.stack.enter_context(nc.semaphore(f"s_{e}{i}")) for i in range(NSEM)]
        dsems = {}
        dma_queues = sorted({op.eng for op in ops if op.isdma})
        for q in dma_queues:
            dsems[q] = [self.stack.enter_context(nc.semaphore(f"d_{q}{i}")) for i in range(NDMASEM)]
        cnt = {e: 0 for e in ENGS}
        dcnt = {q: 0 for q in dma_queues}
        prev_on_slot = {}
        for op in ops:
            if op.isdma:
                k = dcnt[op.eng]
                dcnt[op.eng] += 1
                slot = k % NDMASEM
                op.sem = dsems[op.eng][slot]
                op.val = 16 * (k // NDMASEM + 1)
                pk = (op.eng, slot)
                if pk in prev_on_slot:
                    op.deps.add(prev_on_slot[pk])
                prev_on_slot[pk] = op.id
                op.signal = True
            elif op.signal:
                k = cnt[op.eng]
                cnt[op.eng] += 1
                op.sem = sems[op.eng][k % NSEM]
                op.val = k // NSEM + 1
        waited = {e: {} for e in ENGS}
        per_eng = {e: [] for e in ENGS}
        for op in ops:
            need = {}
            for d in op.deps:
                p = ops[d]
                key = id(p.sem)
                if key not in need or need[key][1] < p.val:
                    need[key] = (p.sem, p.val)
            w = waited[op.eng]
            for key, (sem, val) in need.items():
                if w.get(key, 0) >= val:
                    continue
                w[key] = val
                op.waits.append((sem, val))
            per_eng[op.eng].append(op)
        self.stats = {e: len(per_eng[e]) for e in ENGS}
        self.stats["waits"] = sum(len(o.waits) for o in ops)

        def run(eng_obj, lst):
            for op in lst:
                for sem, val in op.waits:
                    eng_obj.wait_ge(sem, val)
                if op.fn is None:
                    continue
                ins = op.fn(eng_obj)
                if op.signal:
                    ins.then_inc(op.sem, 16 if op.isdma else 1)

        with nc.Block() as block:
            @block.sync
            def _(e):
                run(e, per_eng["sp"])

            @block.tensor
            def _(e):
                run(e, per_eng["pe"])

            @block.vector
            def _(e):
                run(e, per_eng["dve"])

            @block.scalar
            def _(e):
                run(e, per_eng["act"])

            @block.gpsimd
            def _(e):
                run(e, per_eng["pool"])


def _sched_scope_init(self):
    if not hasattr(self, "stacks"):
        self.stacks = [self.stack]
        self.last_barrier = 0


def _sb(self, name, shape, dt=F32):
    _sched_scope_init(self)
    self.n_sb += 1
    name = f"{name}_{self.n_sb}"
    h = self.stacks[-1].enter_context(self.nc.sbuf_tensor(name, list(shape), dt))
    return Tile(h, name)


def _ps(self, name, shape, dt=F32):
    _sched_scope_init(self)
    h = self.stacks[-1].enter_context(self.nc.psum_tensor(name, list(shape), dt))
    return Tile(h, name)


def _barrier_all(self):
    _sched_scope_init(self)
    last = {}
    dmas = set()
    for op in self.ops[self.last_barrier:]:
        if op.fn is None:
            continue
        last[op.eng] = op.id
        if op.isdma:
            dmas.add(op.id)
    deps = set(last.values()) | dmas
    if not deps:
        return
    for e in ENGS:
        b = Op(len(self.ops), e, None, False)
        b.deps = set(deps)
        self.ops.append(b)
    self.last_barrier = len(self.ops)


@contextmanager
def _scope(self):
    _sched_scope_init(self)
    st = ExitStack()
    self.stacks.append(st)
    try:
        yield
    finally:
        self.barrier_all()
        self.stacks.pop()
        st.close()


Sched.sb = _sb
Sched.ps = _ps
Sched.barrier_all = _barrier_all
Sched.scope = _scope


D = 1024
KD = D // 128
FH = 2816
HD = 64
NMEM = 256
T = 512


def dview(v, pattern, **kw):
    return V(v.t, v.k, v.ap.rearrange(pattern, **kw))


def consts(S, Wd):
    C = {}
    for nm, val in (("ones1024", 1.0 / 1024), ("ones512", 1.0 / 512), ("ones256", 1.0 / 256)):
        C[nm] = S.sb(nm, [128, 128], F32)
        S.op("pool", "memset", ap=C[nm][:], constant=val, extra_writes=[C[nm][:]])
    C["ones_mean"] = C["ones1024"]
    C["eps6"] = S.sb("eps6", [128, 1], F32)
    S.op("pool", "memset", ap=C["eps6"][:], constant=1e-6, extra_writes=[C["eps6"][:]])
    C["sel"] = S.sb("sel", [128, 64], F32)
    S.op("pool", "memset", ap=C["sel"][:], constant=0.0, extra_writes=[C["sel"][:]])
    S.op("pool", "memset", ap=C["sel"][64:65, :], constant=1.0, extra_writes=[C["sel"][:]])
    C["ident"] = S.sb("ident", [128, 128], F32)
    S.op("pool", "memset", ap=C["ident"][:], constant=1.0, extra_writes=[C["ident"][:]])
    S.op("pool", "affine_select", out=C["ident"][:], in_=C["ident"][:], pattern=[[-1, 128]],
         compare_op=ALU.is_equal, fill=0.0, base=0, channel_multiplier=1)
    C["tri"] = S.sb("tri", [128, 128], BF16)
    S.op("pool", "memset", ap=C["tri"][:], constant=1.0, extra_writes=[C["tri"][:]])
    S.op("pool", "affine_select", out=C["tri"][:], in_=C["tri"][:], pattern=[[1, 128]],
         compare_op=ALU.is_ge, fill=0.0, base=0, channel_multiplier=-1)
    C["bones"] = S.sb("bones", [128, 128], F32)
    S.op("pool", "memset", ap=C["bones"][:], constant=0.0, extra_writes=[C["bones"][:]])
    S.op("pool", "memset", ap=C["bones"][0:64, 0:64], constant=1.0, extra_writes=[C["bones"][:]])
    S.op("pool", "memset", ap=C["bones"][64:128, 64:128], constant=1.0, extra_writes=[C["bones"][:]])
    C["P"] = [S.ps(f"P{i}", [128, 512], F32) for i in range(8)]
    return C


def rms_rstd(S, C, x, ps_ss, sq, rstd, nk, ones, n=T):
    for kc in range(nk):
        S.op("act", "activation", out=sq[:, :n], in_=x[:, kc, :], func=AF.Square)
        S.mm(ps_ss[:, :n], ones[:], sq[:, :n], start=(kc == 0), stop=(kc == nk - 1))
    S.op("act", "activation", out=rstd[:, :n], in_=ps_ss[:, :n], func=AF.Sqrt, bias=C["eps6"][:, 0:1], scale=1.0)
    S.op("dve", "reciprocal", out=rstd[:, :n], in_=rstd[:, :n])


def load_vec(S, dst, src, q="sp"):
    S.dma(dst, dview(src, "(k p) -> p k", p=128), q=q, allow_slow_non_contiguous=True)


def norm_apply(S, h, x, g_sb, rstd, nk, n=T):
    for kc in range(nk):
        S.op("dve", "scalar_tensor_tensor", out=h.p(kc)[:, kc, :], in0=x[:, kc, :], scalar=g_sb[:, kc:kc + 1],
             in1=rstd[:, :n], op0=ALU.mult, op1=ALU.mult)


def attn_finish(S, C, ps_o, osb, ps_d, rec, out_v, n=T):
    S.op("act", "activation", out=osb[0:65, :n], in_=ps_o[0:65, :n], func=AF.Copy)
    S.mm(ps_d[0:64, :n], C["sel"][0:65, 0:64], osb[0:65, :n])
    S.op("dve", "reciprocal", out=rec[0:64, :n], in_=ps_d[0:64, :n])
    S.op("dve", "tensor_tensor", out=out_v, in0=osb[0:64, :n], in1=rec[0:64, :n], op=ALU.mult)


def ffn_phase(S, C, x_in, x_out, g_dram, gu_dram, down_dram, ntok, FHc=FH, tag="f", final_g=None):
    NH = FHc // 128
    outs = []
    with S.scope():
        gu_sb = S.sb(tag + "gu_sb", [128, KD, 2 * FHc], BF16)
        dn_sb = S.sb(tag + "dn_sb", [128, NH, D], BF16)
        g_sb = S.sb(tag + "g_sb", [128, KD], F32)
        for kc in range(KD):
            S.dma(gu_sb.p(("k", kc))[:, kc, :], gu_dram[kc * 128:(kc + 1) * 128, :], q="pool")
        for j in range(NH):
            S.dma(dn_sb.p(("j", j))[:, j, :], down_dram[j * 128:(j + 1) * 128, :], q="pool")
        load_vec(S, g_sb[:], g_dram)
        if final_g is not None:
            fg_sb = S.sb(tag + "fg_sb", [128, KD], F32)
            load_vec(S, fg_sb[:], final_g)
        xs = [S.sb(tag + f"x{i}", [128, KD, T], F32) for i in range(2)]
        h = S.sb(tag + "h", [128, KD, T], BF16)
        sq = S.sb(tag + "sq", [128, T], F32)
        rstd = S.sb(tag + "rstd", [128, T], F32)
        hid = S.sb(tag + "hid", [128, NH, T], BF16)
        sg = [S.sb(tag + f"sg{i}", [128, T], BF16) for i in range(2)]
        P = C["P"]
        ps_ss, ps_g, ps_u, ps_o = P[0], P[1:3], P[3:5], P[5:7]
        xin_v = x_in.h.rearrange("(k p) t -> p k t", p=128)
        xout_v = x_out.h.rearrange("(k p) t -> p k t", p=128)
        for it in range(ntok // T):
            x = xs[it % 2]
            S.dma(x[:], x_in.ap(xin_v[:, :, it * T:(it + 1) * T], key=it), q="sp")
            rms_rstd(S, C, x, ps_ss, sq, rstd, KD, C["ones1024"])
            norm_apply(S, h, x, g_sb, rstd, KD)
            for j in range(NH):
                pg, pu = ps_g[j % 2], ps_u[j % 2]
                for kc in range(KD):
                    S.mm(pg[:], gu_sb.p(("k", kc))[:, kc, j * 128:(j + 1) * 128], h.p(kc)[:, kc, :],
                         start=(kc == 0), stop=(kc == KD - 1))
                for kc in range(KD):
                    S.mm(pu[:], gu_sb.p(("k", kc))[:, kc, FHc + j * 128:FHc + (j + 1) * 128], h.p(kc)[:, kc, :],
                         start=(kc == 0), stop=(kc == KD - 1))
                s = sg[j % 2]
                S.op("act", "activation", out=s[:], in_=pg[:], func=AF.Silu)
                S.op("dve", "tensor_tensor", out=hid.p(j)[:, j, :], in0=s[:], in1=pu[:], op=ALU.mult)
            for oc in range(KD):
                po = ps_o[oc % 2]
                for j in range(NH):
                    S.mm(po[:], dn_sb.p(("j", j))[:, j, oc * 128:(oc + 1) * 128], hid.p(j)[:, j, :],
                         start=(j == 0), stop=(j == NH - 1))
                S.op("dve", "tensor_tensor", out=x[:, oc, :], in0=x[:, oc, :], in1=po[:], op=ALU.add)
            if final_g is not None:
                rms_rstd(S, C, x, ps_ss, sq, rstd, KD, C["ones1024"])
                for kc in range(KD):
                    S.op("dve", "scalar_tensor_tensor", out=x[:, kc, :], in0=x[:, kc, :], scalar=fg_sb[:, kc:kc + 1],
                         in1=rstd[:], op0=ALU.mult, op1=ALU.mult)
            outs.append(S.dma(x_out.ap(xout_v[:, :, it * T:(it + 1) * T], key=it), x[:], q="pool"))
    return outs


def outproj_phase(S, C, x_in, x_out, YD, MD, w_out, ntok, tag="o"):
    with S.scope():
        wo = S.sb(tag + "wo", [64, 16, D], BF16)
        S.dma(wo[:], dview(w_out, "(c p) n -> p c n", p=64), q="pool")
        xs = [S.sb(tag + f"x{i}", [128, KD, T], F32) for i in range(2)]
        ys = [S.sb(tag + f"y{i}", [64, 16, T], BF16) for i in range(2)]
        P = C["P"]
        xin_v = x_in.h.rearrange("(k p) t -> p k t", p=128)
        xout_v = x_out.h.rearrange("(k p) t -> p k t", p=128)
        yv = YD.h.rearrange("h p t -> p h t")
        mv = MD.h.rearrange("h p t -> p h t")
        for it in range(ntok // T):
            x, y = xs[it % 2], ys[it % 2]
            ts = slice(it * T, (it + 1) * T)
            S.dma(x[:], x_in.ap(xin_v[:, :, ts], key=it), q="sp")
            S.dma(y.p("y")[:, 0:12, :], YD.ap(yv[:, :, ts], key=it), q="sp")
            S.dma(y.p("m")[:, 12:16, :], MD.ap(mv[:, :, ts], key=it), q="sp")
            for oc in range(KD):
                po = P[1 + oc % 2]
                for hc in range(16):
                    S.mm(po[:], wo[:, hc, oc * 128:(oc + 1) * 128], y.p("y" if hc < 12 else "m")[:, hc, :],
                         start=(hc == 0), stop=(hc == 15))
                S.op("dve", "tensor_tensor", out=x[:, oc, :], in0=x[:, oc, :], in1=po[:], op=ALU.add)
            S.dma(x_out.ap(xout_v[:, :, ts], key=it), x[:], q="pool")


NH_MLA = 12
SCALE_MLA = (64 + 32) ** -0.5
SCALE_MEM = 64 ** -0.5


def rope_tables_phase(S, C, pos, invf, tabs, ntok):
    CH = 1024 if ntok >= 1024 else ntok
    C1 = 6.28125
    C2 = 2 * math.pi - 6.28125
    with S.scope():
        iv = S.sb("rt_iv", [128, 1], F32)
        S.dma(iv[:], dview(invf, "(p o) -> p o", o=1), q="sp")
        pi_t = S.sb("rt_pi", [128, CH], I32)
        pf = S.sb("rt_pf", [128, CH], F32)
        ang = S.sb("rt_ang", [128, CH], F32)
        tmp = S.sb("rt_tmp", [128, CH], F32)
        ki = S.sb("rt_ki", [128, CH], I32)
        kf = S.sb("rt_kf", [128, CH], F32)
        r = S.sb("rt_r", [128, CH], F32)
        o = {k: S.sb("rt_o" + k, [128, CH], F32) for k in ("sink", "cosk", "sinq", "cosq")}
        for c in range(ntok // CH):
            cs = slice(c * CH, (c + 1) * CH)
            S.dma(pi_t[:], V(pos.t, None, pos.ap[cs].partition_broadcast(128)), q="sp")
            S.op("dve", "tensor_copy", out=pf[:], in_=pi_t[:])
            S.op("dve", "tensor_scalar", out=ang[:], in0=pf[:], scalar1=iv[:, 0:1], scalar2=None, op0=ALU.mult)
            S.op("dve", "tensor_scalar", out=tmp[:], in0=ang[:], scalar1=1.0 / (2 * math.pi), scalar2=None, op0=ALU.mult)
            S.op("dve", "tensor_copy", out=ki[:], in_=tmp[:])
            S.op("dve", "tensor_copy", out=kf[:], in_=ki[:])
            S.op("dve", "scalar_tensor_tensor", out=r[:], in0=kf[:], scalar=-C1, in1=ang[:], op0=ALU.mult, op1=ALU.add)
            S.op("dve", "scalar_tensor_tensor", out=r[:], in0=kf[:], scalar=-C2, in1=r[:], op0=ALU.mult, op1=ALU.add)
            S.op("dve", "tensor_scalar", out=tmp[:], in0=r[:], scalar1=math.pi, scalar2=-2 * math.pi, op0=ALU.is_gt, op1=ALU.mult)
            S.op("dve", "tensor_tensor", out=r[:], in0=r[:], in1=tmp[:], op=ALU.add)
            S.op("act", "activation", out=o["sink"][:], in_=r[:], func=AF.Sin)
            S.op("dve", "tensor_scalar", out=r[:], in0=r[:], scalar1=math.pi / 2, scalar2=None, op0=ALU.add)
            S.op("dve", "tensor_scalar", out=tmp[:], in0=r[:], scalar1=math.pi, scalar2=-2 * math.pi, op0=ALU.is_gt, op1=ALU.mult)
            S.op("dve", "tensor_tensor", out=r[:], in0=r[:], in1=tmp[:], op=ALU.add)
            S.op("act", "activation", out=o["cosk"][:], in_=r[:], func=AF.Sin)
            S.op("dve", "tensor_scalar", out=o["sinq"][:], in0=o["sink"][:], scalar1=SCALE_MLA, scalar2=None, op0=ALU.mult)
            S.op("dve", "tensor_scalar", out=o["cosq"][:], in0=o["cosk"][:], scalar1=SCALE_MLA, scalar2=None, op0=ALU.mult)
            for k in o:
                S.dma(tabs[k][:, cs], o[k][:], q="pool")


def rope_apply(S, t1, t2, cos, sin, o1, o2, tmpa, tmpb, np_, n=T):
    S.op("dve", "tensor_tensor", out=tmpa[0:np_, :n], in0=t1, in1=cos, op=ALU.mult)
    S.op("dve", "tensor_tensor", out=tmpb[0:np_, :n], in0=t2, in1=sin, op=ALU.mult)
    S.op("dve", "tensor_tensor", out=o1, in0=tmpa[0:np_, :n], in1=tmpb[0:np_, :n], op=ALU.subtract)
    S.op("dve", "tensor_tensor", out=tmpa[0:np_, :n], in0=t2, in1=cos, op=ALU.mult)
    S.op("dve", "tensor_tensor", out=tmpb[0:np_, :n], in0=t1, in1=sin, op=ALU.mult)
    S.op("dve", "tensor_tensor", out=o2, in0=tmpa[0:np_, :n], in1=tmpb[0:np_, :n], op=ALU.add)


def kv_phase(S, C, x_in, Wd, ntok, KN, KR, VD, tabs):
    P = C["P"]
    with S.scope():
        wdn = S.sb("kv_wdn", [128, KD, 288], BF16)
        for kc in range(KD):
            S.dma(wdn.p(kc)[:, kc, :], Wd["kv_w_down"][kc * 128:(kc + 1) * 128, :], q="pool")
        wup_n = S.sb("kv_wupn", [128, 2, 768], BF16)
        wup_v = S.sb("kv_wupv", [128, 2, 768], BF16)
        src = dview(Wd["kv_w_up"], "(k p) (h c) -> p k h c", p=128, c=128)
        for kc in range(2):
            S.dma(dview(wup_n.p(kc)[:, kc, :], "p (h c) -> p h c", c=64), src[:, kc, :, 0:64], q="pool")
            S.dma(dview(wup_v.p(kc)[:, kc, :], "p (h c) -> p h c", c=64), src[:, kc, :, 64:128], q="pool")
        kvg = S.sb("kv_g", [128, KD], F32)
        latg = S.sb("kv_latg", [128, 2], F32)
        load_vec(S, kvg[:], Wd["kv_norm_g"])
        load_vec(S, latg[:], Wd["kv_latent_g"])
        xs = [S.sb(f"kv_x{i}", [128, KD, T], F32) for i in range(2)]
        hk = S.sb("kv_hk", [128, KD, T], BF16)
        sq = S.sb("kv_sq", [128, T], F32)
        rstd = S.sb("kv_rstd", [128, T], F32)
        ckv = S.sb("kv_ckv", [128, 2, T], F32)
        ckvn = S.sb("kv_ckvn", [128, 2, T], BF16)
        cs_t = S.sb("kv_cos", [16, T], F32)
        sn_t = S.sb("kv_sin", [16, T], F32)
        tmpa = S.sb("kv_tmpa", [16, T], F32)
        tmpb = S.sb("kv_tmpb", [16, T], F32)
        kr = [S.sb(f"kv_kr{i}", [16, 2, T], BF16) for i in range(2)]
        kn = [S.sb(f"kv_kn{i}", [128, T], BF16) for i in range(2)]
        vt = [S.sb(f"kv_vt{i}", [128, 4, 12, 65], BF16) for i in range(2)]
        for i in range(2):
            S.op("pool", "memset", ap=vt[i][:], constant=1.0, extra_writes=[vt[i][:]])
        xin_v = x_in.h.rearrange("(k p) t -> p k t", p=128)
        vd_v = VD.h.rearrange("n p h c -> p n h c")
        for it in range(ntok // T):
            ts = slice(it * T, (it + 1) * T)
            x = xs[it % 2]
            S.dma(x[:], x_in.ap(xin_v[:, :, ts], key=it), q="sp")
            S.dma(cs_t[:], tabs["cosk"][0:16, ts], q="sp")
            S.dma(sn_t[:], tabs["sink"][0:16, ts], q="sp")
            rms_rstd(S, C, x, P[0], sq, rstd, KD, C["ones1024"])
            norm_apply(S, hk, x, kvg, rstd, KD)
            for c in range(2):
                pm = P[1 + c]
                for kc in range(KD):
                    S.mm(pm[:], wdn.p(kc)[:, kc, c * 128:(c + 1) * 128], hk.p(kc)[:, kc, :], start=(kc == 0), stop=(kc == KD - 1))
                S.op("act", "activation", out=ckv[:, c, :], in_=pm[:], func=AF.Copy)
            for kc in range(KD):
                S.mm(P[3][0:16, :], wdn.p(kc)[:, kc, 256:272], hk.p(kc)[:, kc, :], start=(kc == 0), stop=(kc == KD - 1))
            for kc in range(KD):
                S.mm(P[4][0:16, :], wdn.p(kc)[:, kc, 272:288], hk.p(kc)[:, kc, :], start=(kc == 0), stop=(kc == KD - 1))
            rms_rstd(S, C, ckv, P[0], sq, rstd, 2, C["ones256"])
            norm_apply(S, ckvn, ckv, latg, rstd, 2)
            k = kr[it % 2]
            rope_apply(S, P[3][0:16, :], P[4][0:16, :], cs_t[:], sn_t[:], k[:, 0, :], k[:, 1, :], tmpa, tmpb, 16)
            S.dma(KR[0:16, ts], k[:, 0, :], q="pool")
            S.dma(KR[16:32, ts], k[:, 1, :], q="pool")
            for a in range(6):
                pm = P[1 + a % 2]
                for kc in range(2):
                    S.mm(pm[:], wup_n.p(kc)[:, kc, a * 128:(a + 1) * 128], ckvn.p(kc)[:, kc, :], start=(kc == 0), stop=(kc == 1))
                kk = kn[a % 2]
                S.op("act", "activation", out=kk[:], in_=pm[:], func=AF.Copy)
                S.dma(KN[2 * a, :, ts], kk[0:64, :], q="pool")
                S.dma(KN[2 * a + 1, :, ts], kk[64:128, :], q="pool")
            v = vt[it % 2]
            for st in range(4):
                for half in range(2):
                    pv = P[5 + half]
                    for kc in range(2):
                        S.mm(pv[:, 0:384], ckvn.p(kc)[:, kc, st * 128:(st + 1) * 128], wup_v.p(kc)[:, kc, half * 384:(half + 1) * 384],
                             start=(kc == 0), stop=(kc == 1))
                    S.op("act" if half == 0 else "dve", "activation" if half == 0 else "tensor_copy",
                         out=v[:, st, half * 6:(half + 1) * 6, 0:64], in_=dview(pv[:, 0:384], "p (h c) -> p h c", c=64),
                         **({"func": AF.Copy} if half == 0 else {}))
            S.dma(VD.ap(vd_v[:, it * 4:(it + 1) * 4, :, :], key=it), v[:], q="pool")


def mem_prep(S, C, Wd, memn):
    P = C["P"]
    with S.scope():
        mx = S.sb("mp_x", [128, KD, NMEM], F32)
        sq = S.sb("mp_sq", [128, NMEM], F32)
        rstd = S.sb("mp_rstd", [128, NMEM], F32)
        g = S.sb("mp_g", [128, KD], F32)
        S.dma(mx[:], dview(Wd["memT"], "(k p) t -> p k t", p=128), q="sp")
        load_vec(S, g[:], Wd["mem_norm_g"])
        rms_rstd(S, C, mx, P[0], sq, rstd, KD, C["ones1024"], n=NMEM)
        norm_apply(S, memn, mx, g, rstd, KD, n=NMEM)


def mem_kv(S, C, memn, w_kv, MK, MV, tag):
    P = C["P"]
    with S.scope():
        wkv = S.sb(tag + "wkv", [128, KD, 512], BF16)
        for kc in range(KD):
            S.dma(wkv.p(kc)[:, kc, :], w_kv[kc * 128:(kc + 1) * 128, :], q="pool")
        S.op("pool", "memset", ap=MV[:], constant=1.0, extra_writes=[MV[:]])
        for c in range(2):
            pm = P[1 + c]
            for kc in range(KD):
                S.mm(pm[:, 0:NMEM], wkv.p(kc)[:, kc, c * 128:(c + 1) * 128], memn[:, kc, :], start=(kc == 0), stop=(kc == KD - 1))
            S.op("act", "activation", out=MK[:, c, :], in_=pm[:, 0:NMEM], func=AF.Copy)
        for mt in range(2):
            pm = P[3 + mt]
            for kc in range(KD):
                S.mm(pm[:, 0:256], memn[:, kc, mt * 128:(mt + 1) * 128], wkv.p(kc)[:, kc, 256:512], start=(kc == 0), stop=(kc == KD - 1))
            S.op("dve", "tensor_copy", out=MV[:, mt, :, 0:64], in_=dview(pm[:, 0:256], "p (h c) -> p h c", c=64))


def mem_attn(S, C, qm, MK, MV, MD, ts, it, bufs):
    P = C["P"]
    pT, osb, rec, mo = bufs
    for hm in range(4):
        c, pb = hm // 2, (hm % 2) * 64
        for mt in range(2):
            ps = P[1 + mt]
            S.mm(ps[:], MK[pb:pb + 64, c, mt * 128:(mt + 1) * 128], qm[pb:pb + 64, c, :])
            p = pT[mt]
            S.op("act", "activation", out=p[:], in_=ps[:], func=AF.Exp, scale=SCALE_MEM)
            S.mm(P[5][0:65, :], MV[:, mt, hm, :], p[:], start=(mt == 0), stop=(mt == 1))
        m = mo[hm % 2]
        attn_finish(S, C, P[5], osb, P[6], rec, m[:])
        S.dma(MD.ap(MD.h[hm, :, ts], key=it), m[:], q="pool")


def mla_q_phase(S, C, x_in, Wd, j, ntok, QD, MD, memn, tabs):
    P = C["P"]
    tag = f"q{j}_"
    with S.scope():
        MK = S.sb(tag + "MK", [128, 2, NMEM], BF16)
        MV = S.sb(tag + "MV", [128, 2, 4, 65], BF16)
        mem_kv(S, C, memn, Wd["b_mem_kv"][j], MK, MV, tag)
        win = S.sb(tag + "win", [128, KD, 768], BF16)
        for kc in range(KD):
            S.dma(win.p(kc)[:, kc, :], Wd["b_w_in"][j, kc * 128:(kc + 1) * 128, :], q="pool")
        qn_w = S.sb(tag + "qn_w", [128, 4, 768], BF16)
        qr1_w = S.sb(tag + "qr1_w", [128, 4, 192], BF16)
        qr2_w = S.sb(tag + "qr2_w", [128, 4, 192], BF16)
        src = dview(Wd["b_q_up"][j], "(k p) (h c) -> p k h c", p=128, c=96)
        for kc in range(4):
            S.dma(dview(qn_w.p(kc)[:, kc, :], "p (h c) -> p h c", c=64), src[:, kc, :, 0:64], q="pool")
            S.dma(dview(qr1_w.p(kc)[:, kc, :], "p (h c) -> p h c", c=16), src[:, kc, :, 64:80], q="pool")
            S.dma(dview(qr2_w.p(kc)[:, kc, :], "p (h c) -> p h c", c=16), src[:, kc, :, 80:96], q="pool")
        g1 = S.sb(tag + "g1", [128, KD], F32)
        qg = S.sb(tag + "qg", [128, 4], F32)
        load_vec(S, g1[:], Wd["b_norm1_g"][j])
        load_vec(S, qg[:], Wd["b_q_norm_g"][j])
        xs = [S.sb(tag + f"x{i}", [128, KD, T], F32) for i in range(2)]
        h = S.sb(tag + "h", [128, KD, T], BF16)
        sq = S.sb(tag + "sq", [128, T], F32)
        rstd = S.sb(tag + "rstd", [128, T], F32)
        cq = S.sb(tag + "cq", [128, 4, T], F32)
        cqn = S.sb(tag + "cqn", [128, 4, T], BF16)
        qm = S.sb(tag + "qm", [128, 2, T], BF16)
        cs_t = S.sb(tag + "cos", [128, T], F32)
        sn_t = S.sb(tag + "sin", [128, T], F32)
        tmpa = S.sb(tag + "tmpa", [128, T], F32)
        tmpb = S.sb(tag + "tmpb", [128, T], F32)
        qn = [S.sb(tag + f"qn{i}", [128, T], BF16) for i in range(2)]
        qr = [S.sb(tag + f"qr{i}", [128, 2, T], BF16) for i in range(2)]
        pT = [S.sb(tag + f"pT{i}", [128, T], BF16) for i in range(2)]
        osb = S.sb(tag + "osb", [65, T], F32)
        rec = S.sb(tag + "rec", [64, T], F32)
        mo = [S.sb(tag + f"mo{i}", [64, T], BF16) for i in range(2)]
        xin_v = x_in.h.rearrange("(k p) t -> p k t", p=128)
        for it in range(ntok // T):
            ts = slice(it * T, (it + 1) * T)
            x = xs[it % 2]
            S.dma(x[:], x_in.ap(xin_v[:, :, ts], key=it), q="sp")
            S.dma(cs_t[:], tabs["cosq"][:, ts], q="sp")
            S.dma(sn_t[:], tabs["sinq"][:, ts], q="sp")
            rms_rstd(S, C, x, P[0], sq, rstd, KD, C["ones1024"])
            norm_apply(S, h, x, g1, rstd, KD)
            for c in range(6):
                pm = P[1 + c % 2]
                for kc in range(KD):
                    S.mm(pm[:], win.p(kc)[:, kc, c * 128:(c + 1) * 128], h.p(kc)[:, kc, :], start=(kc == 0), stop=(kc == KD - 1))
                if c < 4:
                    S.op("act", "activation", out=cq[:, c, :], in_=pm[:], func=AF.Copy)
                else:
                    S.op("act", "activation", out=qm[:, c - 4, :], in_=pm[:], func=AF.Copy)
            rms_rstd(S, C, cq, P[0], sq, rstd, 4, C["ones512"])
            norm_apply(S, cqn, cq, qg, rstd, 4)
            for a in range(6):
                pm = P[1 + a % 2]
                for kc in range(4):
                    S.mm(pm[:], qn_w.p(kc)[:, kc, a * 128:(a + 1) * 128], cqn.p(kc)[:, kc, :], start=(kc == 0), stop=(kc == 3))
                q = qn[a % 2]
                S.op("act", "activation", out=q[:], in_=pm[:], func=AF.Copy, scale=SCALE_MLA)
                S.dma(QD.ap(QD.h[2 * a, 0:64, ts], key=it), q[0:64, :], q="pool")
                S.dma(QD.ap(QD.h[2 * a + 1, 0:64, ts], key=it), q[64:128, :], q="pool")
            for grp, (h0, nh) in enumerate(((0, 8), (8, 4))):
                np_ = nh * 16
                for kc in range(4):
                    S.mm(P[3][0:np_, :], qr1_w.p(kc)[:, kc, h0 * 16:(h0 + nh) * 16], cqn.p(kc)[:, kc, :], start=(kc == 0), stop=(kc == 3))
                for kc in range(4):
                    S.mm(P[4][0:np_, :], qr2_w.p(kc)[:, kc, h0 * 16:(h0 + nh) * 16], cqn.p(kc)[:, kc, :], start=(kc == 0), stop=(kc == 3))
                r = qr[grp]
                rope_apply(S, P[3][0:np_, :], P[4][0:np_, :], cs_t[0:np_, :], sn_t[0:np_, :], r[0:np_, 0, :], r[0:np_, 1, :], tmpa, tmpb, np_)
                for hh in range(nh):
                    S.dma(QD.ap(QD.h[h0 + hh, 64:80, ts], key=it), r[hh * 16:(hh + 1) * 16, 0, :], q="pool")
                    S.dma(QD.ap(QD.h[h0 + hh, 80:96, ts], key=it), r[hh * 16:(hh + 1) * 16, 1, :], q="pool")
            mem_attn(S, C, qm, MK, MV, MD, ts, it, (pT, osb, rec, mo))


def attn_phase(S, C, QD, KN, KR, VD, OD, ntok, heads=range(12)):
    P = C["P"]
    NKT = ntok // 128
    with S.scope():
        Ks = [S.sb(f"at_K{i}", [96, ntok], BF16) for i in range(2)]
        Qs = [S.sb(f"at_Q{i}", [96, ntok], BF16) for i in range(2)]
        Vs = [S.sb(f"at_V{i}", [128, NKT, 65], BF16) for i in range(2)]
        pT = [S.sb(f"at_pT{i}", [128, T], BF16) for i in range(3)]
        osb = S.sb("at_osb", [65, T], F32)
        rec = S.sb("at_rec", [64, T], F32)
        oo = [S.sb(f"at_oo{i}", [64, T], BF16) for i in range(2)]
        vd_v = VD.h.rearrange("n p h c -> p n h c")
        cnt = 0
        for ih, hd in enumerate(heads):
            K, Q, Vv = Ks[ih % 2], Qs[ih % 2], Vs[ih % 2]
            S.dma(K.p("n")[0:64, :], KN[hd, :, :], q="sp")
            S.dma(K.p("r")[64:96, :], KR[:, :], q="sp")
            S.dma(Q[:], QD[hd, :, :], q="sp")
            S.dma(Vv[:], V(VD, None, vd_v[:, :, hd, :]), q="sp")
            for qi in range(ntok // T):
                po = P[3 + qi % 2]
                nk = 4 * qi + 4
                for kj in range(nk):
                    d = kj - 4 * qi
                    c0 = 128 * d if d > 0 else 0
                    ps = P[(1, 2, 7)[cnt % 3]]
                    p = pT[cnt % 3]
                    cnt += 1
                    S.mm(ps[:, c0:T], K[:, kj * 128:(kj + 1) * 128], Q[:, qi * T + c0:(qi + 1) * T])
                    S.op("act", "activation", out=p[:, c0:T], in_=ps[:, c0:T], func=AF.Exp)
                    if d >= 0:
                        S.op("dve", "tensor_tensor", out=p[:, c0:c0 + 128], in0=p[:, c0:c0 + 128], in1=C["tri"][:], op=ALU.mult)
                    S.mm(po[0:65, c0:T], Vv[:, kj, :], p[:, c0:T], start=(kj == 0), stop=(kj == nk - 1))
                o = oo[qi % 2]
                attn_finish(S, C, po, osb, P[6], rec, o[:])
                S.dma(OD.ap(OD.h[hd, :, qi * T:(qi + 1) * T], key=(hd, qi)), o[:], q="pool")


EXPM05 = math.exp(-0.5)
LNX_EPS = 64e-5
CH = 64
NCH = T // CH


def rwkv_layer(S, C, x_in, x_mid, Wd, i, ntok, memn, st):
    tag = f"r{i}_"
    SD = {k: S.dram(tag + k, [768, ntok], F32) for k in ("r", "sg", "k2", "v", "kk", "b")}
    GL = S.dram(tag + "gl", [128, ntok], BF16)
    MD = S.dram(tag + "MD", [4, 64, ntok], BF16)
    YD = S.dram(tag + "YD", [12, 64, ntok], BF16)
    if i == 0:
        st["VF"] = SD["v"]
    rwkv_proj_phase(S, C, x_in, Wd, i, ntok, memn, SD, GL, MD, st, tag)
    rwkv_scan_phase(S, C, Wd, i, ntok, SD, GL, YD, tag)
    outproj_phase(S, C, x_in, x_mid, YD, MD, Wd["a_w_out"][i], ntok, tag=tag + "o")


def rwkv_proj_phase(S, C, x_in, Wd, i, ntok, memn, SD, GL, MD, st, tag):
    P = C["P"]
    with S.scope():
        MK = S.sb(tag + "MK", [128, 2, NMEM], BF16)
        MV = S.sb(tag + "MV", [128, 2, 4, 65], BF16)
        mem_kv(S, C, memn, Wd["a_mem_kv"][i], MK, MV, tag)
        W1 = S.sb(tag + "W1", [128, KD, 2560], BF16)
        W2 = S.sb(tag + "W2", [128, KD, 2560], BF16)
        wq = S.sb(tag + "wq", [128, KD, 256], BF16)
        for kc in range(KD):
            S.dma(wq.p(kc)[:, kc, :], Wd["a_w_in"][i, kc * 128:(kc + 1) * 128, 2560:2816], q="pool")
        with S.scope():
            mu_bc = S.sb(tag + "mu_bc", [128, 2560], F32)
            omm_bc = S.sb(tag + "omm_bc", [128, 2560], F32)
            wf = [S.sb(tag + f"wf{k}", [128, 2560], F32) for k in range(2)]
            S.dma(mu_bc[:], V(Wd["a_shift_mu"].t, None, Wd["a_shift_mu"].ap[i].partition_broadcast(128)), q="sp")
            S.op("dve", "tensor_scalar", out=omm_bc[:], in0=mu_bc[:], scalar1=-1.0, scalar2=1.0, op0=ALU.mult, op1=ALU.add)
            for kc in range(KD):
                w = wf[kc % 2]
                S.dma(w[:], Wd["a_w_in"][i, kc * 128:(kc + 1) * 128, 0:2560], q="sp")
                S.op("dve", "tensor_tensor", out=W1.p(kc)[:, kc, :], in0=w[:], in1=omm_bc[:], op=ALU.mult)
                S.op("dve", "tensor_tensor", out=W2.p(kc)[:, kc, :], in0=w[:], in1=mu_bc[:], op=ALU.mult)
        dup = S.sb(tag + "dup", [128, 768], BF16)
        S.dma(dup.p("d")[0:64, :], Wd["a_decay_up"][i], q="pool")
        S.dma(dup.p("a")[64:128, :], Wd["a_aaa_up"][i], q="pool")
        vecs = S.sb(tag + "vecs", [128, 8, 6], F32)
        for vi, nm in enumerate(("a_decay_bias", "a_aaa_bias", "a_k_k", "a_k_a")):
            S.dma(vecs.p(vi)[:, vi, :], dview(Wd[nm][i], "(k p) -> p k", p=128), q="sp", allow_slow_non_contiguous=True)
        S.op("dve", "tensor_scalar", out=vecs.p(4)[:, 4, :], in0=vecs.p(3)[:, 3, :], scalar1=-1.0, scalar2=1.0, op0=ALU.mult, op1=ALU.add)
        g1 = S.sb(tag + "g1", [128, KD], F32)
        load_vec(S, g1[:], Wd["a_norm1_g"][i])
        if i > 0:
            S.dma(vecs.p(5)[:, 5, :], dview(Wd["vres_bias"][i - 1], "(k p) -> p k", p=128), q="sp", allow_slow_non_contiguous=True)
            vmu = S.sb(tag + "vmu", [128, KD, 2], F32)
            S.dma(vmu.p(0)[:, :, 0], dview(Wd["vres_mu"][i - 1], "(k p) -> p k", p=128), q="sp", allow_slow_non_contiguous=True)
            S.op("dve", "tensor_scalar", out=vmu.p(1)[:, :, 1], in0=vmu.p(0)[:, :, 0], scalar1=-1.0, scalar2=1.0, op0=ALU.mult, op1=ALU.add)
            vdf = S.sb(tag + "vdf", [128, KD, 32], F32)
            S.dma(vdf[:], dview(Wd["vres_down"][i - 1], "(k p) n -> p k n", p=128), q="sp")
            vd1 = S.sb(tag + "vd1", [128, KD, 32], BF16)
            vd2 = S.sb(tag + "vd2", [128, KD, 32], BF16)
            for kc in range(KD):
                S.op("dve", "tensor_scalar", out=vd1.p(kc)[:, kc, :], in0=vdf[:, kc, :], scalar1=vmu.p(1)[:, kc, 1:2], scalar2=None, op0=ALU.mult)
                S.op("dve", "tensor_scalar", out=vd2.p(kc)[:, kc, :], in0=vdf[:, kc, :], scalar1=vmu.p(0)[:, kc, 0:1], scalar2=None, op0=ALU.mult)
            vup = S.sb(tag + "vup", [32, 768], BF16)
            S.dma(vup[:], Wd["vres_up"][i - 1], q="pool")
            vlo = S.sb(tag + "vlo", [32, T], BF16)
        xs = [S.sb(tag + f"x{k}", [128, KD, T], F32) for k in range(2)]
        h = S.sb(tag + "h", [128, KD, T + 1], BF16)
        S.op("pool", "memset", ap=h[:], constant=0.0, extra_writes=[h[:]])
        sq = S.sb(tag + "sq", [128, T], F32)
        rstd = S.sb(tag + "rstd", [128, T], F32)
        lo = S.sb(tag + "lo", [128, T], BF16)
        gl = S.sb(tag + "gl", [128, T], BF16)
        qm = S.sb(tag + "qm", [128, 2, T], BF16)
        pT = [S.sb(tag + f"pT{k}", [128, T], BF16) for k in range(2)]
        osb = S.sb(tag + "osb", [65, T], F32)
        rec = S.sb(tag + "rec", [64, T], F32)
        mo = [S.sb(tag + f"mo{k}", [64, T], BF16) for k in range(2)]
        names = ("r", "k", "v", "sg", "lr", "kks", "kk", "k2", "b", "t1", "vf")
        tb = {nm: [S.sb(tag + f"t_{nm}{k}", [128, T], F32) for k in range(2)] for nm in names}
        xin_v = x_in.h.rearrange("(k p) t -> p k t", p=128)
        bones = C["bones"]

        def proj_chunk(pm, c):
            for kc in range(KD):
                S.mm(pm[:], W1.p(kc)[:, kc, c * 128:(c + 1) * 128], h.p(kc)[:, kc, 1:T + 1], start=(kc == 0), stop=False)
                S.mm(pm[:], W2.p(kc)[:, kc, c * 128:(c + 1) * 128], h.p(kc)[:, kc, 0:T], start=False, stop=(kc == KD - 1))

        for it in range(ntok // T):
            ts = slice(it * T, (it + 1) * T)
            x = xs[it % 2]
            S.dma(x[:], x_in.ap(xin_v[:, :, ts], key=it), q="sp")
            if it > 0:
                S.op("dve", "tensor_copy", out=h[:, :, 0], in_=h[:, :, T])
            rms_rstd(S, C, x, P[0], sq, rstd, KD, C["ones1024"])
            for kc in range(KD):
                S.op("dve", "scalar_tensor_tensor", out=h.p(kc)[:, kc, 1:T + 1], in0=x[:, kc, :], scalar=g1[:, kc:kc + 1],
                     in1=rstd[:], op0=ALU.mult, op1=ALU.mult)
            proj_chunk(P[1], 18)
            S.op("act", "activation", out=lo.p("d")[0:64, :], in_=P[1][0:64, :], func=AF.Tanh)
            S.op("act", "activation", out=lo.p("a")[64:128, :], in_=P[1][64:128, :], func=AF.Copy)
            proj_chunk(P[2], 19)
            S.op("act", "activation", out=gl[:], in_=P[2][:], func=AF.Sigmoid)
            S.dma(GL[:, ts], gl[:], q="pool")
            for c in range(2):
                pm = P[1 + c]
                for kc in range(KD):
                    S.mm(pm[:], wq.p(kc)[:, kc, c * 128:(c + 1) * 128], h.p(kc)[:, kc, 1:T + 1], start=(kc == 0), stop=(kc == KD - 1))
                S.op("act", "activation", out=qm[:, c, :], in_=pm[:], func=AF.Copy)
            mem_attn(S, C, qm, MK, MV, MD, ts, it, (pT, osb, rec, mo))
            if i > 0:
                for kc in range(KD):
                    S.mm(P[7][0:32, :], vd1.p(kc)[:, kc, :], h.p(kc)[:, kc, 1:T + 1], start=(kc == 0), stop=False)
                    S.mm(P[7][0:32, :], vd2.p(kc)[:, kc, :], h.p(kc)[:, kc, 0:T], start=False, stop=(kc == KD - 1))
                S.op("act", "activation", out=vlo[:], in_=P[7][0:32, :], func=AF.Copy)
            for c in range(6):
                b_ = c % 2
                t = {nm: tb[nm][b_] for nm in names}
                for j, nm in enumerate(("r", "k", "v")):
                    pm = P[1 + j]
                    proj_chunk(pm, j * 6 + c)
                    S.op("act", "activation", out=t[nm][:], in_=pm[:], func=AF.Copy)
                S.mm(P[4][:], dup.p("d")[0:64, c * 128:(c + 1) * 128], lo.p("d")[0:64, :])
                S.op("act", "activation", out=t["sg"][:], in_=P[4][:], func=AF.Sigmoid, bias=vecs.p(0)[:, 0, c:c + 1], scale=1.0)
                S.mm(P[5][:], dup.p("a")[64:128, c * 128:(c + 1) * 128], lo.p("a")[64:128, :])
                S.op("act", "activation", out=t["lr"][:], in_=P[5][:], func=AF.Sigmoid, bias=vecs.p(1)[:, 1, c:c + 1], scale=1.0)
                if i > 0:
                    S.mm(P[6][:], vup[:, c * 128:(c + 1) * 128], vlo[:])
                    S.op("act", "activation", out=t["t1"][:], in_=P[6][:], func=AF.Sigmoid, bias=vecs.p(5)[:, 5, c:c + 1], scale=1.0)
                    S.dma(t["vf"][:], st["VF"][c * 128:(c + 1) * 128, ts], q="sp")
                    S.op("dve", "tensor_tensor", out=t["vf"][:], in0=t["vf"][:], in1=t["v"][:], op=ALU.subtract)
                    S.op("dve", "tensor_tensor", out=t["vf"][:], in0=t["vf"][:], in1=t["t1"][:], op=ALU.mult)
                    S.op("dve", "tensor_tensor", out=t["v"][:], in0=t["v"][:], in1=t["vf"][:], op=ALU.add)
                S.op("dve", "tensor_scalar", out=t["kks"][:], in0=t["k"][:], scalar1=vecs.p(2)[:, 2, c:c + 1], scalar2=None, op0=ALU.mult)
                S.op("act", "activation", out=t["t1"][:], in_=t["kks"][:], func=AF.Square)
                S.mm(P[6][:], bones[:], t["t1"][:])
                S.op("act", "activation", out=t["t1"][:], in_=P[6][:], func=AF.Sqrt)
                S.op("dve", "tensor_scalar", out=t["t1"][:], in0=t["t1"][:], scalar1=1e-12, scalar2=None, op0=ALU.max)
                S.op("dve", "reciprocal", out=t["t1"][:], in_=t["t1"][:])
                S.op("dve", "tensor_tensor", out=t["kk"][:], in0=t["kks"][:], in1=t["t1"][:], op=ALU.mult)
                S.op("dve", "tensor_scalar", out=t["t1"][:], in0=t["lr"][:], scalar1=vecs.p(3)[:, 3, c:c + 1], scalar2=vecs.p(4)[:, 4, c:c + 1],
                     op0=ALU.mult, op1=ALU.add)
                S.op("dve", "tensor_tensor", out=t["k2"][:], in0=t["k"][:], in1=t["t1"][:], op=ALU.mult)
                S.op("dve", "tensor_tensor", out=t["b"][:], in0=t["kk"][:], in1=t["lr"][:], op=ALU.mult)
                for nm in ("r", "sg", "k2", "v", "kk", "b"):
                    S.dma(SD[nm][c * 128:(c + 1) * 128, ts], t[nm][:], q="pool")


def c3(v):
    return dview(v, "p (c j) -> p c j", j=CH)


def bcast_mid(v, n):
    p, j = v.ap.shape
    return V(v.t, v.k, v.ap.unsqueeze(1).broadcast_to([p, n, j]))


def bcast_last(v, j):
    p, n = v.ap.shape
    return V(v.t, v.k, v.ap.unsqueeze(2).broadcast_to([p, n, j]))


def to_bd(S, bd, src3, engs=("act", "dve")):
    for hh in range(2):
        ps = slice(hh * 64, hh * 64 + 64)
        if engs[hh] == "act":
            S.op("act", "activation", out=bd[ps, :, hh * 64:hh * 64 + 64], in_=src3[ps, :, :], func=AF.Copy)
        else:
            S.op("dve", "tensor_copy", out=bd[ps, :, hh * 64:hh * 64 + 64], in_=src3[ps, :, :])


def rwkv_scan_phase(S, C, Wd, i, ntok, SD, GL, YD, tag):
    P = C["P"]
    tag = tag + "s"
    ident = C["ident"]
    with S.scope():
        cm = S.sb(tag + "cm", [128, 4, CH], F32)
        S.dma(cm[:], Wd["cmask"], q="sp")
        mXs, mYs, mYi, Ist = (cm[:, k, :] for k in range(4))
        ones_col = S.sb(tag + "ones_col", [128, 1], F32)
        S.op("pool", "memset", ap=ones_col[:], constant=1.0, extra_writes=[ones_col[:]])
        epsl = S.sb(tag + "epsl", [128, 1], F32)
        S.op("pool", "memset", ap=epsl[:], constant=LNX_EPS, extra_writes=[epsl[:]])
        rkv = S.sb(tag + "rkv", [128, 6], F32)
        S.dma(rkv[:], V(Wd["a_r_k"].t, None, Wd["a_r_k"].ap[i].rearrange("h k -> (h k)").rearrange("(c p) -> p c", p=128)),
              q="sp", allow_slow_non_contiguous=True)
        gup = S.sb(tag + "gup", [128, 768], BF16)
        S.dma(gup[:], Wd["a_gate_up"][i], q="pool")
        lng = S.sb(tag + "lng", [128, 6, 64], F32)
        lnb = S.sb(tag + "lnb", [128, 6, 64], F32)
        for nm, dst in (("a_lnx_g", lng), ("a_lnx_b", lnb)):
            src = Wd[nm].ap[i].rearrange("(c h j) -> c h j", h=2, j=64)
            for hh in range(2):
                for c in range(6):
                    S.dma(dst.p((c, hh))[hh * 64:hh * 64 + 64, c, :],
                          V(Wd[nm].t, None, src[c, hh].partition_broadcast(64)), q="sp")
        names_ld = ("r", "sg", "k2", "v", "kk", "b")
        ld = {nm: [S.sb(tag + f"l_{nm}{k}", [128, T], F32) for k in range(2)] for nm in names_ld}
        gl = [S.sb(tag + f"gl{k}", [128, T], BF16) for k in range(2)]
        fnames = ("lw", "csA", "csB", "Ein", "Eex", "Eneg", "Eend", "tmp", "rt", "at", "bt")
        fb = [{nm: S.sb(tag + f"f_{nm}{k}", [128, T], F32) for nm in fnames} for k in range(2)]
        bdn = ("at", "bt", "kt", "bh", "kh", "v", "u", "atT", "bhT", "khT", "X", "Y", "AKT", "RBT", "RKT", "AhT", "TT")
        bd = {nm: S.sb(tag + "bd_" + nm, [128, NCH, 128], F32) for nm in bdn}
        bd_rt = [S.sb(tag + f"bd_rt{k}", [128, NCH, 128], F32) for k in range(2)]
        for tl in list(bd.values()) + bd_rt:
            S.op("pool", "memset", ap=tl[:], constant=0.0, extra_writes=[tl[:]])
        s3 = {nm: S.sb(tag + "s3_" + nm, [128, NCH, CH], F32) for nm in
              ("Vtm", "Xs", "Ys", "ATs", "AVs", "U8", "y8", "yc", "ysq", "yt")}
        st2 = {nm: S.sb(tag + "st_" + nm, [128, NCH], F32) for nm in ("mean", "var", "rk")}
        Sst = [S.sb(tag + f"S{k}", [128, CH], F32) for k in range(2)]
        g_sb = S.sb(tag + "g_sb", [64, 2, T], F32)
        ym = [S.sb(tag + f"ym{k}", [64, T], BF16) for k in range(2)]
        NT = ntok // T
        tiles = [(c, it) for c in range(6) for it in range(NT)]

        def prep(gi):
            c, it = tiles[gi]
            k = gi % 2
            ts = slice(it * T, (it + 1) * T)
            L = {nm: ld[nm][k] for nm in names_ld}
            f = fb[k]
            for nm in names_ld:
                S.dma(L[nm][:], SD[nm][c * 128:(c + 1) * 128, ts], q="sp")
            S.dma(gl[k][:], GL[:, ts], q="sp")
            yield
            S.op("dve", "tensor_scalar", out=f["lw"][:], in0=L["sg"][:], scalar1=-EXPM05, scalar2=None, op0=ALU.mult)
            src, dst = f["lw"], f["csA"]
            for s_ in (1, 2, 4, 8, 16, 32):
                S.op("act", "activation", out=c3(dst[:])[:, :, 0:s_], in_=c3(src[:])[:, :, 0:s_], func=AF.Copy)
                S.op("dve", "tensor_tensor", out=c3(dst[:])[:, :, s_:CH], in0=c3(src[:])[:, :, s_:CH], in1=c3(src[:])[:, :, 0:CH - s_], op=ALU.add)
                src = dst
                dst = f["csB"] if dst is f["csA"] else f["csA"]
                yield
            cs = src
            S.op("act", "activation", out=f["Ein"][:], in_=cs[:], func=AF.Exp)
            S.op("act", "activation", out=f["Eneg"][:], in_=cs[:], func=AF.Exp, scale=-1.0)
            S.op("dve", "tensor_tensor", out=f["tmp"][:], in0=cs[:], in1=f["lw"][:], op=ALU.subtract)
            S.op("act", "activation", out=f["Eex"][:], in_=f["tmp"][:], func=AF.Exp)
            yield
            S.op("dve", "tensor_tensor", out=c3(f["tmp"][:]), in0=c3(cs[:]), in1=bcast_last(c3(cs[:])[:, :, CH - 1], CH), op=ALU.subtract)
            S.op("act", "activation", out=f["Eend"][:], in_=f["tmp"][:], func=AF.Exp, scale=-1.0)
            S.op("dve", "tensor_tensor", out=f["rt"][:], in0=L["r"][:], in1=f["Ein"][:], op=ALU.mult)
            S.op("dve", "scalar_tensor_tensor", out=f["at"][:], in0=L["kk"][:], scalar=-1.0, in1=f["Eex"][:], op0=ALU.mult, op1=ALU.mult)
            S.op("dve", "tensor_tensor", out=f["bt"][:], in0=L["b"][:], in1=f["Eneg"][:], op=ALU.mult)
            yield
            to_bd(S, bd_rt[k], c3(f["rt"][:]))
            to_bd(S, bd["at"], c3(f["at"][:]))
            yield
            to_bd(S, bd["bt"], c3(f["bt"][:]))
            to_bd(S, bd["v"], c3(L["v"][:]))
            yield
            for hh in range(2):
                ps = slice(hh * 64, hh * 64 + 64)
                cs_ = slice(hh * 64, hh * 64 + 64)
                S.op("dve", "tensor_tensor", out=bd["kt"][ps, :, cs_], in0=c3(L["k2"][:])[ps], in1=c3(f["Eneg"][:])[ps], op=ALU.mult)
                S.op("dve", "tensor_tensor", out=bd["bh"][ps, :, cs_], in0=c3(L["b"][:])[ps], in1=c3(f["Eend"][:])[ps], op=ALU.mult)
                yield
                S.op("dve", "tensor_tensor", out=bd["kh"][ps, :, cs_], in0=c3(L["k2"][:])[ps], in1=c3(f["Eend"][:])[ps], op=ALU.mult)
                S.op("dve", "scalar_tensor_tensor", out=bd["u"][ps, :, cs_], in0=c3(L["r"][:])[ps], scalar=rkv[ps, c:c + 1],
                     in1=c3(L["k2"][:])[ps], op0=ALU.mult, op1=ALU.mult)
                yield

        def drain(g):
            if g is not None:
                for _ in g:
                    pass

        def step(g):
            if g is not None:
                try:
                    next(g)
                except StopIteration:
                    pass

        sidx = 0
        drain(prep(0))
        for gi, (c, it) in enumerate(tiles):
            k = gi % 2
            ts = slice(it * T, (it + 1) * T)
            f = fb[k]
            glt = gl[k]
            rtbd = bd_rt[k]
            if it == 0:
                S.op("dve", "memset", ap=Sst[sidx][:], constant=0.0, extra_writes=[Sst[sidx][:]])
            Wc = c3(f["Ein"][:])[:, :, CH - 1]
            for nm_s, nm_d in (("at", "atT"), ("bh", "bhT"), ("kh", "khT")):
                for half in range(2):
                    pt = P[half]
                    for q4 in range(4):
                        ch = half * 4 + q4
                        S.transpose(pt[:, q4 * 128:(q4 + 1) * 128], bd[nm_s][:, ch, :], ident[:])
                    S.op("act" if half == 0 else "dve", "activation" if half == 0 else "tensor_copy",
                         out=bd[nm_d][:, half * 4:half * 4 + 4, :], in_=dview(pt[:], "p (c j) -> p c j", j=128),
                         **({"func": AF.Copy} if half == 0 else {}))
            for half in range(2):
                pt = P[half]
                for q4 in range(4):
                    ch = half * 4 + q4
                    S.transpose(pt[:, q4 * 128:(q4 + 1) * 128], bd["v"][:, ch, :], ident[:])
                p3 = dview(pt[:], "p (c j) -> p c j", j=128)
                S.op("act", "activation", out=s3["Vtm"][:, half * 4:half * 4 + 4, :], in_=p3[:, :, 0:64], func=AF.Copy)
                S.op("dve", "tensor_tensor", out=s3["Vtm"][:, half * 4:half * 4 + 4, :], in0=s3["Vtm"][:, half * 4:half * 4 + 4, :],
                     in1=p3[:, :, 64:128], op=ALU.add)

            def amat(pt, lhs_bd, rhs_f):
                for ch in range(NCH):
                    S.mm(pt[:, ch * CH:(ch + 1) * CH], bd[lhs_bd][:, ch, :], f[rhs_f][:, ch * CH:(ch + 1) * CH])
                return c3(pt[:])
            px = amat(P[0], "at", "bt")
            py = amat(P[1], "bt", "at")
            pk = amat(P[2], "kt", "at")
            S.op("dve", "tensor_tensor", out=s3["Xs"][:], in0=px, in1=bcast_mid(mXs, NCH), op=ALU.mult)
            to_bd(S, bd["X"], s3["Xs"][:])
            S.op("dve", "tensor_tensor", out=s3["Ys"][:], in0=py, in1=bcast_mid(mYs, NCH), op=ALU.mult)
            to_bd(S, bd["Y"], s3["Ys"][:])
            for ch in range(NCH):
                S.mm(P[0][:, ch * CH:(ch + 1) * CH], bd["Y"][:, ch, :], s3["Xs"][:, ch, :])
                S.mm(P[1][:, ch * CH:(ch + 1) * CH], bd["X"][:, ch, :], s3["Ys"][:, ch, :])
            S.op("dve", "tensor_tensor", out=s3["ATs"][:], in0=s3["Ys"][:], in1=bcast_mid(Ist, NCH), op=ALU.add)
            S.op("dve", "tensor_tensor", out=s3["yt"][:], in0=pk, in1=bcast_mid(mYs, NCH), op=ALU.mult)
            to_bd(S, bd["AKT"], s3["yt"][:])
            S.op("act", "activation", out=s3["Xs"][:], in_=c3(P[0][:]), func=AF.Copy)
            S.op("dve", "tensor_copy", out=s3["Ys"][:], in_=c3(P[1][:]))
            to_bd(S, bd["X"], s3["Xs"][:])
            to_bd(S, bd["Y"], s3["Ys"][:])
            for stg in range(1, 6):
                for ch in range(NCH):
                    S.mm(P[2][:, ch * CH:(ch + 1) * CH], bd["X"][:, ch, :], s3["ATs"][:, ch, :])
                if stg < 5:
                    for ch in range(NCH):
                        S.mm(P[0][:, ch * CH:(ch + 1) * CH], bd["Y"][:, ch, :], s3["Xs"][:, ch, :])
                    if stg < 4:
                        for ch in range(NCH):
                            S.mm(P[1][:, ch * CH:(ch + 1) * CH], bd["X"][:, ch, :], s3["Ys"][:, ch, :])
                S.op("dve", "tensor_tensor", out=s3["ATs"][:], in0=s3["ATs"][:], in1=c3(P[2][:]), op=ALU.add)
                if stg < 5:
                    if stg < 4:
                        S.op("act", "activation", out=s3["Xs"][:], in_=c3(P[0][:]), func=AF.Copy)
                        S.op("dve", "tensor_copy", out=s3["Ys"][:], in_=c3(P[1][:]))
                        to_bd(S, bd["X"], s3["Xs"][:])
                        to_bd(S, bd["Y"], s3["Ys"][:])
                    else:
                        to_bd(S, bd["X"], c3(P[0][:]))
                if stg == 1:
                    prb = amat(P[3], "bt", "rt")
                    S.op("dve", "tensor_tensor", out=s3["yc"][:], in0=prb, in1=bcast_mid(mYi, NCH), op=ALU.mult)
                    to_bd(S, bd["RBT"], s3["yc"][:])
                if stg == 2:
                    prk = amat(P[3], "kt", "rt")
                    S.op("dve", "tensor_tensor", out=s3["ysq"][:], in0=prk, in1=bcast_mid(mYi, NCH), op=ALU.mult)
                    to_bd(S, bd["RKT"], s3["ysq"][:])
                if stg == 3:
                    for ch in range(NCH):
                        S.mm(P[3][:, ch * CH:(ch + 1) * CH], bd["AKT"][:, ch, :], s3["Vtm"][:, ch, :])
                    S.op("act", "activation", out=s3["AVs"][:], in_=c3(P[3][:]), func=AF.Copy)
                if stg == 4:
                    for ch in range(NCH):
                        S.mm(P[3][:, ch:ch + 1], bd["u"][:, ch, :], ones_col[:])
                    S.op("dve", "tensor_copy", out=st2["rk"][:], in_=P[3][:, 0:NCH])
                    for hh in range(2):
                        S.mm(P[7][0:64, :], gup[:, (2 * c + hh) * 64:(2 * c + hh + 1) * 64], glt[:])
                        S.op("act", "activation", out=g_sb[:, hh, :], in_=P[7][0:64, :], func=AF.Copy)
            to_bd(S, bd["TT"], s3["ATs"][:])
            for ch in range(NCH):
                S.mm(P[1][:, ch * CH:(ch + 1) * CH], bd["atT"][:, ch, :], s3["ATs"][:, ch, :])
            to_bd(S, bd["AhT"], c3(P[1][:]))
            nxt = prep(gi + 1) if gi + 1 < len(tiles) else None
            step(nxt)
            for ch in range(NCH):
                S0 = Sst[sidx]
                S1 = Sst[1 - sidx]
                cc = slice(ch * CH, (ch + 1) * CH)
                S.mm(P[4].p(ch)[:, cc], bd["AhT"][:, ch, :], S0[:], start=True, stop=False)
                S.mm(P[4].p(ch)[:, cc], bd["TT"][:, ch, :], s3["AVs"][:, ch, :], start=False, stop=True)
                S.op("act", "activation", out=s3["U8"].p(ch)[:, ch, :], in_=P[4].p(ch)[:, cc], func=AF.Copy)
                S.mm(P[6].p(ch)[:, cc], bd["bhT"][:, ch, :], s3["U8"].p(ch)[:, ch, :], start=True, stop=False)
                S.mm(P[6].p(ch)[:, cc], bd["khT"][:, ch, :], s3["Vtm"][:, ch, :], start=False, stop=True)
                S.op("dve", "scalar_tensor_tensor", out=S1[:], in0=S0[:], scalar=Wc[:, ch:ch + 1], in1=P[6].p(ch)[:, cc],
                     op0=ALU.mult, op1=ALU.add)
                S.mm(P[5].p(ch)[:, cc], rtbd[:, ch, :], S0[:], start=True, stop=False)
                S.mm(P[5].p(ch)[:, cc], bd["RBT"][:, ch, :], s3["U8"].p(ch)[:, ch, :], start=False, stop=False)
                S.mm(P[5].p(ch)[:, cc], bd["RKT"][:, ch, :], s3["Vtm"][:, ch, :], start=False, stop=True)
                sidx = 1 - sidx
                step(nxt)
                step(nxt)
            y8, yc, ysq, yt = s3["y8"], s3["yc"], s3["ysq"], s3["yt"]
            S.op("act", "activation", out=y8[:], in_=c3(P[5][:]), func=AF.Copy)
            S.op("dve", "tensor_reduce", out=st2["mean"][:], in_=y8[:], axis=AX.X, op=ALU.add)
            S.op("dve", "tensor_scalar", out=st2["mean"][:], in0=st2["mean"][:], scalar1=1.0 / CH, scalar2=None, op0=ALU.mult)
            S.op("dve", "tensor_tensor", out=yc[:], in0=y8[:], in1=bcast_last(st2["mean"][:], CH), op=ALU.subtract)
            S.op("act", "activation", out=ysq[:], in_=yc[:], func=AF.Square)
            S.op("dve", "tensor_reduce", out=st2["var"][:], in_=ysq[:], axis=AX.X, op=ALU.add)
            S.op("act", "activation", out=st2["var"][:], in_=st2["var"][:], func=AF.Sqrt, bias=epsl[:, 0:1], scale=1.0 / CH)
            S.op("dve", "reciprocal", out=st2["var"][:], in_=st2["var"][:])
            S.op("dve", "tensor_tensor", out=yc[:], in0=yc[:], in1=bcast_last(st2["var"][:], CH), op=ALU.mult)
            S.op("dve", "tensor_tensor", out=yc[:], in0=yc[:], in1=bcast_mid(lng[:, c, :], NCH), op=ALU.mult)
            S.op("dve", "tensor_tensor", out=yc[:], in0=yc[:], in1=bcast_mid(lnb[:, c, :], NCH), op=ALU.add)
            S.op("dve", "tensor_tensor", out=yt[:], in0=s3["Vtm"][:], in1=bcast_last(st2["rk"][:], CH), op=ALU.mult)
            S.op("dve", "tensor_tensor", out=yc[:], in0=yc[:], in1=yt[:], op=ALU.add)
            drain(nxt)
            for half in range(2):
                pt = P[half]
                for q4 in range(4):
                    ch = half * 4 + q4
                    S.transpose(pt[0:64, q4 * 128:(q4 + 1) * 128], yc[:, ch, :], ident[:])
                p4 = dview(pt[0:64, :], "p (c h j) -> p c h j", h=2, j=CH)
                for hh in range(2):
                    S.op("dve", "tensor_tensor", out=dview(ym[hh][:, half * 256:(half + 1) * 256], "p (c j) -> p c j", j=CH),
                         in0=p4[:, :, hh, :], in1=dview(g_sb[:, hh, half * 256:(half + 1) * 256], "p (c j) -> p c j", j=CH), op=ALU.mult)
            for hh in range(2):
                S.dma(YD.ap(YD.h[2 * c + hh, :, ts], key=(c, it, hh)), ym[hh][:], q="pool")


WSHAPES = {
    'mem_norm_g': (D,),
    'a_norm1_g': ('A', D), 'a_w_in': ('A', D, 2816), 'a_shift_mu': ('A', 2560), 'a_decay_up': ('A', 64, 768),
    'a_decay_bias': ('A', 768), 'a_aaa_up': ('A', 64, 768), 'a_aaa_bias': ('A', 768), 'a_gate_up': ('A', 128, 768),
    'a_k_k': ('A', 768), 'a_k_a': ('A', 768), 'a_r_k': ('A', 12, 64), 'a_lnx_g': ('A', 768), 'a_lnx_b': ('A', 768),
    'a_mem_kv': ('A', D, 512), 'a_w_out': ('A', D, D), 'a_norm2_g': ('A', D), 'a_ffn_gu': ('A', D, 2 * FH),
    'a_ffn_down': ('A', FH, D),
    'vres_mu': ('V', D), 'vres_down': ('V', D, 32), 'vres_up': ('V', 32, 768), 'vres_bias': ('V', 768),
    'kv_norm_g': (D,), 'kv_w_down': (D, 288), 'kv_latent_g': (256,), 'kv_w_up': (256, 1536),
    'b_norm1_g': ('B', D), 'b_w_in': ('B', D, 768), 'b_q_norm_g': ('B', 512), 'b_q_up': ('B', 512, 1152),
    'b_mem_kv': ('B', D, 512), 'b_w_out': ('B', D, D), 'b_norm2_g': ('B', D), 'b_ffn_gu': ('B', D, 2 * FH),
    'b_ffn_down': ('B', FH, D),
    'final_norm_g': (D,),
}


def build_program(ntok, N_A, N_B):
    nc = bass.Bass("TRN2", target_bir_lowering=False)
    dims = {'A': N_A, 'B': N_B, 'V': max(N_A - 1, 0)}
    with ExitStack() as stack:
        S = Sched(nc, stack)
        Wd = {}
        Wd["xT"] = S.dram("xT", [D, ntok], F32, kind="ExternalInput")
        Wd["memT"] = S.dram("memT", [D, NMEM], F32, kind="ExternalInput")[:]
        Wd["pos"] = S.dram("pos", [ntok], I32, kind="ExternalInput")[:]
        Wd["invf"] = S.dram("invf", [128], F32, kind="ExternalInput")[:]
        Wd["cmask"] = S.dram("cmask", [128, 4, 64], F32, kind="ExternalInput")[:]
        for nm, shp in WSHAPES.items():
            shp = [dims[s] if isinstance(s, str) else s for s in shp]
            if 0 in shp:
                continue
            Wd[nm] = S.dram(nm, shp, F32, kind="ExternalInput")[:]
        outT = S.dram("outT", [D, ntok], F32, kind="ExternalOutput")
        C = consts(S, Wd)
        memn = S.sb("memn", [128, KD, NMEM], BF16)
        mem_prep(S, C, Wd, memn)
        x_cur = Wd["xT"]
        nlayers = N_A + N_B
        li = 0
        st = {}
        for i in range(N_A):
            x_mid = S.dram(f"xmid{li}", [D, ntok], F32)
            rwkv_layer(S, C, x_cur, x_mid, Wd, i, ntok, memn, st)
            li += 1
            last = (li == nlayers)
            x_next = outT if last else S.dram(f"xres{li}", [D, ntok], F32)
            outs = ffn_phase(S, C, x_mid, x_next, Wd["a_norm2_g"][i], Wd["a_ffn_gu"][i], Wd["a_ffn_down"][i], ntok,
                             tag=f"fa{i}_", final_g=Wd["final_norm_g"] if last else None)
            x_cur = x_next
        if N_B > 0:
            tabs = {k: S.dram("tab_" + k, [128, ntok], F32) for k in ("cosk", "sink", "cosq", "sinq")}
            rope_tables_phase(S, C, Wd["pos"], Wd["invf"], tabs, ntok)
            KN = S.dram("KN", [12, 64, ntok], BF16)
            KR = S.dram("KR", [32, ntok], BF16)
            VD = S.dram("VD", [ntok // 128, 128, 12, 65], BF16)
            kv_phase(S, C, x_cur, Wd, ntok, KN, KR, VD, tabs)
        for j in range(N_B):
            QD = S.dram(f"QD{j}", [12, 96, ntok], BF16)
            MD = S.dram(f"MD{j}", [4, 64, ntok], BF16)
            OD = S.dram(f"OD{j}", [12, 64, ntok], BF16)
            PH = ("q", "attn", "out")
            if "q" in PH:
                mla_q_phase(S, C, x_cur, Wd, j, ntok, QD, MD, memn, tabs)
            if "attn" in PH:
                attn_phase(S, C, QD, KN, KR, VD, OD, ntok)
            x_mid = S.dram(f"xmid{li}", [D, ntok], F32)
            if "out" in PH:
                outproj_phase(S, C, x_cur, x_mid, OD, MD, Wd["b_w_out"][j], ntok, tag=f"ob{j}_")
            else:
                x_mid = x_cur
            li += 1
            last = (li == nlayers)
            x_next = outT if last else S.dram(f"xres{li}", [D, ntok], F32)
            outs = ffn_phase(S, C, x_mid, x_next, Wd["b_norm2_g"][j], Wd["b_ffn_gu"][j], Wd["b_ffn_down"][j], ntok,
                             tag=f"fb{j}_", final_g=Wd["final_norm_g"] if last else None)
            x_cur = x_next
        S.barrier_wait("sp", outs)
        S.emit()
        print("stats", S.stats, flush=True)
    return nc


def host_inputs(inputs, b, ntok):
    m = {}
    m["xT"] = np.ascontiguousarray(np.asarray(inputs["x"])[b, :ntok].T)
    m["memT"] = np.ascontiguousarray(np.asarray(inputs["mem"])[b].T)
    m["pos"] = np.ascontiguousarray(np.asarray(inputs["positions"])[b, :ntok]).astype(np.int32)
    m["invf"] = np.tile((10000.0 ** (-(np.arange(16, dtype=np.float32)) / 16)).astype(np.float32), 8)
    pj = (np.arange(128) % 64)[:, None]
    jj = np.arange(64)[None, :]
    m["cmask"] = np.ascontiguousarray(np.stack([jj < pj, jj > pj, jj >= pj, jj == pj], axis=1).astype(np.float32))
    for nm in WSHAPES:
        a = np.asarray(inputs[nm])
        if a.size == 0:
            continue
        m[nm] = np.ascontiguousarray(a, dtype=np.float32)
    return m

_NC_CACHE = {}


def kernel(**inputs):
    x = np.asarray(inputs["x"])
    B, S_len, _ = x.shape
    key = (S_len,)
    if key not in _NC_CACHE:
        _NC_CACHE[key] = build_program(S_len, 2, 2)
    nc = _NC_CACHE[key]
    in_maps = [host_inputs(inputs, b, S_len) for b in range(B)]
    res = run_bass_kernel_spmd(nc, in_maps, core_ids=list(range(B)))
    out = np.stack([np.ascontiguousarray(res.results[b]["outT"].T) for b in range(B)], axis=0)
    return out.astype(np.float32)
```

```python
import math
import numpy as np
from contextlib import ExitStack, contextmanager
import concourse.bass as bass
import concourse.mybir as mybir
from concourse.bass_utils import run_bass_kernel_spmd


F32 = mybir.dt.float32
BF16 = mybir.dt.bfloat16
I32 = mybir.dt.int32
ALU = mybir.AluOpType
AF = mybir.ActivationFunctionType
AX = mybir.AxisListType

ENGS = ("pe", "dve", "act", "pool", "sp")
NSEM = 6
NDMASEM = 24


class Dep:
    __slots__ = ("w", "r")

    def __init__(self):
        self.w = None
        self.r = []


class Tile:
    def __init__(self, h, name):
        self.h = h
        self.name = name
        self.whole = Dep()
        self.parts = {}

    def __getitem__(self, idx):
        return V(self, None, self.h[idx])

    def p(self, key, idx=None):
        return V(self, key, self.h[idx] if idx is not None else self.h[:])

    def ap(self, ap, key=None):
        return V(self, key, ap)


class V:
    __slots__ = ("t", "k", "ap")

    def __init__(self, t, k, ap):
        self.t, self.k, self.ap = t, k, ap

    def __getitem__(self, idx):
        return V(self.t, self.k, self.ap[idx])

    def deps(self):
        t = self.t
        if self.k is None:
            return [t.whole] + list(t.parts.values())
        if self.k not in t.parts:
            t.parts[self.k] = Dep()
        return [t.whole, t.parts[self.k]]

    def own(self):
        t = self.t
        if self.k is None:
            return t.whole
        return t.parts[self.k]


class Op:
    __slots__ = ("id", "eng", "fn", "deps", "signal", "sem", "val", "isdma", "waits")

    def __init__(self, id, eng, fn, isdma):
        self.id, self.eng, self.fn, self.isdma = id, eng, fn, isdma
        self.deps = set()
        self.signal = False
        self.sem = None
        self.val = 0
        self.waits = []


class Sched:
    def __init__(self, nc, stack):
        self.nc = nc
        self.stack = stack
        self.ops = []
        self.n_sb = 0

    def sb(self, name, shape, dt=F32):
        h = self.stack.enter_context(self.nc.sbuf_tensor(name, list(shape), dt))
        return Tile(h, name)

    def ps(self, name, shape, dt=F32):
        h = self.stack.enter_context(self.nc.psum_tensor(name, list(shape), dt))
        return Tile(h, name)

    def dram(self, name, shape, dt=F32, kind="Internal"):
        h = self.nc.dram_tensor(name, list(shape), dt, kind=kind)
        return Tile(h.ap(), name)

    def add(self, eng, fn, reads=(), writes=(), isdma=False):
        op = Op(len(self.ops), eng, fn, isdma)
        for v in reads:
            for d in v.deps():
                if d.w is not None:
                    op.deps.add(d.w)
        for v in writes:
            for d in v.deps():
                if d.w is not None:
                    op.deps.add(d.w)
                op.deps.update(d.r)
        for v in reads:
            v.own().r.append(op.id)
        for v in writes:
            o = v.own()
            o.w = op.id
            o.r = []
            if v.k is None:
                v.t.parts = {}
        op.deps.discard(op.id)
        if eng == "pe":
            op.deps = {d for d in op.deps if self.ops[d].eng != "pe" or self.ops[d].isdma}
        self.ops.append(op)
        return op

    def op(self, eng, method, extra_reads=(), extra_writes=(), isdma=False, **kw):
        reads, writes, args = list(extra_reads), list(extra_writes), {}
        for k, v in kw.items():
            if isinstance(v, V):
                (writes if (k.startswith("out") or k == "accum_out") else reads).append(v)
                args[k] = v.ap
            else:
                args[k] = v
        return self.add(eng, lambda e: getattr(e, method)(**args), reads=reads, writes=writes, isdma=isdma)

    def dma(self, out, in_, q="sp", **kw):
        return self.op(q, "dma_start", isdma=True, out=out, in_=in_, **kw)

    def mm(self, out, lhsT, rhs, start=True, stop=True):
        o, l, r = out.ap, lhsT.ap, rhs.ap
        return self.add("pe", lambda e: e.matmul(o, l, r, start=start, stop=stop),
                        reads=[lhsT, rhs], writes=[out])

    def transpose(self, out, in_, ident):
        o, i, d = out.ap, in_.ap, ident.ap
        return self.add("pe", lambda e: e.transpose(o, i, d),
                        reads=[in_, ident], writes=[out])

    def barrier_wait(self, eng, ops):
        op = Op(len(self.ops), eng, None, False)
        op.deps = {o.id for o in ops}
        self.ops.append(op)
        return op

    def emit(self):
        nc = self.nc
        ops = self.ops
        for op in ops:
            for d in op.deps:
                ops[d].signal = True
        sems = {}
        for e in ("pe", "dve", "act", "pool"):
            sems[e] = [self.stack.enter_context(nc.semaphore(f"s_{e}{i}")) for i in range(NSEM)]
        dsems = {}
        dma_queues = sorted({op.eng for op in ops if op.isdma})
        for q in dma_queues:
            dsems[q] = [self.stack.enter_context(nc.semaphore(f"d_{q}{i}")) for i in range(NDMASEM)]
        cnt = {e: 0 for e in ENGS}
        dcnt = {q: 0 for q in dma_queues}
        prev_on_slot = {}
        for op in ops:
            if op.isdma:
                k = dcnt[op.eng]
                dcnt[op.eng] += 1
                slot = k % NDMASEM
                op.sem = dsems[op.eng][slot]
                op.val = 16 * (k // NDMASEM + 1)
                pk = (op.eng, slot)
                if pk in prev_on_slot:
                    op.deps.add(prev_on_slot[pk])
                prev_on_slot[pk] = op.id
                op.signal = True
            elif op.signal:
                k = cnt[op.eng]
                cnt[op.eng] += 1
                op.sem = sems[op.eng][k % NSEM]
                op.val = k // NSEM + 1
        waited = {e: {} for e in ENGS}
        per_eng = {e: [] for e in ENGS}
        for op in ops:
            need = {}
            for d in op.deps:
                p = ops[d]
                key = id(p.sem)
                if key not in need or need[key][1] < p.val:
                    need[key] = (p.sem, p.val)
            w = waited[op.eng]
            for key, (sem, val) in need.items():
                if w.get(key, 0) >= val:
                    continue
                w[key] = val
                op.waits.append((sem, val))
            per_eng[op.eng].append(op)
        self.stats = {e: len(per_eng[e]) for e in ENGS}
        self.stats["waits"] = sum(len(o.waits) for o in ops)

        def run(eng_obj, lst):
            for op in lst:
                for sem, val in op.waits:
                    eng_obj.wait_ge(sem, val)
                if op.fn is None:
                    continue
                ins = op.fn(eng_obj)
                if op.signal:
                    ins.then_inc(op.sem, 16 if op.isdma else 1)

        with nc.Block() as block:
            @block.sync
            def _(e):
                run(e, per_eng["sp"])

            @block.tensor
            def _(e):
                run(e, per_eng["pe"])

            @block.vector
            def _(e):
                run(e, per_eng["dve"])

            @block.scalar
            def _(e):
                run(e, per_eng["act"])

            @block.gpsimd
            def _(e):
                run(e, per_eng["pool"])


def _sched_scope_init(self):
    if not hasattr(self, "stacks"):
        self.stacks = [self.stack]
        self.last_barrier = 0


def _sb(self, name, shape, dt=F32):
    _sched_scope_init(self)
    self.n_sb += 1
    name = f"{name}_{self.n_sb}"
    h = self.stacks[-1].enter_context(self.nc.sbuf_tensor(name, list(shape), dt))
    return Tile(h, name)


def _ps(self, name, shape, dt=F32):
    _sched_scope_init(self)
    h = self.stacks[-1].enter_context(self.nc.psum_tensor(name, list(shape), dt))
    return Tile(h, name)


def _barrier_all(self):
    _sched_scope_init(self)
    last = {}
    dmas = set()
    for op in self.ops[self.last_barrier:]:
        if op.fn is None:
            continue
        last[op.eng] = op.id
        if op.isdma:
            dmas.add(op.id)
    deps = set(last.values()) | dmas
    if not deps:
        return
    for e in ENGS:
        b = Op(len(self.ops), e, None, False)
        b.deps = set(deps)
        self.ops.append(b)
    self.last_barrier = len(self.ops)


@contextmanager
def _scope(self):
    _sched_scope_init(self)
    st = ExitStack()
    self.stacks.append(st)
    try:
        yield
    finally:
        self.barrier_all()
        self.stacks.pop()
        st.close()


Sched.sb = _sb
Sched.ps = _ps
Sched.barrier_all = _barrier_all
Sched.scope = _scope


D = 1024
KD = D // 128
FH = 2816
HD = 64
NMEM = 256
T = 512


def dview(v, pattern, **kw):
    return V(v.t, v.k, v.ap.rearrange(pattern, **kw))


def consts(S, Wd):
    C = {}
    for nm, val in (("ones1024", 1.0 / 1024), ("ones512", 1.0 / 512), ("ones256", 1.0 / 256)):
        C[nm] = S.sb(nm, [128, 128], BF16)
        S.op("pool", "memset", ap=C[nm][:], constant=val, extra_writes=[C[nm][:]])
    C["ones_mean"] = C["ones1024"]
    C["eps6"] = S.sb("eps6", [128, 1], F32)
    S.op("pool", "memset", ap=C["eps6"][:], constant=1e-6, extra_writes=[C["eps6"][:]])
    C["sel"] = S.sb("sel", [128, 64], F32)
    S.op("pool", "memset", ap=C["sel"][:], constant=0.0, extra_writes=[C["sel"][:]])
    S.op("pool", "memset", ap=C["sel"][64:65, :], constant=1.0, extra_writes=[C["sel"][:]])
    C["ident"] = S.sb("ident", [128, 128], F32)
    S.op("pool", "memset", ap=C["ident"][:], constant=1.0, extra_writes=[C["ident"][:]])
    S.op("pool", "affine_select", out=C["ident"][:], in_=C["ident"][:], pattern=[[-1, 128]],
         compare_op=ALU.is_equal, fill=0.0, base=0, channel_multiplier=1)
    C["tri"] = S.sb("tri", [128, 128], BF16)
    S.op("pool", "memset", ap=C["tri"][:], constant=1.0, extra_writes=[C["tri"][:]])
    S.op("pool", "affine_select", out=C["tri"][:], in_=C["tri"][:], pattern=[[1, 128]],
         compare_op=ALU.is_ge, fill=0.0, base=0, channel_multiplier=-1)
    C["bones"] = S.sb("bones", [128, 128], F32)
    S.op("pool", "memset", ap=C["bones"][:], constant=0.0, extra_writes=[C["bones"][:]])
    S.op("pool", "memset", ap=C["bones"][0:64, 0:64], constant=1.0, extra_writes=[C["bones"][:]])
    S.op("pool", "memset", ap=C["bones"][64:128, 64:128], constant=1.0, extra_writes=[C["bones"][:]])
    C["PP"] = [S.ps(f"PP{i}", [128, 1024], F32) for i in range(4)]
    C["P"] = []
    for i in range(4):
        for hf in range(2):
            C["P"].append(Tile(C["PP"][i].h[:, hf * 512:(hf + 1) * 512], f"P{2 * i + hf}"))
    return C


def rms_rstd(S, C, x, ps_ss, sq, rstd, nk, ones, n=T):
    for kc in range(nk):
        S.op("act", "activation", out=sq[:, :n], in_=x[:, kc, :], func=AF.Square)
        S.mm(ps_ss[:, :n], ones[:], sq[:, :n], start=(kc == 0), stop=(kc == nk - 1))
    S.op("act", "activation", out=rstd[:, :n], in_=ps_ss[:, :n], func=AF.Sqrt, bias=C["eps6"][:, 0:1], scale=1.0)
    S.op("dve", "reciprocal", out=rstd[:, :n], in_=rstd[:, :n])


def load_vec(S, dst, src, q="sp"):
    S.dma(dst, dview(src, "(k p) -> p k", p=128), q=q, allow_slow_non_contiguous=True)


def norm_apply(S, h, x, g_sb, rstd, nk, n=T):
    for kc in range(nk):
        S.op("dve", "scalar_tensor_tensor", out=h.p(kc)[:, kc, :], in0=x[:, kc, :], scalar=g_sb[:, kc:kc + 1],
             in1=rstd[:, :n], op0=ALU.mult, op1=ALU.mult)


def attn_finish(S, C, ps_o, osb, ps_d, rec, out_v, n=T):
    S.op("act", "activation", out=osb[0:65, :n], in_=ps_o[0:65, :n], func=AF.Copy)
    S.mm(ps_d[0:64, :n], C["sel"][0:65, 0:64], osb[0:65, :n])
    S.op("dve", "reciprocal", out=rec[0:64, :n], in_=ps_d[0:64, :n])
    S.op("dve", "tensor_tensor", out=out_v, in0=osb[0:64, :n], in1=rec[0:64, :n], op=ALU.mult)


def ffn_phase(S, C, x_in, x_out, g_dram, gu_dram, down_dram, ntok, FHc=FH, tag="f", final_g=None):
    NH = FHc // 128
    outs = []
    with S.scope():
        gu_sb = S.sb(tag + "gu_sb", [128, KD, 2 * FHc], BF16)
        dn_sb = S.sb(tag + "dn_sb", [128, NH, D], BF16)
        g_sb = S.sb(tag + "g_sb", [128, KD], F32)
        for kc in range(KD):
            S.dma(gu_sb.p(("k", kc))[:, kc, :], gu_dram[kc * 128:(kc + 1) * 128, :], q="pool")
        for j in range(NH):
            S.dma(dn_sb.p(("j", j))[:, j, :], down_dram[j * 128:(j + 1) * 128, :], q="pool")
        load_vec(S, g_sb[:], g_dram)
        if final_g is not None:
            fg_sb = S.sb(tag + "fg_sb", [128, KD], F32)
            load_vec(S, fg_sb[:], final_g)
        xs = [S.sb(tag + f"x{i}", [128, KD, T], F32) for i in range(2)]
        h = S.sb(tag + "h", [128, KD, T], BF16)
        sq = S.sb(tag + "sq", [128, T], BF16)
        rstd = S.sb(tag + "rstd", [128, T], F32)
        hid = S.sb(tag + "hid", [128, NH, T], BF16)
        sg = [S.sb(tag + f"sg{i}", [128, T], BF16) for i in range(2)]
        P = C["P"]
        ps_ss, ps_g, ps_u, ps_o = P[0], P[1:3], P[3:5], P[5:7]
        xin_v = x_in.h.rearrange("(k p) t -> p k t", p=128)
        xout_v = x_out.h.rearrange("(k p) t -> p k t", p=128)
        for it in range(ntok // T):
            x = xs[it % 2]
            S.dma(x[:], x_in.ap(xin_v[:, :, it * T:(it + 1) * T], key=it), q="sp")
            rms_rstd(S, C, x, ps_ss, sq, rstd, KD, C["ones1024"])
            norm_apply(S, h, x, g_sb, rstd, KD)
            for j in range(NH):
                pg, pu = ps_g[j % 2], ps_u[j % 2]
                for kc in range(KD):
                    S.mm(pg[:], gu_sb.p(("k", kc))[:, kc, j * 128:(j + 1) * 128], h.p(kc)[:, kc, :],
                         start=(kc == 0), stop=(kc == KD - 1))
                for kc in range(KD):
                    S.mm(pu[:], gu_sb.p(("k", kc))[:, kc, FHc + j * 128:FHc + (j + 1) * 128], h.p(kc)[:, kc, :],
                         start=(kc == 0), stop=(kc == KD - 1))
                s = sg[j % 2]
                S.op("act", "activation", out=s[:], in_=pg[:], func=AF.Silu)
                S.op("dve", "tensor_tensor", out=hid.p(j)[:, j, :], in0=s[:], in1=pu[:], op=ALU.mult)
            for oc in range(KD):
                po = ps_o[oc % 2]
                for j in range(NH):
                    S.mm(po[:], dn_sb.p(("j", j))[:, j, oc * 128:(oc + 1) * 128], hid.p(j)[:, j, :],
                         start=(j == 0), stop=(j == NH - 1))
                S.op("dve", "tensor_tensor", out=x[:, oc, :], in0=x[:, oc, :], in1=po[:], op=ALU.add)
            if final_g is not None:
                rms_rstd(S, C, x, ps_ss, sq, rstd, KD, C["ones1024"])
                for kc in range(KD):
                    S.op("dve", "scalar_tensor_tensor", out=x[:, kc, :], in0=x[:, kc, :], scalar=fg_sb[:, kc:kc + 1],
                         in1=rstd[:], op0=ALU.mult, op1=ALU.mult)
            outs.append(S.dma(x_out.ap(xout_v[:, :, it * T:(it + 1) * T], key=it), x[:], q="pool"))
    return outs


def outproj_phase(S, C, x_in, x_out, YD, MD, w_out, ntok, tag="o"):
    with S.scope():
        wo = S.sb(tag + "wo", [64, 16, D], BF16)
        S.dma(wo[:], dview(w_out, "(c p) n -> p c n", p=64), q="pool")
        xs = [S.sb(tag + f"x{i}", [128, KD, T], F32) for i in range(2)]
        ys = [S.sb(tag + f"y{i}", [64, 16, T], BF16) for i in range(2)]
        P = C["P"]
        xin_v = x_in.h.rearrange("(k p) t -> p k t", p=128)
        xout_v = x_out.h.rearrange("(k p) t -> p k t", p=128)
        yv = YD.h.rearrange("h p t -> p h t")
        mv = MD.h.rearrange("h p t -> p h t")
        for it in range(ntok // T):
            x, y = xs[it % 2], ys[it % 2]
            ts = slice(it * T, (it + 1) * T)
            S.dma(x[:], x_in.ap(xin_v[:, :, ts], key=it), q="sp")
            S.dma(y.p("y")[:, 0:12, :], YD.ap(yv[:, :, ts], key=it), q="sp")
            S.dma(y.p("m")[:, 12:16, :], MD.ap(mv[:, :, ts], key=it), q="sp")
            for oc in range(KD):
                po = P[1 + oc % 2]
                for hc in range(16):
                    S.mm(po[:], wo[:, hc, oc * 128:(oc + 1) * 128], y.p("y" if hc < 12 else "m")[:, hc, :],
                         start=(hc == 0), stop=(hc == 15))
                S.op("dve", "tensor_tensor", out=x[:, oc, :], in0=x[:, oc, :], in1=po[:], op=ALU.add)
            S.dma(x_out.ap(xout_v[:, :, ts], key=it), x[:], q="pool")


NH_MLA = 12
SCALE_MLA = (64 + 32) ** -0.5
SCALE_MEM = 64 ** -0.5


def rope_tables_phase(S, C, pos, invf, tabs, ntok):
    CH = 1024 if ntok >= 1024 else ntok
    C1 = 6.28125
    C2 = 2 * math.pi - 6.28125
    with S.scope():
        iv = S.sb("rt_iv", [128, 1], F32)
        S.dma(iv[:], dview(invf, "(p o) -> p o", o=1), q="sp")
        pi_t = S.sb("rt_pi", [128, CH], I32)
        pf = S.sb("rt_pf", [128, CH], F32)
        ang = S.sb("rt_ang", [128, CH], F32)
        tmp = S.sb("rt_tmp", [128, CH], F32)
        ki = S.sb("rt_ki", [128, CH], I32)
        kf = S.sb("rt_kf", [128, CH], F32)
        r = S.sb("rt_r", [128, CH], F32)
        o = {k: S.sb("rt_o" + k, [128, CH], F32) for k in ("sink", "cosk", "sinq", "cosq")}
        for c in range(ntok // CH):
            cs = slice(c * CH, (c + 1) * CH)
            S.dma(pi_t[:], V(pos.t, None, pos.ap[cs].partition_broadcast(128)), q="sp")
            S.op("dve", "tensor_copy", out=pf[:], in_=pi_t[:])
            S.op("dve", "tensor_scalar", out=ang[:], in0=pf[:], scalar1=iv[:, 0:1], scalar2=None, op0=ALU.mult)
            S.op("dve", "tensor_scalar", out=tmp[:], in0=ang[:], scalar1=1.0 / (2 * math.pi), scalar2=None, op0=ALU.mult)
            S.op("dve", "tensor_copy", out=ki[:], in_=tmp[:])
            S.op("dve", "tensor_copy", out=kf[:], in_=ki[:])
            S.op("dve", "scalar_tensor_tensor", out=r[:], in0=kf[:], scalar=-C1, in1=ang[:], op0=ALU.mult, op1=ALU.add)
            S.op("dve", "scalar_tensor_tensor", out=r[:], in0=kf[:], scalar=-C2, in1=r[:], op0=ALU.mult, op1=ALU.add)
            S.op("dve", "tensor_scalar", out=tmp[:], in0=r[:], scalar1=math.pi, scalar2=-2 * math.pi, op0=ALU.is_gt, op1=ALU.mult)
            S.op("dve", "tensor_tensor", out=r[:], in0=r[:], in1=tmp[:], op=ALU.add)
            S.op("act", "activation", out=o["sink"][:], in_=r[:], func=AF.Sin)
            S.op("dve", "tensor_scalar", out=r[:], in0=r[:], scalar1=math.pi / 2, scalar2=None, op0=ALU.add)
            S.op("dve", "tensor_scalar", out=tmp[:], in0=r[:], scalar1=math.pi, scalar2=-2 * math.pi, op0=ALU.is_gt, op1=ALU.mult)
            S.op("dve", "tensor_tensor", out=r[:], in0=r[:], in1=tmp[:], op=ALU.add)
            S.op("act", "activation", out=o["cosk"][:], in_=r[:], func=AF.Sin)
            S.op("dve", "tensor_scalar", out=o["sinq"][:], in0=o["sink"][:], scalar1=SCALE_MLA, scalar2=None, op0=ALU.mult)
            S.op("dve", "tensor_scalar", out=o["cosq"][:], in0=o["cosk"][:], scalar1=SCALE_MLA, scalar2=None, op0=ALU.mult)
            for k in o:
                S.dma(tabs[k][:, cs], o[k][:], q="pool")


def rope_apply(S, t1, t2, cos, sin, o1, o2, tmpa, tmpb, np_, n=T):
    S.op("dve", "tensor_tensor", out=tmpa[0:np_, :n], in0=t1, in1=cos, op=ALU.mult)
    S.op("dve", "tensor_tensor", out=tmpb[0:np_, :n], in0=t2, in1=sin, op=ALU.mult)
    S.op("dve", "tensor_tensor", out=o1, in0=tmpa[0:np_, :n], in1=tmpb[0:np_, :n], op=ALU.subtract)
    S.op("dve", "tensor_tensor", out=tmpa[0:np_, :n], in0=t2, in1=cos, op=ALU.mult)
    S.op("dve", "tensor_tensor", out=tmpb[0:np_, :n], in0=t1, in1=sin, op=ALU.mult)
    S.op("dve", "tensor_tensor", out=o2, in0=tmpa[0:np_, :n], in1=tmpb[0:np_, :n], op=ALU.add)


def kv_phase(S, C, x_in, Wd, ntok, KN, KR, VD, tabs):
    P = C["P"]
    with S.scope():
        wdn = S.sb("kv_wdn", [128, KD, 288], BF16)
        for kc in range(KD):
            S.dma(wdn.p(kc)[:, kc, :], Wd["kv_w_down"][kc * 128:(kc + 1) * 128, :], q="pool")
        wup_n = S.sb("kv_wupn", [128, 2, 768], BF16)
        wup_v = S.sb("kv_wupv", [128, 2, 768], BF16)
        src = dview(Wd["kv_w_up"], "(k p) (h c) -> p k h c", p=128, c=128)
        for kc in range(2):
            S.dma(dview(wup_n.p(kc)[:, kc, :], "p (h c) -> p h c", c=64), src[:, kc, :, 0:64], q="pool")
            S.dma(dview(wup_v.p(kc)[:, kc, :], "p (h c) -> p h c", c=64), src[:, kc, :, 64:128], q="pool")
        kvg = S.sb("kv_g", [128, KD], F32)
        latg = S.sb("kv_latg", [128, 2], F32)
        load_vec(S, kvg[:], Wd["kv_norm_g"])
        load_vec(S, latg[:], Wd["kv_latent_g"])
        xs = [S.sb(f"kv_x{i}", [128, KD, T], F32) for i in range(2)]
        hk = S.sb("kv_hk", [128, KD, T], BF16)
        sq = S.sb("kv_sq", [128, T], BF16)
        rstd = S.sb("kv_rstd", [128, T], F32)
        ckv = S.sb("kv_ckv", [128, 2, T], F32)
        ckvn = S.sb("kv_ckvn", [128, 2, T], BF16)
        cs_t = S.sb("kv_cos", [16, T], F32)
        sn_t = S.sb("kv_sin", [16, T], F32)
        tmpa = S.sb("kv_tmpa", [16, T], F32)
        tmpb = S.sb("kv_tmpb", [16, T], F32)
        kr = [S.sb(f"kv_kr{i}", [16, 2, T], BF16) for i in range(2)]
        kn = [S.sb(f"kv_kn{i}", [128, T], BF16) for i in range(2)]
        vt = [S.sb(f"kv_vt{i}", [128, 4, 12, 65], BF16) for i in range(2)]
        for i in range(2):
            S.op("pool", "memset", ap=vt[i][:], constant=1.0, extra_writes=[vt[i][:]])
        xin_v = x_in.h.rearrange("(k p) t -> p k t", p=128)
        vd_v = VD.h.rearrange("n p h c -> p n h c")
        for it in range(ntok // T):
            ts = slice(it * T, (it + 1) * T)
            x = xs[it % 2]
            S.dma(x[:], x_in.ap(xin_v[:, :, ts], key=it), q="sp")
            S.dma(cs_t[:], tabs["cosk"][0:16, ts], q="sp")
            S.dma(sn_t[:], tabs["sink"][0:16, ts], q="sp")
            rms_rstd(S, C, x, P[0], sq, rstd, KD, C["ones1024"])
            norm_apply(S, hk, x, kvg, rstd, KD)
            for c in range(2):
                pm = P[1 + c]
                for kc in range(KD):
                    S.mm(pm[:], wdn.p(kc)[:, kc, c * 128:(c + 1) * 128], hk.p(kc)[:, kc, :], start=(kc == 0), stop=(kc == KD - 1))
                S.op("act", "activation", out=ckv[:, c, :], in_=pm[:], func=AF.Copy)
            for kc in range(KD):
                S.mm(P[3][0:16, :], wdn.p(kc)[:, kc, 256:272], hk.p(kc)[:, kc, :], start=(kc == 0), stop=(kc == KD - 1))
            for kc in range(KD):
                S.mm(P[4][0:16, :], wdn.p(kc)[:, kc, 272:288], hk.p(kc)[:, kc, :], start=(kc == 0), stop=(kc == KD - 1))
            rms_rstd(S, C, ckv, P[0], sq, rstd, 2, C["ones256"])
            norm_apply(S, ckvn, ckv, latg, rstd, 2)
            k = kr[it % 2]
            rope_apply(S, P[3][0:16, :], P[4][0:16, :], cs_t[:], sn_t[:], k[:, 0, :], k[:, 1, :], tmpa, tmpb, 16)
            S.dma(KR[0:16, ts], k[:, 0, :], q="pool")
            S.dma(KR[16:32, ts], k[:, 1, :], q="pool")
            for a in range(6):
                pm = P[1 + a % 2]
                for kc in range(2):
                    S.mm(pm[:], wup_n.p(kc)[:, kc, a * 128:(a + 1) * 128], ckvn.p(kc)[:, kc, :], start=(kc == 0), stop=(kc == 1))
                kk = kn[a % 2]
                S.op("act", "activation", out=kk[:], in_=pm[:], func=AF.Copy)
                S.dma(KN[2 * a, :, ts], kk[0:64, :], q="pool")
                S.dma(KN[2 * a + 1, :, ts], kk[64:128, :], q="pool")
            v = vt[it % 2]
            for st in range(4):
                for half in range(2):
                    pv = P[5 + half]
                    for kc in range(2):
                        S.mm(pv[:, 0:384], ckvn.p(kc)[:, kc, st * 128:(st + 1) * 128], wup_v.p(kc)[:, kc, half * 384:(half + 1) * 384],
                             start=(kc == 0), stop=(kc == 1))
                    S.op("act" if half == 0 else "dve", "activation" if half == 0 else "tensor_copy",
                         out=v[:, st, half * 6:(half + 1) * 6, 0:64], in_=dview(pv[:, 0:384], "p (h c) -> p h c", c=64),
                         **({"func": AF.Copy} if half == 0 else {}))
            S.dma(VD.ap(vd_v[:, it * 4:(it + 1) * 4, :, :], key=it), v[:], q="pool")


def mem_prep(S, C, Wd, memn):
    P = C["P"]
    with S.scope():
        mx = S.sb("mp_x", [128, KD, NMEM], F32)
        sq = S.sb("mp_sq", [128, NMEM], BF16)
        rstd = S.sb("mp_rstd", [128, NMEM], F32)
        g = S.sb("mp_g", [128, KD], F32)
        S.dma(mx[:], dview(Wd["memT"], "(k p) t -> p k t", p=128), q="sp")
        load_vec(S, g[:], Wd["mem_norm_g"])
        rms_rstd(S, C, mx, P[0], sq, rstd, KD, C["ones1024"], n=NMEM)
        norm_apply(S, memn, mx, g, rstd, KD, n=NMEM)


def mem_kv(S, C, memn, w_kv, MK, MV, tag):
    P = C["P"]
    with S.scope():
        wkv = S.sb(tag + "wkv", [128, KD, 512], BF16)
        for kc in range(KD):
            S.dma(wkv.p(kc)[:, kc, :], w_kv[kc * 128:(kc + 1) * 128, :], q="pool")
        S.op("pool", "memset", ap=MV[:], constant=1.0, extra_writes=[MV[:]])
        for c in range(2):
            pm = P[1 + c]
            for kc in range(KD):
                S.mm(pm[:, 0:NMEM], wkv.p(kc)[:, kc, c * 128:(c + 1) * 128], memn[:, kc, :], start=(kc == 0), stop=(kc == KD - 1))
            S.op("act", "activation", out=MK[:, c, :], in_=pm[:, 0:NMEM], func=AF.Copy)
        for mt in range(2):
            pm = P[3 + mt]
            for kc in range(KD):
                S.mm(pm[:, 0:256], memn[:, kc, mt * 128:(mt + 1) * 128], wkv.p(kc)[:, kc, 256:512], start=(kc == 0), stop=(kc == KD - 1))
            S.op("dve", "tensor_copy", out=MV[:, mt, :, 0:64], in_=dview(pm[:, 0:256], "p (h c) -> p h c", c=64))


def mem_attn(S, C, qm, MK, MV, MD, ts, it, bufs):
    P = C["P"]
    pT, osb, rec, mo = bufs
    for hm in range(4):
        c, pb = hm // 2, (hm % 2) * 64
        for mt in range(2):
            ps = P[1 + mt]
            S.mm(ps[:], MK[pb:pb + 64, c, mt * 128:(mt + 1) * 128], qm[pb:pb + 64, c, :])
            p = pT[mt]
            S.op("act", "activation", out=p[:], in_=ps[:], func=AF.Exp, scale=SCALE_MEM)
            S.mm(P[5][0:65, :], MV[:, mt, hm, :], p[:], start=(mt == 0), stop=(mt == 1))
        m = mo[hm % 2]
        attn_finish(S, C, P[5], osb, P[6], rec, m[:])
        S.dma(MD.ap(MD.h[hm, :, ts], key=it), m[:], q="pool")


def mla_q_phase(S, C, x_in, Wd, j, ntok, QD, MD, memn, tabs):
    P = C["P"]
    tag = f"q{j}_"
    with S.scope():
        MK = S.sb(tag + "MK", [128, 2, NMEM], BF16)
        MV = S.sb(tag + "MV", [128, 2, 4, 65], BF16)
        mem_kv(S, C, memn, Wd["b_mem_kv"][j], MK, MV, tag)
        win = S.sb(tag + "win", [128, KD, 768], BF16)
        for kc in range(KD):
            S.dma(win.p(kc)[:, kc, :], Wd["b_w_in"][j, kc * 128:(kc + 1) * 128, :], q="pool")
        qn_w = S.sb(tag + "qn_w", [128, 4, 768], BF16)
        qr1_w = S.sb(tag + "qr1_w", [128, 4, 192], BF16)
        qr2_w = S.sb(tag + "qr2_w", [128, 4, 192], BF16)
        src = dview(Wd["b_q_up"][j], "(k p) (h c) -> p k h c", p=128, c=96)
        for kc in range(4):
            S.dma(dview(qn_w.p(kc)[:, kc, :], "p (h c) -> p h c", c=64), src[:, kc, :, 0:64], q="pool")
            S.dma(dview(qr1_w.p(kc)[:, kc, :], "p (h c) -> p h c", c=16), src[:, kc, :, 64:80], q="pool")
            S.dma(dview(qr2_w.p(kc)[:, kc, :], "p (h c) -> p h c", c=16), src[:, kc, :, 80:96], q="pool")
        g1 = S.sb(tag + "g1", [128, KD], F32)
        qg = S.sb(tag + "qg", [128, 4], F32)
        load_vec(S, g1[:], Wd["b_norm1_g"][j])
        load_vec(S, qg[:], Wd["b_q_norm_g"][j])
        xs = [S.sb(tag + f"x{i}", [128, KD, T], F32) for i in range(2)]
        h = S.sb(tag + "h", [128, KD, T], BF16)
        sq = S.sb(tag + "sq", [128, T], BF16)
        rstd = S.sb(tag + "rstd", [128, T], F32)
        cq = S.sb(tag + "cq", [128, 4, T], F32)
        cqn = S.sb(tag + "cqn", [128, 4, T], BF16)
        qm = S.sb(tag + "qm", [128, 2, T], BF16)
        cs_t = S.sb(tag + "cos", [128, T], F32)
        sn_t = S.sb(tag + "sin", [128, T], F32)
        tmpa = S.sb(tag + "tmpa", [128, T], F32)
        tmpb = S.sb(tag + "tmpb", [128, T], F32)
        qn = [S.sb(tag + f"qn{i}", [128, T], BF16) for i in range(2)]
        qr = [S.sb(tag + f"qr{i}", [128, 2, T], BF16) for i in range(2)]
        pT = [S.sb(tag + f"pT{i}", [128, T], BF16) for i in range(2)]
        osb = S.sb(tag + "osb", [65, T], F32)
        rec = S.sb(tag + "rec", [64, T], F32)
        mo = [S.sb(tag + f"mo{i}", [64, T], BF16) for i in range(2)]
        xin_v = x_in.h.rearrange("(k p) t -> p k t", p=128)
        for it in range(ntok // T):
            ts = slice(it * T, (it + 1) * T)
            x = xs[it % 2]
            S.dma(x[:], x_in.ap(xin_v[:, :, ts], key=it), q="sp")
            S.dma(cs_t[:], tabs["cosq"][:, ts], q="sp")
            S.dma(sn_t[:], tabs["sinq"][:, ts], q="sp")
            rms_rstd(S, C, x, P[0], sq, rstd, KD, C["ones1024"])
            norm_apply(S, h, x, g1, rstd, KD)
            for c in range(6):
                pm = P[1 + c % 2]
                for kc in range(KD):
                    S.mm(pm[:], win.p(kc)[:, kc, c * 128:(c + 1) * 128], h.p(kc)[:, kc, :], start=(kc == 0), stop=(kc == KD - 1))
                if c < 4:
                    S.op("act", "activation", out=cq[:, c, :], in_=pm[:], func=AF.Copy)
                else:
                    S.op("act", "activation", out=qm[:, c - 4, :], in_=pm[:], func=AF.Copy)
            rms_rstd(S, C, cq, P[0], sq, rstd, 4, C["ones512"])
            norm_apply(S, cqn, cq, qg, rstd, 4)
            for a in range(6):
                pm = P[1 + a % 2]
                for kc in range(4):
                    S.mm(pm[:], qn_w.p(kc)[:, kc, a * 128:(a + 1) * 128], cqn.p(kc)[:, kc, :], start=(kc == 0), stop=(kc == 3))
                q = qn[a % 2]
                S.op("act", "activation", out=q[:], in_=pm[:], func=AF.Copy, scale=SCALE_MLA)
                S.dma(QD.ap(QD.h[2 * a, 0:64, ts], key=it), q[0:64, :], q="pool")
                S.dma(QD.ap(QD.h[2 * a + 1, 0:64, ts], key=it), q[64:128, :], q="pool")
            for grp, (h0, nh) in enumerate(((0, 8), (8, 4))):
                np_ = nh * 16
                for kc in range(4):
                    S.mm(P[3][0:np_, :], qr1_w.p(kc)[:, kc, h0 * 16:(h0 + nh) * 16], cqn.p(kc)[:, kc, :], start=(kc == 0), stop=(kc == 3))
                for kc in range(4):
                    S.mm(P[4][0:np_, :], qr2_w.p(kc)[:, kc, h0 * 16:(h0 + nh) * 16], cqn.p(kc)[:, kc, :], start=(kc == 0), stop=(kc == 3))
                r = qr[grp]
                rope_apply(S, P[3][0:np_, :], P[4][0:np_, :], cs_t[0:np_, :], sn_t[0:np_, :], r[0:np_, 0, :], r[0:np_, 1, :], tmpa, tmpb, np_)
                for hh in range(nh):
                    S.dma(QD.ap(QD.h[h0 + hh, 64:80, ts], key=it), r[hh * 16:(hh + 1) * 16, 0, :], q="pool")
                    S.dma(QD.ap(QD.h[h0 + hh, 80:96, ts], key=it), r[hh * 16:(hh + 1) * 16, 1, :], q="pool")
            mem_attn(S, C, qm, MK, MV, MD, ts, it, (pT, osb, rec, mo))


def attn_phase(S, C, QD, KN, KR, VD, OD, ntok, heads=range(12)):
    P = C["P"]
    NKT = ntok // 128
    with S.scope():
        Ks = [S.sb(f"at_K{i}", [96, ntok], BF16) for i in range(2)]
        Qs = [S.sb(f"at_Q{i}", [96, ntok], BF16) for i in range(2)]
        Vs = [S.sb(f"at_V{i}", [128, NKT, 65], BF16) for i in range(2)]
        pT2 = [S.sb(f"at_pT{i}", [128, 2 * T], BF16) for i in range(2)]
        osb = S.sb("at_osb", [65, T], F32)
        rec = S.sb("at_rec", [64, T], F32)
        oo = [S.sb(f"at_oo{i}", [64, T], BF16) for i in range(2)]
        vd_v = VD.h.rearrange("n p h c -> p n h c")
        cnt = 0
        for ih, hd in enumerate(heads):
            K, Q, Vv = Ks[ih % 2], Qs[ih % 2], Vs[ih % 2]
            S.dma(K.p("n")[0:64, :], KN[hd, :, :], q="sp")
            S.dma(K.p("r")[64:96, :], KR[:, :], q="sp")
            S.dma(Q[:], QD[hd, :, :], q="sp")
            S.dma(Vv[:], V(VD, None, vd_v[:, :, hd, :]), q="sp")
            for qi in range(ntok // T):
                po = P[4 + qi % 2]
                nk = 4 * qi + 4
                nfull = 4 * qi
                for kp in range(nfull // 2):
                    pp = C["PP"][cnt % 2]
                    p2 = pT2[cnt % 2]
                    cnt += 1
                    for u in range(2):
                        kj = 2 * kp + u
                        S.mm(pp[:, u * T:(u + 1) * T], K[:, kj * 128:(kj + 1) * 128], Q[:, qi * T:(qi + 1) * T])
                    S.op("act", "activation", out=p2[:], in_=pp[:], func=AF.Exp)
                    for u in range(2):
                        kj = 2 * kp + u
                        S.mm(po[0:65, :], Vv[:, kj, :], p2[:, u * T:(u + 1) * T], start=(kj == 0), stop=False)
                for kj in range(nfull, nk):
                    d = kj - 4 * qi
                    c0 = 128 * d
                    pp = C["PP"][cnt % 2]
                    p2 = pT2[cnt % 2]
                    cnt += 1
                    S.mm(pp[:, c0:T], K[:, kj * 128:(kj + 1) * 128], Q[:, qi * T + c0:(qi + 1) * T])
                    S.op("act", "activation", out=p2[:, c0:T], in_=pp[:, c0:T], func=AF.Exp)
                    S.op("dve", "tensor_tensor", out=p2[:, c0:c0 + 128], in0=p2[:, c0:c0 + 128], in1=C["tri"][:], op=ALU.mult)
                    S.mm(po[0:65, c0:T], Vv[:, kj, :], p2[:, c0:T], start=(kj == 0), stop=(kj == nk - 1))
                o = oo[qi % 2]
                attn_finish(S, C, po, osb, P[6], rec, o[:])
                S.dma(OD.ap(OD.h[hd, :, qi * T:(qi + 1) * T], key=(hd, qi)), o[:], q="pool")


EXPM05 = math.exp(-0.5)
LNX_EPS = 64e-5
CH = 64
NCH = T // CH


def rwkv_layer(S, C, x_in, x_mid, Wd, i, ntok, memn, st):
    tag = f"r{i}_"
    SD = {k: S.dram(tag + k, [768, ntok], F32) for k in ("r", "sg", "k2", "v", "kk", "b")}
    GL = S.dram(tag + "gl", [128, ntok], BF16)
    MD = S.dram(tag + "MD", [4, 64, ntok], BF16)
    YD = S.dram(tag + "YD", [12, 64, ntok], BF16)
    if i == 0:
        st["VF"] = SD["v"]
    rwkv_proj_phase(S, C, x_in, Wd, i, ntok, memn, SD, GL, MD, st, tag)
    rwkv_scan_phase(S, C, Wd, i, ntok, SD, GL, YD, tag)
    outproj_phase(S, C, x_in, x_mid, YD, MD, Wd["a_w_out"][i], ntok, tag=tag + "o")


def rwkv_proj_phase(S, C, x_in, Wd, i, ntok, memn, SD, GL, MD, st, tag):
    P = C["P"]
    with S.scope():
        MK = S.sb(tag + "MK", [128, 2, NMEM], BF16)
        MV = S.sb(tag + "MV", [128, 2, 4, 65], BF16)
        mem_kv(S, C, memn, Wd["a_mem_kv"][i], MK, MV, tag)
        W1 = S.sb(tag + "W1", [128, KD, 2560], BF16)
        W2 = S.sb(tag + "W2", [128, KD, 2560], BF16)
        wq = S.sb(tag + "wq", [128, KD, 256], BF16)
        for kc in range(KD):
            S.dma(wq.p(kc)[:, kc, :], Wd["a_w_in"][i, kc * 128:(kc + 1) * 128, 2560:2816], q="pool")
        with S.scope():
            mu_bc = S.sb(tag + "mu_bc", [128, 2560], F32)
            omm_bc = S.sb(tag + "omm_bc", [128, 2560], F32)
            wf = [S.sb(tag + f"wf{k}", [128, 2560], F32) for k in range(2)]
            S.dma(mu_bc[:], V(Wd["a_shift_mu"].t, None, Wd["a_shift_mu"].ap[i].partition_broadcast(128)), q="sp")
            S.op("dve", "tensor_scalar", out=omm_bc[:], in0=mu_bc[:], scalar1=-1.0, scalar2=1.0, op0=ALU.mult, op1=ALU.add)
            for kc in range(KD):
                w = wf[kc % 2]
                S.dma(w[:], Wd["a_w_in"][i, kc * 128:(kc + 1) * 128, 0:2560], q="sp")
                S.op("dve", "tensor_tensor", out=W1.p(kc)[:, kc, :], in0=w[:], in1=omm_bc[:], op=ALU.mult)
                S.op("dve", "tensor_tensor", out=W2.p(kc)[:, kc, :], in0=w[:], in1=mu_bc[:], op=ALU.mult)
        dup = S.sb(tag + "dup", [128, 768], BF16)
        S.dma(dup.p("d")[0:64, :], Wd["a_decay_up"][i], q="pool")
        S.dma(dup.p("a")[64:128, :], Wd["a_aaa_up"][i], q="pool")
        vecs = S.sb(tag + "vecs", [128, 8, 6], F32)
        for vi, nm in enumerate(("a_decay_bias", "a_aaa_bias", "a_k_k", "a_k_a")):
            S.dma(vecs.p(vi)[:, vi, :], dview(Wd[nm][i], "(k p) -> p k", p=128), q="sp", allow_slow_non_contiguous=True)
        S.op("dve", "tensor_scalar", out=vecs.p(4)[:, 4, :], in0=vecs.p(3)[:, 3, :], scalar1=-1.0, scalar2=1.0, op0=ALU.mult, op1=ALU.add)
        g1 = S.sb(tag + "g1", [128, KD], F32)
        load_vec(S, g1[:], Wd["a_norm1_g"][i])
        if i > 0:
            S.dma(vecs.p(5)[:, 5, :], dview(Wd["vres_bias"][i - 1], "(k p) -> p k", p=128), q="sp", allow_slow_non_contiguous=True)
            vmu = S.sb(tag + "vmu", [128, KD, 2], F32)
            S.dma(vmu.p(0)[:, :, 0], dview(Wd["vres_mu"][i - 1], "(k p) -> p k", p=128), q="sp", allow_slow_non_contiguous=True)
            S.op("dve", "tensor_scalar", out=vmu.p(1)[:, :, 1], in0=vmu.p(0)[:, :, 0], scalar1=-1.0, scalar2=1.0, op0=ALU.mult, op1=ALU.add)
            vdf = S.sb(tag + "vdf", [128, KD, 32], F32)
            S.dma(vdf[:], dview(Wd["vres_down"][i - 1], "(k p) n -> p k n", p=128), q="sp")
            vd1 = S.sb(tag + "vd1", [128, KD, 32], BF16)
            vd2 = S.sb(tag + "vd2", [128, KD, 32], BF16)
            for kc in range(KD):
                S.op("dve", "tensor_scalar", out=vd1.p(kc)[:, kc, :], in0=vdf[:, kc, :], scalar1=vmu.p(1)[:, kc, 1:2], scalar2=None, op0=ALU.mult)
                S.op("dve", "tensor_scalar", out=vd2.p(kc)[:, kc, :], in0=vdf[:, kc, :], scalar1=vmu.p(0)[:, kc, 0:1], scalar2=None, op0=ALU.mult)
            vup = S.sb(tag + "vup", [32, 768], BF16)
            S.dma(vup[:], Wd["vres_up"][i - 1], q="pool")
            vlo = S.sb(tag + "vlo", [32, T], BF16)
        xs = [S.sb(tag + f"x{k}", [128, KD, T], F32) for k in range(2)]
        h = S.sb(tag + "h", [128, KD, T + 1], BF16)
        S.op("pool", "memset", ap=h[:], constant=0.0, extra_writes=[h[:]])
        sq = S.sb(tag + "sq", [128, T], BF16)
        rstd = S.sb(tag + "rstd", [128, T], F32)
        lo = S.sb(tag + "lo", [128, T], BF16)
        gl = S.sb(tag + "gl", [128, T], BF16)
        qm = S.sb(tag + "qm", [128, 2, T], BF16)
        pT = [S.sb(tag + f"pT{k}", [128, T], BF16) for k in range(2)]
        osb = S.sb(tag + "osb", [65, T], F32)
        rec = S.sb(tag + "rec", [64, T], F32)
        mo = [S.sb(tag + f"mo{k}", [64, T], BF16) for k in range(2)]
        names = ("r", "k", "v", "sg", "lr", "kks", "kk", "k2", "b", "t1", "vf")
        tb = {nm: [S.sb(tag + f"t_{nm}{k}", [128, T], F32) for k in range(2)] for nm in names}
        xin_v = x_in.h.rearrange("(k p) t -> p k t", p=128)
        bones = C["bones"]

        def proj_chunk(pm, c):
            for kc in range(KD):
                S.mm(pm[:], W1.p(kc)[:, kc, c * 128:(c + 1) * 128], h.p(kc)[:, kc, 1:T + 1], start=(kc == 0), stop=False)
                S.mm(pm[:], W2.p(kc)[:, kc, c * 128:(c + 1) * 128], h.p(kc)[:, kc, 0:T], start=False, stop=(kc == KD - 1))

        for it in range(ntok // T):
            ts = slice(it * T, (it + 1) * T)
            x = xs[it % 2]
            S.dma(x[:], x_in.ap(xin_v[:, :, ts], key=it), q="sp")
            if it > 0:
                S.op("dve", "tensor_copy", out=h[:, :, 0], in_=h[:, :, T])
            rms_rstd(S, C, x, P[0], sq, rstd, KD, C["ones1024"])
            for kc in range(KD):
                S.op("dve", "scalar_tensor_tensor", out=h.p(kc)[:, kc, 1:T + 1], in0=x[:, kc, :], scalar=g1[:, kc:kc + 1],
                     in1=rstd[:], op0=ALU.mult, op1=ALU.mult)
            proj_chunk(P[1], 18)
            S.op("act", "activation", out=lo.p("d")[0:64, :], in_=P[1][0:64, :], func=AF.Tanh)
            S.op("act", "activation", out=lo.p("a")[64:128, :], in_=P[1][64:128, :], func=AF.Copy)
            proj_chunk(P[2], 19)
            S.op("act", "activation", out=gl[:], in_=P[2][:], func=AF.Sigmoid)
            S.dma(GL[:, ts], gl[:], q="pool")
            for c in range(2):
                pm = P[1 + c]
                for kc in range(KD):
                    S.mm(pm[:], wq.p(kc)[:, kc, c * 128:(c + 1) * 128], h.p(kc)[:, kc, 1:T + 1], start=(kc == 0), stop=(kc == KD - 1))
                S.op("act", "activation", out=qm[:, c, :], in_=pm[:], func=AF.Copy)
            mem_attn(S, C, qm, MK, MV, MD, ts, it, (pT, osb, rec, mo))
            if i > 0:
                for kc in range(KD):
                    S.mm(P[7][0:32, :], vd1.p(kc)[:, kc, :], h.p(kc)[:, kc, 1:T + 1], start=(kc == 0), stop=False)
                    S.mm(P[7][0:32, :], vd2.p(kc)[:, kc, :], h.p(kc)[:, kc, 0:T], start=False, stop=(kc == KD - 1))
                S.op("act", "activation", out=vlo[:], in_=P[7][0:32, :], func=AF.Copy)
            for c in range(6):
                b_ = c % 2
                t = {nm: tb[nm][b_] for nm in names}
                for j, nm in enumerate(("r", "k", "v")):
                    pm = P[1 + j]
                    proj_chunk(pm, j * 6 + c)
                    S.op("act", "activation", out=t[nm][:], in_=pm[:], func=AF.Copy)
                S.mm(P[4][:], dup.p("d")[0:64, c * 128:(c + 1) * 128], lo.p("d")[0:64, :])
                S.op("act", "activation", out=t["sg"][:], in_=P[4][:], func=AF.Sigmoid, bias=vecs.p(0)[:, 0, c:c + 1], scale=1.0)
                S.mm(P[5][:], dup.p("a")[64:128, c * 128:(c + 1) * 128], lo.p("a")[64:128, :])
                S.op("act", "activation", out=t["lr"][:], in_=P[5][:], func=AF.Sigmoid, bias=vecs.p(1)[:, 1, c:c + 1], scale=1.0)
                if i > 0:
                    S.mm(P[6][:], vup[:, c * 128:(c + 1) * 128], vlo[:])
                    S.op("act", "activation", out=t["t1"][:], in_=P[6][:], func=AF.Sigmoid, bias=vecs.p(5)[:, 5, c:c + 1], scale=1.0)
                    S.dma(t["vf"][:], st["VF"][c * 128:(c + 1) * 128, ts], q="sp")
                    S.op("dve", "tensor_tensor", out=t["vf"][:], in0=t["vf"][:], in1=t["v"][:], op=ALU.subtract)
                    S.op("dve", "tensor_tensor", out=t["vf"][:], in0=t["vf"][:], in1=t["t1"][:], op=ALU.mult)
                    S.op("dve", "tensor_tensor", out=t["v"][:], in0=t["v"][:], in1=t["vf"][:], op=ALU.add)
                S.op("dve", "tensor_scalar", out=t["kks"][:], in0=t["k"][:], scalar1=vecs.p(2)[:, 2, c:c + 1], scalar2=None, op0=ALU.mult)
                S.op("act", "activation", out=t["t1"][:], in_=t["kks"][:], func=AF.Square)
                S.mm(P[6][:], bones[:], t["t1"][:])
                S.op("act", "activation", out=t["t1"][:], in_=P[6][:], func=AF.Sqrt)
                S.op("dve", "tensor_scalar", out=t["t1"][:], in0=t["t1"][:], scalar1=1e-12, scalar2=None, op0=ALU.max)
                S.op("dve", "reciprocal", out=t["t1"][:], in_=t["t1"][:])
                S.op("dve", "tensor_tensor", out=t["kk"][:], in0=t["kks"][:], in1=t["t1"][:], op=ALU.mult)
                S.op("dve", "tensor_scalar", out=t["t1"][:], in0=t["lr"][:], scalar1=vecs.p(3)[:, 3, c:c + 1], scalar2=vecs.p(4)[:, 4, c:c + 1],
                     op0=ALU.mult, op1=ALU.add)
                S.op("dve", "tensor_tensor", out=t["k2"][:], in0=t["k"][:], in1=t["t1"][:], op=ALU.mult)
                S.op("dve", "tensor_tensor", out=t["b"][:], in0=t["kk"][:], in1=t["lr"][:], op=ALU.mult)
                for nm in ("r", "sg", "k2", "v", "kk", "b"):
                    S.dma(SD[nm][c * 128:(c + 1) * 128, ts], t[nm][:], q="pool")


def c3(v):
    return dview(v, "p (c j) -> p c j", j=CH)


def bcast_mid(v, n):
    p, j = v.ap.shape
    return V(v.t, v.k, v.ap.unsqueeze(1).broadcast_to([p, n, j]))


def bcast_last(v, j):
    p, n = v.ap.shape
    return V(v.t, v.k, v.ap.unsqueeze(2).broadcast_to([p, n, j]))


def to_bd(S, bd, src3, engs=("act", "dve")):
    for hh in range(2):
        ps = slice(hh * 64, hh * 64 + 64)
        if engs[hh] == "act":
            S.op("act", "activation", out=bd[ps, :, hh * 64:hh * 64 + 64], in_=src3[ps, :, :], func=AF.Copy)
        else:
            S.op("dve", "tensor_copy", out=bd[ps, :, hh * 64:hh * 64 + 64], in_=src3[ps, :, :])


def rwkv_scan_phase(S, C, Wd, i, ntok, SD, GL, YD, tag):
    P = C["P"]
    tag = tag + "s"
    ident = C["ident"]
    with S.scope():
        cm = S.sb(tag + "cm", [128, 4, CH], F32)
        S.dma(cm[:], Wd["cmask"], q="sp")
        mXs, mYs, mYi, Ist = (cm[:, k, :] for k in range(4))
        ones_col = S.sb(tag + "ones_col", [128, 1], F32)
        S.op("pool", "memset", ap=ones_col[:], constant=1.0, extra_writes=[ones_col[:]])
        epsl = S.sb(tag + "epsl", [128, 1], F32)
        S.op("pool", "memset", ap=epsl[:], constant=LNX_EPS, extra_writes=[epsl[:]])
        rkv = S.sb(tag + "rkv", [128, 6], F32)
        S.dma(rkv[:], V(Wd["a_r_k"].t, None, Wd["a_r_k"].ap[i].rearrange("h k -> (h k)").rearrange("(c p) -> p c", p=128)),
              q="sp", allow_slow_non_contiguous=True)
        gup = S.sb(tag + "gup", [128, 768], BF16)
        S.dma(gup[:], Wd["a_gate_up"][i], q="pool")
        lng = S.sb(tag + "lng", [128, 6, 64], F32)
        lnb = S.sb(tag + "lnb", [128, 6, 64], F32)
        for nm, dst in (("a_lnx_g", lng), ("a_lnx_b", lnb)):
            src = Wd[nm].ap[i].rearrange("(c h j) -> c h j", h=2, j=64)
            for hh in range(2):
                for c in range(6):
                    S.dma(dst.p((c, hh))[hh * 64:hh * 64 + 64, c, :],
                          V(Wd[nm].t, None, src[c, hh].partition_broadcast(64)), q="sp")
        names_ld = ("r", "sg", "k2", "v", "kk", "b")
        ld = {nm: [S.sb(tag + f"l_{nm}{k}", [128, T], F32) for k in range(2)] for nm in names_ld}
        gl = [S.sb(tag + f"gl{k}", [128, T], BF16) for k in range(2)]
        fnames = ("lw", "csA", "csB", "Ein", "Eex", "Eneg", "Eend", "tmp", "rt", "at", "bt")
        fb = [{nm: S.sb(tag + f"f_{nm}{k}", [128, T], F32) for nm in fnames} for k in range(2)]
        bdn = ("at", "bt", "kt", "bh", "kh", "v", "u", "atT", "bhT", "khT", "X", "Y", "AKT", "RBT", "RKT", "AhT", "TT")
        bd = {nm: S.sb(tag + "bd_" + nm, [128, NCH, 128], F32) for nm in bdn}
        bd_rt = [S.sb(tag + f"bd_rt{k}", [128, NCH, 128], F32) for k in range(2)]
        for tl in list(bd.values()) + bd_rt:
            S.op("pool", "memset", ap=tl[:], constant=0.0, extra_writes=[tl[:]])
        s3 = {nm: S.sb(tag + "s3_" + nm, [128, NCH, CH], F32) for nm in
              ("Vtm", "Xs", "Ys", "ATs", "AVs", "U8", "y8", "yc", "ysq", "yt")}
        st2 = {nm: S.sb(tag + "st_" + nm, [128, NCH], F32) for nm in ("mean", "var", "rk")}
        Sst = [S.sb(tag + f"S{k}", [128, CH], F32) for k in range(2)]
        g_sb = S.sb(tag + "g_sb", [64, 2, T], F32)
        ym = [S.sb(tag + f"ym{k}", [64, T], BF16) for k in range(2)]
        NT = ntok // T
        tiles = [(c, it) for c in range(6) for it in range(NT)]

        def prep(gi):
            c, it = tiles[gi]
            k = gi % 2
            ts = slice(it * T, (it + 1) * T)
            L = {nm: ld[nm][k] for nm in names_ld}
            f = fb[k]
            for nm in names_ld:
                S.dma(L[nm][:], SD[nm][c * 128:(c + 1) * 128, ts], q="sp")
            S.dma(gl[k][:], GL[:, ts], q="sp")
            yield
            S.op("dve", "tensor_scalar", out=f["lw"][:], in0=L["sg"][:], scalar1=-EXPM05, scalar2=None, op0=ALU.mult)
            src, dst = f["lw"], f["csA"]
            for s_ in (1, 2, 4, 8, 16, 32):
                S.op("act", "activation", out=c3(dst[:])[:, :, 0:s_], in_=c3(src[:])[:, :, 0:s_], func=AF.Copy)
                S.op("dve", "tensor_tensor", out=c3(dst[:])[:, :, s_:CH], in0=c3(src[:])[:, :, s_:CH], in1=c3(src[:])[:, :, 0:CH - s_], op=ALU.add)
                src = dst
                dst = f["csB"] if dst is f["csA"] else f["csA"]
                yield
            cs = src
            S.op("act", "activation", out=f["Ein"][:], in_=cs[:], func=AF.Exp)
            S.op("act", "activation", out=f["Eneg"][:], in_=cs[:], func=AF.Exp, scale=-1.0)
            S.op("dve", "tensor_tensor", out=f["tmp"][:], in0=cs[:], in1=f["lw"][:], op=ALU.subtract)
            S.op("act", "activation", out=f["Eex"][:], in_=f["tmp"][:], func=AF.Exp)
            yield
            S.op("dve", "tensor_tensor", out=c3(f["tmp"][:]), in0=c3(cs[:]), in1=bcast_last(c3(cs[:])[:, :, CH - 1], CH), op=ALU.subtract)
            S.op("act", "activation", out=f["Eend"][:], in_=f["tmp"][:], func=AF.Exp, scale=-1.0)
            S.op("dve", "tensor_tensor", out=f["rt"][:], in0=L["r"][:], in1=f["Ein"][:], op=ALU.mult)
            S.op("dve", "scalar_tensor_tensor", out=f["at"][:], in0=L["kk"][:], scalar=-1.0, in1=f["Eex"][:], op0=ALU.mult, op1=ALU.mult)
            S.op("dve", "tensor_tensor", out=f["bt"][:], in0=L["b"][:], in1=f["Eneg"][:], op=ALU.mult)
            yield
            to_bd(S, bd_rt[k], c3(f["rt"][:]))
            to_bd(S, bd["at"], c3(f["at"][:]))
            yield
            to_bd(S, bd["bt"], c3(f["bt"][:]))
            to_bd(S, bd["v"], c3(L["v"][:]))
            yield
            for hh in range(2):
                ps = slice(hh * 64, hh * 64 + 64)
                cs_ = slice(hh * 64, hh * 64 + 64)
                S.op("dve", "tensor_tensor", out=bd["kt"][ps, :, cs_], in0=c3(L["k2"][:])[ps], in1=c3(f["Eneg"][:])[ps], op=ALU.mult)
                S.op("dve", "tensor_tensor", out=bd["bh"][ps, :, cs_], in0=c3(L["b"][:])[ps], in1=c3(f["Eend"][:])[ps], op=ALU.mult)
                yield
                S.op("dve", "tensor_tensor", out=bd["kh"][ps, :, cs_], in0=c3(L["k2"][:])[ps], in1=c3(f["Eend"][:])[ps], op=ALU.mult)
                S.op("dve", "scalar_tensor_tensor", out=bd["u"][ps, :, cs_], in0=c3(L["r"][:])[ps], scalar=rkv[ps, c:c + 1],
                     in1=c3(L["k2"][:])[ps], op0=ALU.mult, op1=ALU.mult)
                yield

        def drain(g):
            if g is not None:
                for _ in g:
                    pass

        def step(g):
            if g is not None:
                try:
                    next(g)
                except StopIteration:
                    pass

        sidx = 0
        drain(prep(0))
        for gi, (c, it) in enumerate(tiles):
            k = gi % 2
            ts = slice(it * T, (it + 1) * T)
            f = fb[k]
            glt = gl[k]
            rtbd = bd_rt[k]
            if it == 0:
                S.op("dve", "memset", ap=Sst[sidx][:], constant=0.0, extra_writes=[Sst[sidx][:]])
            Wc = c3(f["Ein"][:])[:, :, CH - 1]
            for nm_s, nm_d in (("at", "atT"), ("bh", "bhT"), ("kh", "khT")):
                for half in range(2):
                    pt = P[half]
                    for q4 in range(4):
                        ch = half * 4 + q4
                        S.transpose(pt[:, q4 * 128:(q4 + 1) * 128], bd[nm_s][:, ch, :], ident[:])
                    S.op("act" if half == 0 else "dve", "activation" if half == 0 else "tensor_copy",
                         out=bd[nm_d][:, half * 4:half * 4 + 4, :], in_=dview(pt[:], "p (c j) -> p c j", j=128),
                         **({"func": AF.Copy} if half == 0 else {}))
            for half in range(2):
                pt = P[half]
                for q4 in range(4):
                    ch = half * 4 + q4
                    S.transpose(pt[:, q4 * 128:(q4 + 1) * 128], bd["v"][:, ch, :], ident[:])
                p3 = dview(pt[:], "p (c j) -> p c j", j=128)
                S.op("act", "activation", out=s3["Vtm"][:, half * 4:half * 4 + 4, :], in_=p3[:, :, 0:64], func=AF.Copy)
                S.op("dve", "tensor_tensor", out=s3["Vtm"][:, half * 4:half * 4 + 4, :], in0=s3["Vtm"][:, half * 4:half * 4 + 4, :],
                     in1=p3[:, :, 64:128], op=ALU.add)

            def amat(pt, lhs_bd, rhs_f):
                for ch in range(NCH):
                    S.mm(pt[:, ch * CH:(ch + 1) * CH], bd[lhs_bd][:, ch, :], f[rhs_f][:, ch * CH:(ch + 1) * CH])
                return c3(pt[:])
            px = amat(P[0], "at", "bt")
            py = amat(P[1], "bt", "at")
            pk = amat(P[2], "kt", "at")
            S.op("dve", "tensor_tensor", out=s3["Xs"][:], in0=px, in1=bcast_mid(mXs, NCH), op=ALU.mult)
            to_bd(S, bd["X"], s3["Xs"][:])
            S.op("dve", "tensor_tensor", out=s3["Ys"][:], in0=py, in1=bcast_mid(mYs, NCH), op=ALU.mult)
            to_bd(S, bd["Y"], s3["Ys"][:])
            for ch in range(NCH):
                S.mm(P[0][:, ch * CH:(ch + 1) * CH], bd["Y"][:, ch, :], s3["Xs"][:, ch, :])
                S.mm(P[1][:, ch * CH:(ch + 1) * CH], bd["X"][:, ch, :], s3["Ys"][:, ch, :])
            S.op("dve", "tensor_tensor", out=s3["ATs"][:], in0=s3["Ys"][:], in1=bcast_mid(Ist, NCH), op=ALU.add)
            S.op("dve", "tensor_tensor", out=s3["yt"][:], in0=pk, in1=bcast_mid(mYs, NCH), op=ALU.mult)
            to_bd(S, bd["AKT"], s3["yt"][:])
            S.op("act", "activation", out=s3["Xs"][:], in_=c3(P[0][:]), func=AF.Copy)
            S.op("dve", "tensor_copy", out=s3["Ys"][:], in_=c3(P[1][:]))
            to_bd(S, bd["X"], s3["Xs"][:])
            to_bd(S, bd["Y"], s3["Ys"][:])
            for stg in range(1, 6):
                for ch in range(NCH):
                    S.mm(P[2][:, ch * CH:(ch + 1) * CH], bd["X"][:, ch, :], s3["ATs"][:, ch, :])
                if stg < 5:
                    for ch in range(NCH):
                        S.mm(P[0][:, ch * CH:(ch + 1) * CH], bd["Y"][:, ch, :], s3["Xs"][:, ch, :])
                    if stg < 4:
                        for ch in range(NCH):
                            S.mm(P[1][:, ch * CH:(ch + 1) * CH], bd["X"][:, ch, :], s3["Ys"][:, ch, :])
                S.op("dve", "tensor_tensor", out=s3["ATs"][:], in0=s3["ATs"][:], in1=c3(P[2][:]), op=ALU.add)
                if stg < 5:
                    if stg < 4:
                        S.op("act", "activation", out=s3["Xs"][:], in_=c3(P[0][:]), func=AF.Copy)
                        S.op("dve", "tensor_copy", out=s3["Ys"][:], in_=c3(P[1][:]))
                        to_bd(S, bd["X"], s3["Xs"][:])
                        to_bd(S, bd["Y"], s3["Ys"][:])
                    else:
                        to_bd(S, bd["X"], c3(P[0][:]))
                if stg == 1:
                    prb = amat(P[3], "bt", "rt")
                    S.op("dve", "tensor_tensor", out=s3["yc"][:], in0=prb, in1=bcast_mid(mYi, NCH), op=ALU.mult)
                    to_bd(S, bd["RBT"], s3["yc"][:])
                if stg == 2:
                    prk = amat(P[3], "kt", "rt")
                    S.op("dve", "tensor_tensor", out=s3["ysq"][:], in0=prk, in1=bcast_mid(mYi, NCH), op=ALU.mult)
                    to_bd(S, bd["RKT"], s3["ysq"][:])
                if stg == 3:
                    for ch in range(NCH):
                        S.mm(P[3][:, ch * CH:(ch + 1) * CH], bd["AKT"][:, ch, :], s3["Vtm"][:, ch, :])
                    S.op("act", "activation", out=s3["AVs"][:], in_=c3(P[3][:]), func=AF.Copy)
                if stg == 4:
                    for ch in range(NCH):
                        S.mm(P[3][:, ch:ch + 1], bd["u"][:, ch, :], ones_col[:])
                    S.op("dve", "tensor_copy", out=st2["rk"][:], in_=P[3][:, 0:NCH])
                    for hh in range(2):
                        S.mm(P[7][0:64, :], gup[:, (2 * c + hh) * 64:(2 * c + hh + 1) * 64], glt[:])
                        S.op("act", "activation", out=g_sb[:, hh, :], in_=P[7][0:64, :], func=AF.Copy)
            to_bd(S, bd["TT"], s3["ATs"][:])
            for ch in range(NCH):
                S.mm(P[1][:, ch * CH:(ch + 1) * CH], bd["atT"][:, ch, :], s3["ATs"][:, ch, :])
            to_bd(S, bd["AhT"], c3(P[1][:]))
            nxt = prep(gi + 1) if gi + 1 < len(tiles) else None
            step(nxt)
            for ch in range(NCH):
                S0 = Sst[sidx]
                S1 = Sst[1 - sidx]
                cc = slice(ch * CH, (ch + 1) * CH)
                S.mm(P[4].p(ch)[:, cc], bd["AhT"][:, ch, :], S0[:], start=True, stop=False)
                S.mm(P[4].p(ch)[:, cc], bd["TT"][:, ch, :], s3["AVs"][:, ch, :], start=False, stop=True)
                S.op("act", "activation", out=s3["U8"].p(ch)[:, ch, :], in_=P[4].p(ch)[:, cc], func=AF.Copy)
                S.mm(P[6].p(ch)[:, cc], bd["bhT"][:, ch, :], s3["U8"].p(ch)[:, ch, :], start=True, stop=False)
                S.mm(P[6].p(ch)[:, cc], bd["khT"][:, ch, :], s3["Vtm"][:, ch, :], start=False, stop=True)
                S.op("dve", "scalar_tensor_tensor", out=S1[:], in0=S0[:], scalar=Wc[:, ch:ch + 1], in1=P[6].p(ch)[:, cc],
                     op0=ALU.mult, op1=ALU.add)
                S.mm(P[5].p(ch)[:, cc], rtbd[:, ch, :], S0[:], start=True, stop=False)
                S.mm(P[5].p(ch)[:, cc], bd["RBT"][:, ch, :], s3["U8"].p(ch)[:, ch, :], start=False, stop=False)
                S.mm(P[5].p(ch)[:, cc], bd["RKT"][:, ch, :], s3["Vtm"][:, ch, :], start=False, stop=True)
                sidx = 1 - sidx
                step(nxt)
                step(nxt)
            y8, yc, ysq, yt = s3["y8"], s3["yc"], s3["ysq"], s3["yt"]
            S.op("act", "activation", out=y8[:], in_=c3(P[5][:]), func=AF.Copy)
            S.op("dve", "tensor_reduce", out=st2["mean"][:], in_=y8[:], axis=AX.X, op=ALU.add)
            S.op("dve", "tensor_scalar", out=st2["mean"][:], in0=st2["mean"][:], scalar1=1.0 / CH, scalar2=None, op0=ALU.mult)
            S.op("dve", "tensor_tensor", out=yc[:], in0=y8[:], in1=bcast_last(st2["mean"][:], CH), op=ALU.subtract)
            S.op("act", "activation", out=ysq[:], in_=yc[:], func=AF.Square)
            S.op("dve", "tensor_reduce", out=st2["var"][:], in_=ysq[:], axis=AX.X, op=ALU.add)
            S.op("act", "activation", out=st2["var"][:], in_=st2["var"][:], func=AF.Sqrt, bias=epsl[:, 0:1], scale=1.0 / CH)
            S.op("dve", "reciprocal", out=st2["var"][:], in_=st2["var"][:])
            S.op("dve", "tensor_tensor", out=yc[:], in0=yc[:], in1=bcast_last(st2["var"][:], CH), op=ALU.mult)
            S.op("dve", "tensor_tensor", out=yc[:], in0=yc[:], in1=bcast_mid(lng[:, c, :], NCH), op=ALU.mult)
            S.op("dve", "tensor_tensor", out=yc[:], in0=yc[:], in1=bcast_mid(lnb[:, c, :], NCH), op=ALU.add)
            S.op("dve", "tensor_tensor", out=yt[:], in0=s3["Vtm"][:], in1=bcast_last(st2["rk"][:], CH), op=ALU.mult)
            S.op("dve", "tensor_tensor", out=yc[:], in0=yc[:], in1=yt[:], op=ALU.add)
            drain(nxt)
            for half in range(2):
                pt = P[half]
                for q4 in range(4):
                    ch = half * 4 + q4
                    S.transpose(pt[0:64, q4 * 128:(q4 + 1) * 128], yc[:, ch, :], ident[:])
                p4 = dview(pt[0:64, :], "p (c h j) -> p c h j", h=2, j=CH)
                for hh in range(2):
                    S.op("dve", "tensor_tensor", out=dview(ym[hh][:, half * 256:(half + 1) * 256], "p (c j) -> p c j", j=CH),
                         in0=p4[:, :, hh, :], in1=dview(g_sb[:, hh, half * 256:(half + 1) * 256], "p (c j) -> p c j", j=CH), op=ALU.mult)
            for hh in range(2):
                S.dma(YD.ap(YD.h[2 * c + hh, :, ts], key=(c, it, hh)), ym[hh][:], q="pool")


WSHAPES = {
    'mem_norm_g': (D,),
    'a_norm1_g': ('A', D), 'a_w_in': ('A', D, 2816), 'a_shift_mu': ('A', 2560), 'a_decay_up': ('A', 64, 768),
    'a_decay_bias': ('A', 768), 'a_aaa_up': ('A', 64, 768), 'a_aaa_bias': ('A', 768), 'a_gate_up': ('A', 128, 768),
    'a_k_k': ('A', 768), 'a_k_a': ('A', 768), 'a_r_k': ('A', 12, 64), 'a_lnx_g': ('A', 768), 'a_lnx_b': ('A', 768),
    'a_mem_kv': ('A', D, 512), 'a_w_out': ('A', D, D), 'a_norm2_g': ('A', D), 'a_ffn_gu': ('A', D, 2 * FH),
    'a_ffn_down': ('A', FH, D),
    'vres_mu': ('V', D), 'vres_down': ('V', D, 32), 'vres_up': ('V', 32, 768), 'vres_bias': ('V', 768),
    'kv_norm_g': (D,), 'kv_w_down': (D, 288), 'kv_latent_g': (256,), 'kv_w_up': (256, 1536),
    'b_norm1_g': ('B', D), 'b_w_in': ('B', D, 768), 'b_q_norm_g': ('B', 512), 'b_q_up': ('B', 512, 1152),
    'b_mem_kv': ('B', D, 512), 'b_w_out': ('B', D, D), 'b_norm2_g': ('B', D), 'b_ffn_gu': ('B', D, 2 * FH),
    'b_ffn_down': ('B', FH, D),
    'final_norm_g': (D,),
}


def build_program(ntok, N_A, N_B):
    nc = bass.Bass("TRN2", target_bir_lowering=False)
    dims = {'A': N_A, 'B': N_B, 'V': max(N_A - 1, 0)}
    with ExitStack() as stack:
        S = Sched(nc, stack)
        Wd = {}
        Wd["xT"] = S.dram("xT", [D, ntok], F32, kind="ExternalInput")
        Wd["memT"] = S.dram("memT", [D, NMEM], F32, kind="ExternalInput")[:]
        Wd["pos"] = S.dram("pos", [ntok], I32, kind="ExternalInput")[:]
        Wd["invf"] = S.dram("invf", [128], F32, kind="ExternalInput")[:]
        Wd["cmask"] = S.dram("cmask", [128, 4, 64], F32, kind="ExternalInput")[:]
        for nm, shp in WSHAPES.items():
            shp = [dims[s] if isinstance(s, str) else s for s in shp]
            if 0 in shp:
                continue
            Wd[nm] = S.dram(nm, shp, F32, kind="ExternalInput")[:]
        outT = S.dram("outT", [D, ntok], F32, kind="ExternalOutput")
        C = consts(S, Wd)
        memn = S.sb("memn", [128, KD, NMEM], BF16)
        mem_prep(S, C, Wd, memn)
        x_cur = Wd["xT"]
        nlayers = N_A + N_B
        li = 0
        st = {}
        for i in range(N_A):
            x_mid = S.dram(f"xmid{li}", [D, ntok], F32)
            rwkv_layer(S, C, x_cur, x_mid, Wd, i, ntok, memn, st)
            li += 1
            last = (li == nlayers)
            x_next = outT if last else S.dram(f"xres{li}", [D, ntok], F32)
            outs = ffn_phase(S, C, x_mid, x_next, Wd["a_norm2_g"][i], Wd["a_ffn_gu"][i], Wd["a_ffn_down"][i], ntok,
                             tag=f"fa{i}_", final_g=Wd["final_norm_g"] if last else None)
            x_cur = x_next
        if N_B > 0:
            tabs = {k: S.dram("tab_" + k, [128, ntok], F32) for k in ("cosk", "sink", "cosq", "sinq")}
            rope_tables_phase(S, C, Wd["pos"], Wd["invf"], tabs, ntok)
            KN = S.dram("KN", [12, 64, ntok], BF16)
            KR = S.dram("KR", [32, ntok], BF16)
            VD = S.dram("VD", [ntok // 128, 128, 12, 65], BF16)
            kv_phase(S, C, x_cur, Wd, ntok, KN, KR, VD, tabs)
        for j in range(N_B):
            QD = S.dram(f"QD{j}", [12, 96, ntok], BF16)
            MD = S.dram(f"MD{j}", [4, 64, ntok], BF16)
            OD = S.dram(f"OD{j}", [12, 64, ntok], BF16)
            PH = ("q", "attn", "out")
            if "q" in PH:
                mla_q_phase(S, C, x_cur, Wd, j, ntok, QD, MD, memn, tabs)
            if "attn" in PH:
                attn_phase(S, C, QD, KN, KR, VD, OD, ntok)
            x_mid = S.dram(f"xmid{li}", [D, ntok], F32)
            if "out" in PH:
                outproj_phase(S, C, x_cur, x_mid, OD, MD, Wd["b_w_out"][j], ntok, tag=f"ob{j}_")
            else:
                x_mid = x_cur
            li += 1
            last = (li == nlayers)
            x_next = outT if last else S.dram(f"xres{li}", [D, ntok], F32)
            outs = ffn_phase(S, C, x_mid, x_next, Wd["b_norm2_g"][j], Wd["b_ffn_gu"][j], Wd["b_ffn_down"][j], ntok,
                             tag=f"fb{j}_", final_g=Wd["final_norm_g"] if last else None)
            x_cur = x_next
        S.barrier_wait("sp", outs)
        S.emit()
        print("stats", S.stats, flush=True)
    return nc


def host_inputs(inputs, b, ntok):
    m = {}
    m["xT"] = np.ascontiguousarray(np.asarray(inputs["x"])[b, :ntok].T)
    m["memT"] = np.ascontiguousarray(np.asarray(inputs["mem"])[b].T)
    m["pos"] = np.ascontiguousarray(np.asarray(inputs["positions"])[b, :ntok]).astype(np.int32)
    m["invf"] = np.tile((10000.0 ** (-(np.arange(16, dtype=np.float32)) / 16)).astype(np.float32), 8)
    pj = (np.arange(128) % 64)[:, None]
    jj = np.arange(64)[None, :]
    m["cmask"] = np.ascontiguousarray(np.stack([jj < pj, jj > pj, jj >= pj, jj == pj], axis=1).astype(np.float32))
    for nm in WSHAPES:
        a = np.asarray(inputs[nm])
        if a.size == 0:
            continue
        m[nm] = np.ascontiguousarray(a, dtype=np.float32)
    return m

_NC_CACHE = {}


def kernel(**inputs):
    x = np.asarray(inputs["x"])
    B, S_len, _ = x.shape
    key = (S_len,)
    if key not in _NC_CACHE:
        _NC_CACHE[key] = build_program(S_len, 2, 2)
    nc = _NC_CACHE[key]
    in_maps = [host_inputs(inputs, b, S_len) for b in range(B)]
    res = run_bass_kernel_spmd(nc, in_maps, core_ids=list(range(B)))
    out = np.stack([np.ascontiguousarray(res.results[b]["outT"].T) for b in range(B)], axis=0)
    return out.astype(np.float32)
```
